# Optimizing a Trainium2 kernel written in Bass

```python
import math
import jax
import jax.numpy as jnp
from jax import lax
import numpy as np

D_MODEL = 1024
BATCH = 1
SEQ = 16384
DEPTH = 4

HEAD_DIM = 64
CHUNK = 64
FOX_HEADS = 4
FOX_BLOCK = 128
GLA_HEADS = 4
GLA_LOWRANK = 16
GLA_TAU = 16.0
RET_HEADS = 4
ROPE_THETA = 10000.0
SSD_HEADS = 8
SSD_GROUPS = 2
SSD_STATE = 64
SSD_CONV = 4
SSD_INNER = SSD_HEADS * HEAD_DIM
SSD_CONV_CH = SSD_INNER + 2 * SSD_GROUPS * SSD_STATE
FOX_W = FOX_HEADS * HEAD_DIM
GLA_W = GLA_HEADS * HEAD_DIM
RET_W = RET_HEADS * HEAD_DIM
N_BRANCH = 4
FFN_HIDDEN = ((8 * D_MODEL + 3 * 256 - 1) // (3 * 256)) * 256
NORM_EPS = 1e-6
IN_SPLITS = (FOX_W, FOX_W, FOX_W, FOX_HEADS,
             GLA_W, GLA_W, GLA_W, GLA_LOWRANK, GLA_W,
             RET_W, RET_W, RET_W, RET_W,
             SSD_INNER, SSD_CONV_CH, SSD_HEADS,
             N_BRANCH * D_MODEL)
N_IN = sum(IN_SPLITS)

kernel_name = 'hybrid_fox_gla_ret_ssd_trunk'


def rms_norm(x, g):
    xf = x.astype(jnp.float32)
    y = xf * lax.rsqrt(jnp.mean(xf * xf, axis=-1, keepdims=True) + NORM_EPS)
    return (y * g.astype(jnp.float32)).astype(x.dtype)


def split_cols(u, sizes):
    return jnp.split(u, np.cumsum(sizes)[:-1].tolist(), axis=-1)


def rotary_tables(positions):
    half = HEAD_DIM // 2
    inv = ROPE_THETA ** (-jnp.arange(half, dtype=jnp.float32) / half)
    ang = positions.astype(jnp.float32)[..., None] * inv
    return jnp.cos(ang)[:, :, None, :], jnp.sin(ang)[:, :, None, :]


def apply_rotary(x, cos, sin):
    half = x.shape[-1] // 2
    xf = x.astype(jnp.float32)
    x1, x2 = xf[..., :half], xf[..., half:]
    return jnp.concatenate([x1 * cos - x2 * sin, x1 * sin + x2 * cos], axis=-1).astype(x.dtype)


def causal_depthwise_conv(x, w, b):
    k_w, ch = w.shape
    y = lax.conv_general_dilated(x, w[:, None, :], window_strides=(1,), padding=[(k_w - 1, 0)],
                                 dimension_numbers=('NWC', 'WIO', 'NWC'), feature_group_count=ch)
    return y + b


def forgetting_attention(q, k, v, f_logit, q_gain, k_gain):
    bsz, seq, nh, dh = q.shape
    nb = seq // FOX_BLOCK
    qf = rms_norm(q, q_gain).astype(jnp.float32) * (dh ** -0.5)
    kf = rms_norm(k, k_gain).astype(jnp.float32)
    vf = v.astype(jnp.float32)
    c = jnp.cumsum(jax.nn.log_sigmoid(f_logit.astype(jnp.float32)), axis=1)
    qb = qf.reshape(bsz, nb, FOX_BLOCK, nh, dh).transpose(1, 0, 3, 2, 4)
    cb = c.reshape(bsz, nb, FOX_BLOCK, nh).transpose(1, 0, 3, 2)
    ck = c.transpose(0, 2, 1)
    kpos = jnp.arange(seq)

    def block(args):
        q_blk, c_blk, i = args
        s = jnp.einsum('bhqd,bkhd->bhqk', q_blk, kf) + c_blk[..., :, None] - ck[..., None, :]
        qpos = i * FOX_BLOCK + jnp.arange(FOX_BLOCK)
        s = jnp.where(kpos[None, :] <= qpos[:, None], s, -jnp.inf)
        p = jax.nn.softmax(s, axis=-1)
        return jnp.einsum('bhqk,bkhd->bqhd', p, vf)

    o = lax.map(block, (qb, cb, jnp.arange(nb)))
    return o.transpose(1, 0, 2, 3, 4).reshape(bsz, seq, nh * dh).astype(v.dtype)


def gla_chunked(q, k, v, log_a):
    bsz, seq, nh, dk = q.shape
    dv = v.shape[-1]
    nc = seq // CHUNK
    qf = (q.astype(jnp.float32) * (dk ** -0.5)).reshape(bsz, nc, CHUNK, nh, dk)
    kf = k.astype(jnp.float32).reshape(bsz, nc, CHUNK, nh, dk)
    vf = v.astype(jnp.float32).reshape(bsz, nc, CHUNK, nh, dv)
    b = jnp.cumsum(log_a.astype(jnp.float32).reshape(bsz, nc, CHUNK, nh, dk), axis=2)
    b_last = b[:, :, -1]
    q_dec = qf * jnp.exp(b)
    k_inv = kf * jnp.exp(-b)
    k_end = kf * jnp.exp(b_last[:, :, None] - b)
    mask = jnp.tril(jnp.ones((CHUNK, CHUNK), dtype=bool))
    attn = jnp.where(mask, jnp.einsum('bcthd,bcshd->bchts', q_dec, k_inv), 0.0)
    o_intra = jnp.einsum('bchts,bcshv->bcthv', attn, vf)
    kv = jnp.einsum('bcshd,bcshv->bchdv', k_end, vf)

    def step(state, inp):
        kv_c, dec_c = inp
        return state * dec_c[..., None] + kv_c, state

    s0 = jnp.zeros((bsz, nh, dk, dv), jnp.float32)
    _, s_in = lax.scan(step, s0, (jnp.moveaxis(kv, 1, 0), jnp.moveaxis(jnp.exp(b_last), 1, 0)))
    s_in = jnp.moveaxis(s_in, 0, 1)
    o_inter = jnp.einsum('bcthd,bchdv->bcthv', q_dec, s_in)
    return (o_intra + o_inter).reshape(bsz, seq, nh, dv).astype(v.dtype)


def retention_chunked(q, k, v, log_gamma):
    bsz, seq, nh, dk = q.shape
    dv = v.shape[-1]
    nc = seq // CHUNK
    qf = q.astype(jnp.float32).reshape(bsz, nc, CHUNK, nh, dk)
    kf = (k.astype(jnp.float32) * (dk ** -0.5)).reshape(bsz, nc, CHUNK, nh, dk)
    vf = v.astype(jnp.float32).reshape(bsz, nc, CHUNK, nh, dv)
    idx = jnp.arange(CHUNK, dtype=jnp.float32)
    rel = idx[:, None] - idx[None, :]
    decay = jnp.where(rel >= 0, jnp.exp(jnp.maximum(rel, 0.0)[None] * log_gamma[:, None, None]), 0.0)
    scores = jnp.einsum('bcthd,bcshd->bchts', qf, kf) * decay
    o_intra = jnp.einsum('bchts,bcshv->bcthv', scores, vf)
    q_w = jnp.exp((idx + 1.0)[:, None] * log_gamma)
    k_w = jnp.exp((CHUNK - 1.0 - idx)[:, None] * log_gamma)
    kv = jnp.einsum('bcshd,sh,bcshv->bchdv', kf, k_w, vf)
    chunk_decay = jnp.exp(CHUNK * log_gamma)

    def step(state, kv_c):
        return state * chunk_decay[:, None, None] + kv_c, state

    s0 = jnp.zeros((bsz, nh, dk, dv), jnp.float32)
    _, s_in = lax.scan(step, s0, jnp.moveaxis(kv, 1, 0))
    s_in = jnp.moveaxis(s_in, 0, 1)
    o_inter = jnp.einsum('bcthd,th,bchdv->bcthv', qf, q_w, s_in)
    return (o_intra + o_inter).reshape(bsz, seq, nh, dv).astype(v.dtype)


def ssd_chunked(x, dt, a, bmat, cmat):
    bsz, seq, nh, hp = x.shape
    ng, ns = bmat.shape[2], bmat.shape[3]
    r = nh // ng
    nc = seq // CHUNK
    xf = x.astype(jnp.float32).reshape(bsz, nc, CHUNK, ng, r, hp)
    dtf = dt.astype(jnp.float32).reshape(bsz, nc, CHUNK, ng, r)
    bf = bmat.astype(jnp.float32).reshape(bsz, nc, CHUNK, ng, ns)
    cf = cmat.astype(jnp.float32).reshape(bsz, nc, CHUNK, ng, ns)
    xdt = xf * dtf[..., None]
    cs = jnp.cumsum(dtf * a.astype(jnp.float32).reshape(ng, r), axis=2)
    cs_h = jnp.moveaxis(cs, 2, -1)
    mask = jnp.tril(jnp.ones((CHUNK, CHUNK), dtype=bool))
    seg = cs_h[..., :, None] - cs_h[..., None, :]
    decay = jnp.exp(jnp.where(mask, seg, -jnp.inf))
    cb = jnp.einsum('bctgn,bcsgn->bcgts', cf, bf)
    y_intra = jnp.einsum('bcgts,bcgrts,bcsgrp->bctgrp', cb, decay, xdt)
    cs_last = cs[:, :, -1]
    w_end = jnp.exp(cs_last[:, :, None] - cs)
    contrib = jnp.einsum('bcsgn,bcsgr,bcsgrp->bcgrpn', bf, w_end, xdt)

    def step(h, inp):
        c_c, d_c = inp
        return h * d_c[..., None, None] + c_c, h

    h0 = jnp.zeros((bsz, ng, r, hp, ns), jnp.float32)
    _, h_in = lax.scan(step, h0, (jnp.moveaxis(contrib, 1, 0), jnp.moveaxis(jnp.exp(cs_last), 1, 0)))
    h_in = jnp.moveaxis(h_in, 0, 1)
    y_inter = jnp.einsum('bctgn,bcgrpn,bctgr->bctgrp', cf, h_in, jnp.exp(cs))
    return (y_intra + y_inter).reshape(bsz, seq, nh, hp).astype(x.dtype)


def mixer_block(h, cos, sin, w_in, fox_bf, fox_qn, fox_kn, gla_w2, gla_b, gla_norm, ret_norm,
                conv_w, conv_b, dt_bias, a_log, d_skip, ssd_norm, w_up_a, w_up_b, w_up_c, w_up_d, w_out):
    bsz, seq, _ = h.shape
    u = h @ w_in
    (fq, fk, fv, ff, gq, gk, gv, glr, gr, rq, rk, rv, rg, z, xbc, dt, gates) = split_cols(u, IN_SPLITS)
    heads = lambda t: t.reshape(bsz, seq, -1, HEAD_DIM)
    y_a = forgetting_attention(heads(fq), heads(fk), heads(fv), ff + fox_bf, fox_qn, fox_kn)
    log_a = jax.nn.log_sigmoid((glr @ gla_w2 + gla_b).astype(jnp.float32)) / GLA_TAU
    o_b = gla_chunked(heads(gq), heads(gk), heads(gv), heads(log_a))
    y_b = (rms_norm(o_b, gla_norm) * jax.nn.silu(heads(gr))).reshape(bsz, seq, GLA_W)
    log_gamma = jnp.log(1.0 - 2.0 ** (-5.0 - jnp.arange(RET_HEADS, dtype=jnp.float32)))
    o_c = retention_chunked(apply_rotary(heads(rq), cos, sin), apply_rotary(heads(rk), cos, sin), heads(rv), log_gamma)
    y_c = (rms_norm(o_c, ret_norm) * jax.nn.silu(heads(rg))).reshape(bsz, seq, RET_W)
    xbc = jax.nn.silu(causal_depthwise_conv(xbc, conv_w, conv_b))
    xs, bm, cm = jnp.split(xbc, [SSD_INNER, SSD_INNER + SSD_GROUPS * SSD_STATE], axis=-1)
    dt = jax.nn.softplus((dt + dt_bias).astype(jnp.float32))
    a = -jnp.exp(a_log.astype(jnp.float32))
    xs_h = heads(xs)
    o_d = ssd_chunked(xs_h, dt, a, bm.reshape(bsz, seq, SSD_GROUPS, SSD_STATE), cm.reshape(bsz, seq, SSD_GROUPS, SSD_STATE))
    o_d = o_d + d_skip[:, None] * xs_h
    y_d = (o_d.reshape(bsz, seq, SSD_INNER) * jax.nn.silu(z)).reshape(bsz, seq, SSD_GROUPS, -1)
    y_d = rms_norm(y_d, ssd_norm.reshape(SSD_GROUPS, -1)).reshape(bsz, seq, SSD_INNER)
    g = jax.nn.sigmoid(gates.reshape(bsz, seq, N_BRANCH, D_MODEL))
    merged = (g[:, :, 0] * (y_a @ w_up_a) + g[:, :, 1] * (y_b @ w_up_b)
              + g[:, :, 2] * (y_c @ w_up_c) + g[:, :, 3] * (y_d @ w_up_d))
    return merged @ w_out


def swiglu(h, w_in, w_out):
    gate, up = jnp.split(h @ w_in, 2, axis=-1)
    return (jax.nn.silu(gate) * up) @ w_out


def setup_inputs(seed: int = 0) -> dict:
    key = jax.random.key(seed)
    ks = jax.random.split(key, 26)
    f32 = jnp.float32
    L = DEPTH

    def normal(k, shape, fan_in):
        return jax.random.normal(k, shape, f32) * (fan_in ** -0.5)

    def gain(k, shape):
        return 1.0 + 0.02 * jax.random.normal(k, shape, f32)

    x = jax.random.normal(ks[0], (BATCH, SEQ, D_MODEL), f32)
    start = jax.random.randint(ks[1], (BATCH, 1), 0, 1024, dtype=jnp.int32)
    positions = start + jnp.arange(SEQ, dtype=jnp.int32)[None, :]
    dt0 = jnp.exp(jax.random.uniform(ks[14], (L, SSD_HEADS), f32, math.log(1e-3), math.log(1e-1)))
    return {
        'x': x,
        'positions': positions,
        'ln1': gain(ks[2], (L, D_MODEL)),
        'ln2': gain(ks[3], (L, D_MODEL)),
        'w_in': normal(ks[4], (L, D_MODEL, N_IN), D_MODEL),
        'fox_bf': jax.random.uniform(ks[5], (L, FOX_HEADS), f32, 1.0, 5.0),
        'fox_qn': gain(ks[6], (L, HEAD_DIM)),
        'fox_kn': gain(ks[7], (L, HEAD_DIM)),
        'gla_w2': normal(ks[8], (L, GLA_LOWRANK, GLA_W), GLA_LOWRANK),
        'gla_b': 0.1 * jax.random.normal(ks[9], (L, GLA_W), f32),
        'gla_norm': gain(ks[10], (L, HEAD_DIM)),
        'ret_norm': gain(ks[11], (L, HEAD_DIM)),
        'ssd_conv_w': normal(ks[12], (L, SSD_CONV, SSD_CONV_CH), SSD_CONV),
        'ssd_conv_b': 0.01 * jax.random.normal(ks[13], (L, SSD_CONV_CH), f32),
        'ssd_dt_bias': dt0 + jnp.log(-jnp.expm1(-dt0)),
        'ssd_a_log': jnp.log(jax.random.uniform(ks[15], (L, SSD_HEADS), f32, 1.0, 16.0)),
        'ssd_d': 1.0 + 0.1 * jax.random.normal(ks[16], (L, SSD_HEADS), f32),
        'ssd_norm': gain(ks[17], (L, SSD_INNER)),
        'w_up_a': normal(ks[18], (L, FOX_W, D_MODEL), FOX_W),
        'w_up_b': normal(ks[19], (L, GLA_W, D_MODEL), GLA_W),
        'w_up_c': normal(ks[20], (L, RET_W, D_MODEL), RET_W),
        'w_up_d': normal(ks[21], (L, SSD_INNER, D_MODEL), SSD_INNER),
        'w_out': normal(ks[22], (L, D_MODEL, D_MODEL), D_MODEL),
        'w_ffn_in': normal(ks[23], (L, D_MODEL, 2 * FFN_HIDDEN), D_MODEL),
        'w_ffn_out': normal(ks[24], (L, FFN_HIDDEN, D_MODEL), FFN_HIDDEN),
    }


def reference(x, positions, ln1, ln2, w_in, fox_bf, fox_qn, fox_kn, gla_w2, gla_b, gla_norm, ret_norm,
              ssd_conv_w, ssd_conv_b, ssd_dt_bias, ssd_a_log, ssd_d, ssd_norm,
              w_up_a, w_up_b, w_up_c, w_up_d, w_out, w_ffn_in, w_ffn_out):
    cos, sin = rotary_tables(positions)
    for l in range(DEPTH):
        h = rms_norm(x, ln1[l])
        x = x + mixer_block(h, cos, sin, w_in[l], fox_bf[l], fox_qn[l], fox_kn[l], gla_w2[l], gla_b[l],
                            gla_norm[l], ret_norm[l], ssd_conv_w[l], ssd_conv_b[l], ssd_dt_bias[l],
                            ssd_a_log[l], ssd_d[l], ssd_norm[l], w_up_a[l], w_up_b[l], w_up_c[l],
                            w_up_d[l], w_out[l])
        h = rms_norm(x, ln2[l])
        x = x + swiglu(h, w_ffn_in[l], w_ffn_out[l])
    return x
```

```python
import numpy as np
import concourse.bass as bass
import concourse.mybir as mybir

F32 = mybir.dt.float32
BF16 = mybir.dt.bfloat16
I32 = mybir.dt.int32
AF = mybir.ActivationFunctionType
ALU = mybir.AluOpType
AX = mybir.AxisListType

SAME_ENGINE_SYNC = True
N_DMA_CH = 8


class Prog:
    def __init__(self, nc):
        self.nc = nc
        self.ops = []
        self.dma_rr = {}

    def op(self, eng, fn, reads=(), writes=(), dma=False):
        self.ops.append(dict(eng=eng, fn=fn, reads=[_k(k) for k in reads],
                             writes=[_k(k) for k in writes], dma=dma))

    def dma(self, fn, reads=(), writes=(), eng="sp"):
        self.op(eng, fn, reads, writes, dma=True)

    def plan(self):
        ops = self.ops
        state = {}
        eng_cnt = {}
        ch_cnt = {}
        ch_last = {}
        rr = {}
        for i, o in enumerate(ops):
            deps = set()
            for (name, sub) in o["reads"]:
                for rec in state.get(name, []):
                    if rec[0] is None or sub is None or rec[0] == sub:
                        if rec[1] is not None:
                            deps.add(rec[1])
            for (name, sub) in o["writes"]:
                for rec in state.get(name, []):
                    if rec[0] is None or sub is None or rec[0] == sub:
                        if rec[1] is not None:
                            deps.add(rec[1])
                        deps.update(rec[2])
            if o["dma"]:
                e = o["eng"]
                ch = rr.get(e, 0)
                rr[e] = (ch + 1) % N_DMA_CH
                key = ("dma", e, ch)
                if key in ch_last:
                    deps.add(ch_last[key])
                ch_last[key] = i
                ch_cnt[key] = ch_cnt.get(key, 0) + 1
                o["sem"] = key
                o["semval"] = 16 * ch_cnt[key]
            else:
                e = o["eng"]
                eng_cnt[e] = eng_cnt.get(e, 0) + 1
                o["sem"] = ("eng", e)
                o["semval"] = eng_cnt[e]
            deps.discard(i)
            o["deps"] = deps
            for (name, sub) in o["reads"]:
                recs = state.setdefault(name, [])
                hit = False
                for rec in recs:
                    if rec[0] == sub:
                        rec[2].add(i)
                        hit = True
                if not hit:
                    recs.append([sub, None, {i}])
                for rec in recs:
                    if rec[0] != sub and (rec[0] is None or sub is None):
                        rec[2].add(i)
            for (name, sub) in o["writes"]:
                recs = state.setdefault(name, [])
                hit = False
                for rec in recs:
                    if rec[0] == sub:
                        rec[1] = i
                        rec[2] = set()
                        hit = True
                    elif rec[0] is None or sub is None:
                        rec[1] = i
                        rec[2] = set()
                if not hit:
                    recs.append([sub, i, set()])
        waited = {}
        for i, o in enumerate(ops):
            need = {}
            for d in o["deps"]:
                od = ops[d]
                if (not od["dma"]) and od["eng"] == o["eng"] and not o["dma"]:
                    if o["eng"] == "pe" or not SAME_ENGINE_SYNC:
                        continue
                    raw = any(_overlap(r, w) for r in o["reads"] for w in od["writes"])
                    if not raw:
                        continue
                need[od["sem"]] = max(need.get(od["sem"], 0), od["semval"])
            w = []
            for s, v in need.items():
                if waited.get((o["eng"], s), 0) < v:
                    waited[(o["eng"], s)] = v
                    w.append((s, v))
            o["waits"] = w
        self.sem_keys = sorted({o["sem"] for o in ops}, key=str)
        return self

    def emit(self, block_engines, sems):
        raise NotImplementedError

    def emit_engine(self, name, eng, sems):
        for o in self.ops:
            if o["eng"] != name:
                continue
            for (s, v) in o["waits"]:
                eng.wait_ge(sems[s], v)
            ins = o["fn"](eng)
            inc = 16 if o["dma"] else 1
            ins.then_inc(sems[o["sem"]], inc)


def _k(k):
    if isinstance(k, tuple):
        return (k[0], k[1])
    return (k, None)


def _overlap(a, b):
    return a[0] == b[0] and (a[1] is None or b[1] is None or a[1] == b[1])


ENG_ATTR = {"pe": "tensor", "act": "scalar", "dve": "vector", "pool": "gpsimd", "sp": "sync"}


def run_prog(nc, prog, tail_waits=True):
    prog.plan()
    import contextlib
    with contextlib.ExitStack() as st:
        sems = {}
        for k in prog.sem_keys:
            sems[k] = st.enter_context(nc.semaphore("s_" + "_".join(str(x) for x in k)))
        block = st.enter_context(nc.Block())
        used = {o["eng"] for o in prog.ops}
        finals = {}
        for o in prog.ops:
            finals[o["sem"]] = o["semval"]

        def mk(name):
            def body(eng):
                prog.emit_engine(name, eng, sems)
                if name == "sp":
                    for k, v in finals.items():
                        eng.wait_ge(sems[k], v)
            return body

        for name in ["sp", "pe", "act", "dve", "pool"]:
            if name in used or name == "sp":
                getattr(block, ENG_ATTR[name])(mk(name))
    return nc

from concourse.bass_utils import run_bass_kernel_spmd
import ml_dtypes

NT = 2048
TT = 512
NTT = 4
HP = 4
EPS = 1e-6
TWO_PI = 6.283185307179586


class Rot:
    def __init__(self, tiles):
        self.tiles = tiles
        self.i = 0

    def next(self):
        t = self.tiles[self.i % len(self.tiles)]
        self.i += 1
        return t


class Ctx:
    def __init__(self):
        self.nc = bass.Bass("TRN2", target_bir_lowering=False)
        self.p = Prog(self.nc)
        self.names = {}

    def dram(self, name, shape, dt, kind):
        return self.nc.dram_tensor(name, shape, dt, kind=kind).ap()

    def sb(self, name, shape, dt):
        t = self.nc.alloc_sbuf_tensor(name, shape, dt)
        return _Tile(t, name)

    def ps(self, name, shape=(128, 512), dt=F32):
        t = self.nc.alloc_psum_tensor(name, list(shape), dt)
        return _Tile(t, name)

    def rot(self, prefix, n, shape, dt, psum=False):
        return Rot([(self.ps if psum else self.sb)(f"{prefix}{i}", list(shape), dt) for i in range(n)])


class _Tile:
    def __init__(self, t, name):
        self.t = t
        self.name = name

    def __getitem__(self, k):
        return self.t[k]


def build_l1():
    c = Ctx()
    nc, p = c.nc, c.p
    xT = c.dram("xT", [1024, HP + NT], F32, "ExternalInput")
    w1 = c.dram("w1", [1024, 4116], F32, "ExternalInput")
    ln1 = c.dram("ln1", [128, 8], F32, "ExternalInput")
    pp = c.dram("pp", [128, 48], F32, "ExternalInput")
    w2 = c.dram("w2", [16, 256], F32, "ExternalInput")
    pos = c.dram("pos", [128, NT], I32, "ExternalInput")
    cst = c.dram("cst", [128, 4], F32, "ExternalInput")
    ob = c.dram("ob", [3584, NT], BF16, "ExternalOutput")
    of = c.dram("of", [772, NT], F32, "ExternalOutput")

    hT = c.sb("hT", [128, 8, HP + NT], BF16)
    xt = c.rot("xt", 2, [128, 8, TT], F32)
    xh = c.sb("xh", [128, 8, HP], F32)
    sq = c.rot("sq", 2, [128, TT], BF16)
    rs = c.rot("rs", 2, [128, TT], F32)
    ones = c.sb("ones", [128, 128], BF16)
    bd = c.sb("bd", [128, 128], BF16)
    lnw = c.sb("lnw", [128, 8], F32)
    ppt = c.sb("ppt", [128, 48], F32)
    npp = c.sb("npp", [128, 48], F32)
    na = c.sb("na", [128, 4], F32)
    cstt = c.sb("cstt", [128, 4], F32)
    epsb = c.sb("epsb", [128, 1], F32)
    ws = c.rot("ws", 3, [128, 8, 512], BF16)
    wsm = c.sb("wsm", [128, 8, 20], BF16)
    w2t = c.sb("w2t", [16, 256], BF16)
    cos2 = c.sb("cos2", [128, NT], F32)
    sin2 = c.sb("sin2", [128, NT], F32)
    posi = c.sb("posi", [128, NT], I32)
    tr_a = c.sb("tr_a", [128, NT], F32)
    tr_b = c.sb("tr_b", [128, NT], F32)
    tr_i = c.sb("tr_i", [128, NT], I32)
    t1 = c.rot("t1_", 2, [128, TT], F32)
    t2 = c.rot("t2_", 2, [128, TT], F32)
    obuf = c.rot("obuf", 4, [128, TT], BF16)
    fbuf = c.rot("fbuf", 3, [128, TT], F32)
    pre = c.rot("pre", 2, [128, TT + 3], F32)
    acc = c.rot("acc", 2, [128, TT], F32)
    sil = c.rot("sil", 2, [128, TT], F32)
    dtr = c.rot("dtr", 2, [128, TT], F32)
    glrT = c.rot("glrT", 2, [16, TT], BF16)
    psA = c.rot("psA", 4, [128, TT], F32, psum=True)
    psB = c.rot("psB", 3, [128, TT], F32, psum=True)
    psH = c.ps("psH", [128, 8])
    onesf = c.sb("onesf", [128, TT], F32)
    crow = c.sb("crow", [4, NT], F32)
    fsc = c.rot("fsc", 2, [128, TT], F32)

    D = p.dma
    D(lambda e: e.dma_start(out=lnw[:], in_=ln1), writes=["lnw"])
    D(lambda e: e.dma_start(out=ppt[:], in_=pp), writes=["ppt"])
    D(lambda e: e.dma_start(out=cstt[:], in_=cst), writes=["cstt"])
    D(lambda e: e.dma_start(out=posi[:], in_=pos), writes=["posi"])
    D(lambda e: e.dma_start(out=w2t[:], in_=w2), writes=["w2t"], eng="pool")
    D(lambda e: e.dma_start(out=wsm[:], in_=w1[:, 4096:4116].rearrange("(kc p) m -> p kc m", p=128)), writes=["wsm"], eng="pool")
    D(lambda e: e.dma_start(out=xh[:], in_=xT[:, 0:HP].rearrange("(kc p) t -> p kc t", p=128)), writes=["xh"])
    p.op("pool", lambda e: e.memset(ones[:], 1.0), writes=["ones"])
    p.op("pool", lambda e: e.memset(onesf[:], 1.0), writes=["onesf"])
    p.op("pool", lambda e: e.memset(bd[:], 0.0), writes=["bd"])
    p.op("pool", lambda e: e.memset(bd[0:64, 0:64], 1.0), reads=["bd"], writes=["bd"])
    p.op("pool", lambda e: e.memset(bd[64:128, 64:128], 1.0), reads=["bd"], writes=["bd"])
    p.op("pool", lambda e: e.memset(epsb[:], EPS), writes=["epsb"])
    p.op("dve", lambda e: e.tensor_scalar(out=npp[:], in0=ppt[:], scalar1=-1.0, scalar2=None, op0=ALU.mult), reads=["ppt"], writes=["npp"])
    p.op("act", lambda e: e.activation(out=na[:], in_=ppt[:, 9:13], func=AF.Exp), reads=["ppt"], writes=["na"])
    p.op("dve", lambda e: e.tensor_scalar(out=na[:], in0=na[:], scalar1=-1.0, scalar2=None, op0=ALU.mult), reads=["na"], writes=["na"])
    p.op("dve", lambda e: e.tensor_copy(out=tr_a[:], in_=posi[:]), reads=["posi"], writes=["tr_a"])
    p.op("dve", lambda e: e.tensor_scalar(out=tr_a[:], in0=tr_a[:], scalar1=cstt[:, 0:1], scalar2=1.0 / TWO_PI, op0=ALU.mult, op1=ALU.mult), reads=["tr_a", "cstt"], writes=["tr_a"])
    for which, dst, col in (("s", sin2, 1), ("c", cos2, 2)):
        if which == "c":
            p.op("dve", lambda e: e.tensor_scalar(out=tr_a[:], in0=tr_a[:], scalar1=0.25, scalar2=None, op0=ALU.add), reads=["tr_a"], writes=["tr_a"])
        p.op("dve", lambda e: e.tensor_copy(out=tr_i[:], in_=tr_a[:]), reads=["tr_a"], writes=["tr_i"])
        p.op("dve", lambda e: e.tensor_copy(out=tr_b[:], in_=tr_i[:]), reads=["tr_i"], writes=["tr_b"])
        p.op("dve", lambda e: e.tensor_tensor(out=tr_b[:], in0=tr_a[:], in1=tr_b[:], op=ALU.subtract), reads=["tr_a", "tr_b"], writes=["tr_b"])
        p.op("dve", lambda e, dst=dst: e.tensor_scalar(out=dst[:], in0=tr_b[:], scalar1=0.5, scalar2=None, op0=ALU.is_gt), reads=["tr_b"], writes=[dst.name])
        p.op("dve", lambda e, dst=dst: e.tensor_tensor(out=tr_b[:], in0=tr_b[:], in1=dst[:], op=ALU.subtract), reads=["tr_b", dst.name], writes=["tr_b"])
        p.op("act", lambda e, dst=dst: e.activation(out=dst[:], in_=tr_b[:], func=AF.Sin, scale=TWO_PI), reads=["tr_b"], writes=[dst.name])
        p.op("dve", lambda e, dst=dst, col=col: e.tensor_scalar(out=dst[:], in0=dst[:], scalar1=cstt[:, col:col + 1], scalar2=None, op0=ALU.mult), reads=[dst.name, "cstt"], writes=[dst.name])

    def rms(xs_, n, dst_cols):
        ps = psA.next()
        for kc in range(8):
            s = sq.next()
            p.op("act", lambda e, s=s, kc=kc: e.activation(out=s[:, 0:n], in_=xs_[:, kc, 0:n], func=AF.Square), reads=[xs_.name], writes=[s.name])
            p.op("pe", lambda e, s=s, kc=kc: e.matmul(ps[:, 0:n], ones[:], s[:, 0:n], start=(kc == 0), stop=(kc == 7)), reads=[s.name, "ones"], writes=[ps.name])
        r = rs.next()
        p.op("act", lambda e: e.activation(out=r[:, 0:n], in_=ps[:, 0:n], func=AF.Ln, scale=1.0 / 1024, bias=epsb[:, 0:1]), reads=[ps.name, "epsb"], writes=[r.name])
        p.op("act", lambda e: e.activation(out=r[:, 0:n], in_=r[:, 0:n], func=AF.Exp, scale=-0.5), reads=[r.name], writes=[r.name])
        for kc in range(8):
            p.op("dve", lambda e, kc=kc: e.scalar_tensor_tensor(out=hT[:, kc, dst_cols[0]:dst_cols[1]], in0=xs_[:, kc, 0:n], scalar=lnw[:, kc:kc + 1], in1=r[:, 0:n], op0=ALU.mult, op1=ALU.mult),
                 reads=[xs_.name, "lnw", r.name], writes=[("hT", dst_cols[0])])

    rms(xh, HP, (0, HP))
    for tt in range(NTT):
        x_ = xt.next()
        D(lambda e, x_=x_, tt=tt: e.dma_start(out=x_[:], in_=xT[:, HP + tt * TT:HP + (tt + 1) * TT].rearrange("(kc p) t -> p kc t", p=128)), writes=[x_.name])
        rms(x_, TT, (HP + tt * TT, HP + (tt + 1) * TT))

    def proj(ps, w, c0, m, col0, n):
        def f(e):
            ins = None
            for kc in range(8):
                ins = e.matmul(ps[0:m, 0:n], w[:, kc, c0:c0 + m], hT[:, kc, col0:col0 + n], start=(kc == 0), stop=(kc == 7))
            return ins
        p.op("pe", f, reads=[w.name, "hT"], writes=[ps.name])

    def load_group(g):
        w = ws.next()
        D(lambda e: e.dma_start(out=w[:], in_=w1[:, 512 * g:512 * (g + 1)].rearrange("(kc p) m -> p kc m", p=128)), writes=[w.name], eng="pool")
        return w

    def out_b(row0, tt, src, m=128):
        D(lambda e: e.dma_start(out=ob[row0:row0 + m, tt * TT:(tt + 1) * TT], in_=src[0:m, :]), reads=[src.name])

    def out_f(row0, tt, src, p0, m):
        D(lambda e: e.dma_start(out=of[row0:row0 + m, tt * TT:(tt + 1) * TT], in_=src[p0:p0 + m, :]), reads=[src.name])

    def qknorm(ps, gcol, extra_bias, row0, tt):
        s = sq.next()
        p.op("act", lambda e: e.activation(out=s[:], in_=ps[:], func=AF.Square), reads=[ps.name], writes=[s.name])
        ps2 = psB.next()
        p.op("pe", lambda e: e.matmul(ps2[:], bd[:], s[:], start=True, stop=True), reads=["bd", s.name], writes=[ps2.name])
        r = rs.next()
        p.op("act", lambda e: e.activation(out=r[:], in_=ps2[:], func=AF.Ln, scale=1.0 / 64, bias=epsb[:, 0:1]), reads=[ps2.name, "epsb"], writes=[r.name])
        p.op("act", lambda e: e.activation(out=r[:], in_=r[:], func=AF.Exp, scale=-0.5, bias=extra_bias), reads=[r.name], writes=[r.name])
        o = obuf.next()
        p.op("dve", lambda e: e.scalar_tensor_tensor(out=o[:], in0=ps[:], scalar=ppt[:, gcol:gcol + 1], in1=r[:], op0=ALU.mult, op1=ALU.mult), reads=[ps.name, "ppt", r.name], writes=[o.name])
        out_b(row0, tt, o)

    def copy_out(ps, scale, row0, tt):
        o = obuf.next()
        p.op("act", lambda e: e.activation(out=o[:], in_=ps[:], func=AF.Copy, scale=scale), reads=[ps.name], writes=[o.name])
        out_b(row0, tt, o)

    def conv_chunk(w, ci, cidx, T0, wd=None, j=None, tt=0, row0=0):
        psM = psA.next()
        proj(psM, w, ci * 128, 128, T0, TT)
        proj(psH, w, ci * 128, 128, T0 - 3, 3)
        pr = pre.next()
        p.op("act", lambda e: e.activation(out=pr[:, 0:3], in_=psH[:, 0:3], func=AF.Copy), reads=["psH"], writes=[(pr.name, 0)])
        p.op("act", lambda e: e.activation(out=pr[:, 3:TT + 3], in_=psM[:], func=AF.Copy), reads=[psM.name], writes=[(pr.name, 1)])
        a = acc.next()
        p.op("dve", lambda e: e.tensor_scalar(out=a[:], in0=pr[:, 0:TT], scalar1=ppt[:, 13 + 4 * cidx:14 + 4 * cidx], scalar2=ppt[:, 37 + cidx:38 + cidx], op0=ALU.mult, op1=ALU.add), reads=[pr.name, "ppt"], writes=[a.name])
        for jj in range(1, 4):
            p.op("dve", lambda e, jj=jj: e.scalar_tensor_tensor(out=a[:], in0=pr[:, jj:jj + TT], scalar=ppt[:, 13 + 4 * cidx + jj:14 + 4 * cidx + jj], in1=a[:], op0=ALU.mult, op1=ALU.add), reads=[pr.name, "ppt", a.name], writes=[a.name])
        s = sil.next()
        p.op("act", lambda e: e.activation(out=s[:], in_=a[:], func=AF.Silu), reads=[a.name], writes=[s.name])
        o = obuf.next()
        p.op("act", lambda e: e.activation(out=o[:], in_=s[:], func=AF.Copy), reads=[s.name], writes=[o.name])
        out_b(row0, tt, o)
        if wd is not None:
            psD = psB.next()
            proj(psD, wd, j * 128, 128, T0, TT)
            d = dtr.next()
            p.op("act", lambda e: e.activation(out=d[:], in_=psD[:], func=AF.Exp, bias=ppt[:, 5 + j:6 + j]), reads=[psD.name, "ppt"], writes=[d.name])
            p.op("act", lambda e: e.activation(out=d[:], in_=d[:], func=AF.Ln, bias=1.0), reads=[d.name], writes=[d.name])
            o2 = obuf.next()
            p.op("dve", lambda e: e.tensor_tensor(out=o2[:], in0=s[:], in1=d[:], op=ALU.mult), reads=[s.name, d.name], writes=[o2.name])
            out_b(2816 + 128 * j, tt, o2)
            f = fbuf.next()
            p.op("dve", lambda e: e.tensor_scalar(out=f[:], in0=d[:], scalar1=na[:, j:j + 1], scalar2=None, op0=ALU.mult), reads=[d.name, "na"], writes=[f.name])
            f2 = fsc.next()
            for q4 in range(4):
                p.op("dve", lambda e, q4=q4: e.tensor_tensor_scan(out=f2[:, 128 * q4:128 * (q4 + 1)], data0=onesf[:, 0:128], data1=f[:, 128 * q4:128 * (q4 + 1)], initial=0.0, op0=ALU.mult, op1=ALU.add), reads=[f.name, "onesf"], writes=[(f2.name, q4)])
            out_f(260 + 128 * j, tt, f2, 0, 128)

    LN8 = float(np.log(0.125))
    for tt in range(NTT):
        T0 = HP + tt * TT
        w = load_group(0)
        for ci in range(4):
            ps = psA.next()
            proj(ps, w, ci * 128, 128, T0, TT)
            if ci < 2:
                qknorm(ps, 0, LN8, 0 + 128 * ci, tt)
            else:
                qknorm(ps, 1, 0.0, 256 + 128 * (ci - 2), tt)
        w = load_group(1)
        for ci in range(4):
            ps = psA.next()
            proj(ps, w, ci * 128, 128, T0, TT)
            copy_out(ps, 1.0 if ci < 2 else 0.125, (512 + 128 * ci) if ci < 2 else (768 + 128 * (ci - 2)), tt)
        w = load_group(2)
        for ci in range(4):
            ps = psA.next()
            proj(ps, w, ci * 128, 128, T0, TT)
            copy_out(ps, 1.0, 1024 + 128 * ci, tt)
        wa = load_group(3)
        wb = load_group(4)
        for ci in range(4):
            pu = psA.next()
            proj(pu, wa, ci * 128, 128, T0, TT)
            pw = psB.next()
            proj(pw, wb, ci * 128, 128, T0, TT)
            a_, b_ = t1.next(), t2.next()
            p.op("dve", lambda e, pu=pu, a_=a_, tt=tt: e.tensor_tensor(out=a_[:], in0=pu[:], in1=cos2[:, tt * TT:(tt + 1) * TT], op=ALU.mult), reads=[pu.name, "cos2"], writes=[a_.name])
            p.op("dve", lambda e, pw=pw, b_=b_, tt=tt: e.tensor_tensor(out=b_[:], in0=pw[:], in1=sin2[:, tt * TT:(tt + 1) * TT], op=ALU.mult), reads=[pw.name, "sin2"], writes=[b_.name])
            o = obuf.next()
            p.op("pool", lambda e, o=o, a_=a_, b_=b_: e.tensor_tensor(out=o[:], in0=a_[:], in1=b_[:], op=ALU.add), reads=[a_.name, b_.name], writes=[o.name])
            out_b(1536 + 128 * ci, tt, o)
        w = load_group(5)
        for ci in range(2):
            ps = psA.next()
            proj(ps, w, ci * 128, 128, T0, TT)
            copy_out(ps, 1.0, 2048 + 128 * ci, tt)
        conv_chunk(w, 2, 4, T0, tt=tt, row0=3328)
        conv_chunk(w, 3, 5, T0, tt=tt, row0=3456)
        wx = load_group(6)
        wd = load_group(7)
        for j in range(4):
            conv_chunk(wx, j, j, T0, wd=wd, j=j, tt=tt, row0=2304 + 128 * j)
        ps = psA.next()
        proj(ps, wsm, 0, 4, T0, TT)
        f = fbuf.next()
        p.op("act", lambda e, ps=ps, f=f: e.activation(out=f[0:4, :], in_=ps[0:4, :], func=AF.Exp, scale=-1.0, bias=npp[0:4, 2:3]), reads=[ps.name, "npp"], writes=[f.name])
        p.op("act", lambda e, f=f: e.activation(out=f[0:4, :], in_=f[0:4, :], func=AF.Ln, bias=1.0), reads=[f.name], writes=[f.name])
        p.op("dve", lambda e, f=f: e.tensor_scalar(out=f[0:4, :], in0=f[0:4, :], scalar1=-1.0, scalar2=None, op0=ALU.mult), reads=[f.name], writes=[f.name])
        if tt == 0:
            p.op("dve", lambda e, f=f: e.tensor_tensor_scan(out=crow[0:4, 0:TT], data0=onesf[0:4, :], data1=f[0:4, :], initial=0.0, op0=ALU.mult, op1=ALU.add), reads=[f.name, "onesf"], writes=[("crow", 0)])
        else:
            p.op("dve", lambda e, f=f, tt=tt: e.tensor_tensor_scan(out=crow[0:4, tt * TT:(tt + 1) * TT], data0=onesf[0:4, :], data1=f[0:4, :], initial=crow[0:4, tt * TT - 1:tt * TT], op0=ALU.mult, op1=ALU.add), reads=[f.name, "onesf", ("crow", tt - 1)], writes=[("crow", tt)])
        D(lambda e, tt=tt: e.dma_start(out=of[0:4, tt * TT:(tt + 1) * TT], in_=crow[0:4, tt * TT:(tt + 1) * TT]), reads=[("crow", tt)])
        ps = psA.next()
        proj(ps, wsm, 4, 16, T0, TT)
        g_ = glrT.next()
        p.op("act", lambda e, ps=ps, g_=g_: e.activation(out=g_[:], in_=ps[0:16, :], func=AF.Copy), reads=[ps.name], writes=[g_.name])
        for c2 in range(2):
            ps2 = psB.next()
            p.op("pe", lambda e, ps2=ps2, g_=g_, c2=c2: e.matmul(ps2[:], w2t[0:16, c2 * 128:(c2 + 1) * 128], g_[0:16, :], start=True, stop=True), reads=["w2t", g_.name], writes=[ps2.name])
            f = fbuf.next()
            p.op("act", lambda e, ps2=ps2, f=f, c2=c2: e.activation(out=f[:], in_=ps2[:], func=AF.Exp, scale=-1.0, bias=npp[:, 3 + c2:4 + c2]), reads=[ps2.name, "npp"], writes=[f.name])
            p.op("act", lambda e, f=f: e.activation(out=f[:], in_=f[:], func=AF.Ln, bias=1.0), reads=[f.name], writes=[f.name])
            p.op("dve", lambda e, f=f: e.tensor_scalar(out=f[:], in0=f[:], scalar1=-1.0 / 16.0, scalar2=None, op0=ALU.mult), reads=[f.name], writes=[f.name])
            f2 = fsc.next()
            for q4 in range(4):
                p.op("dve", lambda e, q4=q4, f=f, f2=f2: e.tensor_tensor_scan(out=f2[:, 128 * q4:128 * (q4 + 1)], data0=onesf[:, 0:128], data1=f[:, 128 * q4:128 * (q4 + 1)], initial=0.0, op0=ALU.mult, op1=ALU.add), reads=[f.name, "onesf"], writes=[(f2.name, q4)])
            out_f(4 + 128 * c2, tt, f2, 0, 128)
    run_prog(nc, p)
    return nc


COLS = dict(fq=0, fk=256, fv=512, ff=768, gq=772, gk=1028, gv=1284, glr=1540, gr=1556, rq=1812, rk=2068,
            rv=2324, rg=2580, z=2836, xs=3348, B=3860, C=3988, dt=4116, gates=4124)


def _swap_halves(idx):
    idx = idx.reshape(-1, 2, 32)
    return idx[:, ::-1, :].reshape(-1)


def l1_weight_cols():
    C = COLS
    r = np.arange
    rq = r(C["rq"], C["rq"] + 256)
    rk = r(C["rk"], C["rk"] + 256)
    dtrep = np.repeat(r(C["dt"], C["dt"] + 8), 64)
    cols = np.concatenate([
        r(C["fq"], C["fq"] + 256), r(C["fk"], C["fk"] + 256),
        r(C["fv"], C["fv"] + 256), r(C["gq"], C["gq"] + 256),
        r(C["gk"], C["gk"] + 256), r(C["gv"], C["gv"] + 256),
        rq, rk, _swap_halves(rq), _swap_halves(rk),
        r(C["rv"], C["rv"] + 256), r(C["B"], C["B"] + 128), r(C["C"], C["C"] + 128),
        r(C["xs"], C["xs"] + 512), dtrep,
        r(C["ff"], C["ff"] + 4), r(C["glr"], C["glr"] + 16)])
    return cols


def l1_params(P):
    pp = np.zeros((128, 48), np.float32)
    pp[:, 0] = np.tile(P["fox_qn"], 2)
    pp[:, 1] = np.tile(P["fox_kn"], 2)
    pp[0:4, 2] = P["fox_bf"]
    pp[:, 3] = P["gla_b"][0:128]
    pp[:, 4] = P["gla_b"][128:256]
    for j in range(4):
        pp[:, 5 + j] = np.repeat(P["ssd_dt_bias"][2 * j:2 * j + 2], 64)
        pp[:, 9 + j] = np.repeat(P["ssd_a_log"][2 * j:2 * j + 2], 64)
    for cidx in range(6):
        ch = slice(128 * cidx, 128 * (cidx + 1))
        for jj in range(4):
            pp[:, 13 + 4 * cidx + jj] = P["ssd_conv_w"][jj, ch]
        pp[:, 37 + cidx] = P["ssd_conv_b"][ch]
    return pp


def l1_consts():
    half = 32
    inv = (10000.0 ** (-(np.arange(half, dtype=np.float32)) / np.float32(half))).astype(np.float32)
    cst = np.zeros((128, 4), np.float32)
    d = np.arange(128) % 64
    cst[:, 0] = inv[d % 32]
    s = np.float32(np.sqrt(0.125))
    cst[:, 1] = np.where(d < 32, -s, s)
    cst[:, 2] = s
    return cst

SEQ = 16384
NCH = 128


def build_l2():
    c = Ctx()
    nc, p = c.nc, c.p
    D = p.dma
    fqT = c.dram("fqT", [64, 8192], BF16, "ExternalInput")
    fkT = c.dram("fkT", [64, SEQ], BF16, "ExternalInput")
    fv = c.dram("fv", [128, NCH, 64], BF16, "ExternalInput")
    cq = c.dram("cq", [8, 1024], F32, "ExternalInput")
    id8d = c.dram("id8", [8, 8], F32, "ExternalInput")
    ck = c.dram("ck", [128, NCH], F32, "ExternalInput")
    Ttm = c.dram("Ttm", [128, 8], F32, "ExternalInput")
    mskd = c.dram("msk", [128, 8, 512], BF16, "ExternalInput")
    idbd = c.dram("idb", [128, 128], BF16, "ExternalInput")
    mnegd = c.dram("mneg", [128, 128], F32, "ExternalInput")
    fo = c.dram("fo", [64, 8192], BF16, "ExternalOutput")
    gq = c.dram("gq", [2, 64, SEQ], BF16, "ExternalInput")
    gk = c.dram("gk", [2, 64, SEQ], BF16, "ExternalInput")
    gktm = c.dram("gktm", [2, 128, NCH, 64], BF16, "ExternalInput")
    gvtm = c.dram("gvtm", [2, 128, NCH, 64], BF16, "ExternalInput")
    gb = c.dram("gb", [2, 64, SEQ], F32, "ExternalInput")
    gbtm = c.dram("gbtm", [2, 128, NCH, 64], F32, "ExternalInput")
    gblast = c.dram("gblast", [2, 64, NCH], F32, "ExternalInput")
    gc128 = c.dram("gc128", [2, 128, SEQ], F32, "ExternalInput")
    gctm = c.dram("gctm", [2, 128, NCH], F32, "ExternalInput")
    gclast = c.dram("gclast", [2, 128, NCH], F32, "ExternalInput")
    go = c.dram("go", [2, 64, SEQ], F32, "ExternalOutput")

    GT = 2048
    q_g = c.rot("q_g", 2, [64, GT], BF16)
    k_g = c.rot("k_g", 2, [64, GT], BF16)
    ktm_g = c.rot("ktm_g", 2, [128, 16, 64], BF16)
    vtm_g = c.rot("vtm_g", 2, [128, 16, 64], BF16)
    b_g = c.rot("b_g", 2, [64, GT], F32)
    btm_g = c.rot("btm_g", 2, [128, 16, 64], F32)
    c_g = c.rot("c_g", 2, [128, GT], F32)
    ost = c.rot("ost", 2, [64, GT], F32)
    blast = c.sb("blast", [64, NCH], F32)
    ctm = c.sb("ctm", [128, NCH], F32)
    clast = c.sb("clast", [128, NCH], F32)
    cend = c.sb("cend", [128, NCH], F32)
    nctm = c.sb("nctm", [128, NCH], F32)
    dtot = c.sb("dtot", [64, NCH], F32)
    eb = c.sb("eb", [64, NCH], F32)
    mneg = c.sb("mneg_s", [128, 128], F32)
    S = c.sb("S", [64, 64], F32)
    Sb = c.sb("Sb", [64, 64], BF16)
    e1 = c.rot("e1_", 2, [64, 128], F32)
    e2 = c.rot("e2_", 2, [64, 128], F32)
    ec = c.rot("ec_", 2, [64, 128], F32)
    e13 = c.rot("e13_", 2, [64, 128], F32)
    qd = c.rot("qd", 2, [64, 128], BF16)
    ki = c.rot("ki", 2, [64, 128], BF16)
    qt = c.rot("qt", 2, [64, 128], BF16)
    tmpd = c.rot("tmpd", 2, [128, 128], F32)
    Dm = c.rot("Dm", 2, [128, 128], F32)
    AD = c.rot("AD", 2, [128, 128], BF16)
    tk = c.rot("tk", 2, [128, 64], F32)
    kie = c.rot("kie", 2, [128, 64], BF16)
    t3 = c.rot("t3_", 2, [64, 64], F32)
    psg = c.rot("psg", 2, [128, 512], F32, psum=True)

    D(lambda e: e.dma_start(out=mneg[:], in_=mnegd), writes=["mneg_s"])
    for h2 in range(2):
        D(lambda e, h2=h2: e.dma_start(out=blast[:], in_=gblast[h2]), writes=["blast"])
        D(lambda e, h2=h2: e.dma_start(out=ctm[:], in_=gctm[h2]), writes=["ctm"])
        D(lambda e, h2=h2: e.dma_start(out=clast[:], in_=gclast[h2]), writes=["clast"])
        p.op("dve", lambda e: e.tensor_tensor(out=cend[:], in0=clast[:], in1=ctm[:], op=ALU.subtract), reads=["clast", "ctm"], writes=["cend"])
        p.op("dve", lambda e: e.tensor_scalar(out=nctm[:], in0=ctm[:], scalar1=-1.0, scalar2=None, op0=ALU.mult), reads=["ctm"], writes=["nctm"])
        p.op("dve", lambda e: e.tensor_tensor(out=dtot[:], in0=blast[:], in1=clast[0:64, :], op=ALU.add), reads=["blast", "clast"], writes=["dtot"])
        p.op("act", lambda e: e.activation(out=dtot[:], in_=dtot[:], func=AF.Exp), reads=["dtot"], writes=["dtot"])
        p.op("act", lambda e: e.activation(out=eb[:], in_=blast[:], func=AF.Exp), reads=["blast"], writes=["eb"])
        p.op("pool", lambda e: e.memset(S[:], 0.0), writes=["S"])
        p.op("pool", lambda e: e.memset(Sb[:], 0.0), writes=["Sb"])
        for G in range(SEQ // GT):
            qg, kg, ktg, vtg, bg, btg, cg, og = q_g.next(), k_g.next(), ktm_g.next(), vtm_g.next(), b_g.next(), btm_g.next(), c_g.next(), ost.next()
            cs0 = G * GT
            D(lambda e, qg=qg, h2=h2, cs0=cs0: e.dma_start(out=qg[:], in_=gq[h2, :, cs0:cs0 + GT]), writes=[qg.name])
            D(lambda e, kg=kg, h2=h2, cs0=cs0: e.dma_start(out=kg[:], in_=gk[h2, :, cs0:cs0 + GT]), writes=[kg.name])
            D(lambda e, ktg=ktg, h2=h2, G=G: e.dma_start(out=ktg[:], in_=gktm[h2, :, 16 * G:16 * G + 16, :]), writes=[ktg.name])
            D(lambda e, vtg=vtg, h2=h2, G=G: e.dma_start(out=vtg[:], in_=gvtm[h2, :, 16 * G:16 * G + 16, :]), writes=[vtg.name])
            D(lambda e, bg=bg, h2=h2, cs0=cs0: e.dma_start(out=bg[:], in_=gb[h2, :, cs0:cs0 + GT]), writes=[bg.name])
            D(lambda e, btg=btg, h2=h2, G=G: e.dma_start(out=btg[:], in_=gbtm[h2, :, 16 * G:16 * G + 16, :]), writes=[btg.name])
            D(lambda e, cg=cg, h2=h2, cs0=cs0: e.dma_start(out=cg[:], in_=gc128[h2, :, cs0:cs0 + GT]), writes=[cg.name])
            for j in range(16):
                jj = 16 * G + j
                sl = slice(128 * j, 128 * (j + 1))
                a1, a2, ac, a13 = e1.next(), e2.next(), ec.next(), e13.next()
                p.op("act", lambda e, a1=a1, bg=bg, sl=sl: e.activation(out=a1[:], in_=bg[:, sl], func=AF.Exp), reads=[bg.name], writes=[a1.name])
                p.op("act", lambda e, a2=a2, bg=bg, sl=sl: e.activation(out=a2[:], in_=bg[:, sl], func=AF.Exp, scale=-1.0), reads=[bg.name], writes=[a2.name])
                p.op("act", lambda e, ac=ac, cg=cg, sl=sl: e.activation(out=ac[:], in_=cg[0:64, sl], func=AF.Exp), reads=[cg.name], writes=[ac.name])
                qd_, ki_, qt_ = qd.next(), ki.next(), qt.next()
                p.op("dve", lambda e, qd_=qd_, qg=qg, a1=a1, sl=sl: e.tensor_tensor(out=qd_[:], in0=qg[:, sl], in1=a1[:], op=ALU.mult), reads=[qg.name, a1.name], writes=[qd_.name])
                p.op("dve", lambda e, ki_=ki_, kg=kg, a2=a2, sl=sl: e.tensor_tensor(out=ki_[:], in0=kg[:, sl], in1=a2[:], op=ALU.mult), reads=[kg.name, a2.name], writes=[ki_.name])
                p.op("dve", lambda e, a13=a13, a1=a1, ac=ac: e.tensor_tensor(out=a13[:], in0=a1[:], in1=ac[:], op=ALU.mult), reads=[a1.name, ac.name], writes=[a13.name])
                p.op("dve", lambda e, qt_=qt_, qg=qg, a13=a13, sl=sl: e.tensor_tensor(out=qt_[:], in0=qg[:, sl], in1=a13[:], op=ALU.mult), reads=[qg.name, a13.name], writes=[qt_.name])
                ps = psg.next()
                p.op("pe", lambda e, ps=ps, ki_=ki_, qd_=qd_: e.matmul(ps[:, 0:128], ki_[:], qd_[:], start=True, stop=True), reads=[ki_.name, qd_.name], writes=[(ps.name, 0)])
                td, dm, ad = tmpd.next(), Dm.next(), AD.next()
                p.op("dve", lambda e, td=td, cg=cg, sl=sl, jj=jj: e.scalar_tensor_tensor(out=td[:], in0=cg[:, sl], scalar=nctm[:, jj:jj + 1], in1=mneg[:], op0=ALU.add, op1=ALU.add), reads=[cg.name, "nctm", "mneg_s"], writes=[td.name])
                p.op("act", lambda e, td=td, dm=dm: e.activation(out=dm[:], in_=td[:], func=AF.Exp), reads=[td.name], writes=[dm.name])
                p.op("dve", lambda e, ad=ad, ps=ps, dm=dm: e.tensor_tensor(out=ad[:], in0=ps[:, 0:128], in1=dm[:], op=ALU.mult), reads=[(ps.name, 0), dm.name], writes=[ad.name])

                def mo(e, ps=ps, vtg=vtg, j=j, ad=ad, qt_=qt_):
                    e.matmul(ps[0:64, 128:256], vtg[:, j, :], ad[:], start=True, stop=False)
                    return e.matmul(ps[0:64, 128:256], Sb[:], qt_[:], start=False, stop=True)
                p.op("pe", mo, reads=[vtg.name, ad.name, "Sb", qt_.name], writes=[(ps.name, 1)])
                p.op("act", lambda e, og=og, ps=ps, sl=sl: e.activation(out=og[:, sl], in_=ps[0:64, 128:256], func=AF.Copy), reads=[(ps.name, 1)], writes=[(og.name, j)])
                tk_, kie_, t3_ = tk.next(), kie.next(), t3.next()
                p.op("act", lambda e, tk_=tk_, btg=btg, j=j, jj=jj: e.activation(out=tk_[:], in_=btg[:, j, :], func=AF.Exp, scale=-1.0, bias=cend[:, jj:jj + 1]), reads=[btg.name, "cend"], writes=[tk_.name])
                p.op("dve", lambda e, kie_=kie_, ktg=ktg, j=j, tk_=tk_: e.tensor_tensor(out=kie_[:], in0=ktg[:, j, :], in1=tk_[:], op=ALU.mult), reads=[ktg.name, tk_.name], writes=[kie_.name])
                p.op("pe", lambda e, ps=ps, kie_=kie_, vtg=vtg, j=j: e.matmul(ps[0:64, 256:320], kie_[:], vtg[:, j, :], start=True, stop=True), reads=[kie_.name, vtg.name], writes=[(ps.name, 2)])
                p.op("dve", lambda e, t3_=t3_, ps=ps, jj=jj: e.tensor_scalar(out=t3_[:], in0=ps[0:64, 256:320], scalar1=eb[:, jj:jj + 1], scalar2=None, op0=ALU.mult), reads=[(ps.name, 2), "eb"], writes=[t3_.name])
                p.op("dve", lambda e, t3_=t3_, jj=jj: e.scalar_tensor_tensor(out=S[:], in0=S[:], scalar=dtot[:, jj:jj + 1], in1=t3_[:], op0=ALU.mult, op1=ALU.add), reads=["S", "dtot", t3_.name], writes=["S"])
                p.op("act", lambda e: e.activation(out=Sb[:], in_=S[:], func=AF.Copy), reads=["S"], writes=["Sb"])
            D(lambda e, og=og, h2=h2, cs0=cs0: e.dma_start(out=go[h2, :, cs0:cs0 + GT], in_=og[:]), reads=[og.name])

    QT = c.sb("QT", [67, 8192], BF16)
    KT = c.sb("KT", [67, SEQ], BF16)
    V = c.sb("V", [128, NCH, 64], BF16)
    msk = c.sb("msk_s", [128, 8, 512], BF16)
    idb = c.sb("idb_s", [128, 128], BF16)
    onesb = c.sb("onesb", [128, 64], BF16)
    cqr = c.sb("cqr", [8, 1024], F32)
    r1 = c.sb("r1", [8, 1024], F32)
    hi = c.sb("hi", [8, 1024], BF16)
    mid = c.sb("mid", [8, 1024], BF16)
    lo = c.sb("lo", [8, 1024], BF16)
    id8 = c.sb("id8_s", [8, 8], F32)
    od8 = c.sb("od8", [8, 8], F32)
    offd = c.sb("offd", [8, 1], F32)
    negc = c.sb("negc", [128, NCH], F32)
    Tt = c.sb("Tt", [128, 8], F32)
    offk = c.sb("offk", [128, 8], F32)
    pt = c.rot("pt", 3, [128, 512], BF16)
    rec = c.rot("rec", 2, [64, 512], F32)
    ofo = c.rot("ofo", 2, [64, 512], BF16)
    ps_s = c.rot("ps_s", 3, [128, 512], F32, psum=True)
    ps_o = c.ps("ps_o")
    ps_d = c.ps("ps_d")

    D(lambda e: e.dma_start(out=QT[0:64, :], in_=fqT), writes=[("QT", 0)])
    D(lambda e: e.dma_start(out=KT[0:64, :], in_=fkT), writes=[("KT", 0)])
    D(lambda e: e.dma_start(out=V[:], in_=fv), writes=["V"])
    D(lambda e: e.dma_start(out=msk[:], in_=mskd), writes=["msk_s"])
    D(lambda e: e.dma_start(out=idb[:], in_=idbd), writes=["idb_s"])
    D(lambda e: e.dma_start(out=cqr[:], in_=cq), writes=["cqr"])
    D(lambda e: e.dma_start(out=id8[:], in_=id8d), writes=["id8_s"])
    D(lambda e: e.dma_start(out=negc[:], in_=ck), writes=["negc"])
    D(lambda e: e.dma_start(out=Tt[:], in_=Ttm), writes=["Tt"])
    p.op("pool", lambda e: e.memset(onesb[:], 1.0), writes=["onesb"])
    p.op("pool", lambda e: e.memset(KT[64:67, :], 1.0), writes=[("KT", 1)])
    p.op("pool", lambda e: e.memset(offk[:], 0.0), writes=["offk"])
    for j in range(1, 8):
        p.op("dve", lambda e, j=j: e.tensor_tensor(out=offk[:, j:j + 1], in0=offk[:, j - 1:j], in1=Tt[:, j - 1:j], op=ALU.add), reads=["offk", "Tt"], writes=["offk"])
    for j in range(1, 8):
        p.op("dve", lambda e, j=j: e.tensor_scalar(out=negc[:, 16 * j:16 * j + 16], in0=negc[:, 16 * j:16 * j + 16], scalar1=offk[:, j:j + 1], scalar2=None, op0=ALU.add), reads=["negc", "offk"], writes=["negc"])
    p.op("dve", lambda e: e.tensor_scalar(out=negc[:], in0=negc[:], scalar1=-1.0, scalar2=None, op0=ALU.mult), reads=["negc"], writes=["negc"])
    p.op("dve", lambda e: e.tensor_tensor(out=od8[:], in0=offk[0:8, 0:8], in1=id8[:], op=ALU.mult), reads=["offk", "id8_s"], writes=["od8"])
    p.op("dve", lambda e: e.reduce_sum(out=offd[:], in_=od8[:], axis=AX.X), reads=["od8"], writes=["offd"])
    p.op("dve", lambda e: e.tensor_scalar(out=cqr[:], in0=cqr[:], scalar1=offd[:, 0:1], scalar2=None, op0=ALU.add), reads=["cqr", "offd"], writes=["cqr"])
    p.op("dve", lambda e: e.tensor_copy(out=hi[:], in_=cqr[:]), reads=["cqr"], writes=["hi"])
    p.op("dve", lambda e: e.tensor_tensor(out=r1[:], in0=cqr[:], in1=hi[:], op=ALU.subtract), reads=["cqr", "hi"], writes=["r1"])
    p.op("dve", lambda e: e.tensor_copy(out=mid[:], in_=r1[:]), reads=["r1"], writes=["mid"])
    p.op("dve", lambda e: e.tensor_tensor(out=r1[:], in0=r1[:], in1=mid[:], op=ALU.subtract), reads=["r1", "mid"], writes=["r1"])
    p.op("dve", lambda e: e.tensor_copy(out=lo[:], in_=r1[:]), reads=["r1"], writes=["lo"])
    for j in range(8):
        D(lambda e, j=j: e.dma_start(out=QT[64:65, 1024 * j:1024 * (j + 1)], in_=hi[j:j + 1, :]), reads=["hi"], writes=[("QT", 10 + j)])
        D(lambda e, j=j: e.dma_start(out=QT[65:66, 1024 * j:1024 * (j + 1)], in_=mid[j:j + 1, :]), reads=["mid"], writes=[("QT", 20 + j)])
        D(lambda e, j=j: e.dma_start(out=QT[66:67, 1024 * j:1024 * (j + 1)], in_=lo[j:j + 1, :]), reads=["lo"], writes=[("QT", 30 + j)])

    for i in range(16):
        nkb = 8 * i + 8
        qs = slice(512 * i, 512 * (i + 1))
        pend = {}

        def mmS(kb, i=i, qs=qs):
            ps = ps_s.next()
            pend[kb] = ps

            def f(e, ps=ps, kb=kb):
                ins = e.matmul(ps[:], KT[0:67, 128 * kb:128 * (kb + 1)], QT[0:67, qs], start=True, stop=(kb < 8 * i))
                if kb >= 8 * i:
                    ins = e.matmul(ps[:], idb[:], msk[:, kb - 8 * i, :], start=False, stop=True)
                return ins
            p.op("pe", f, reads=["KT", "QT", "idb_s", "msk_s"], writes=[ps.name])

        mmS(0)
        for kb in range(nkb):
            if kb + 1 < nkb:
                mmS(kb + 1)
            ps = pend.pop(kb)
            t_ = pt.next()
            p.op("act", lambda e, t_=t_, ps=ps, kb=kb: e.activation(out=t_[:], in_=ps[:], func=AF.Exp, bias=negc[:, kb:kb + 1]), reads=[ps.name, "negc"], writes=[t_.name])

            def pv(e, t_=t_, kb=kb, nkb=nkb):
                e.matmul(ps_o[0:64, :], V[:, kb, :], t_[:], start=(kb == 0), stop=(kb == nkb - 1))
                return e.matmul(ps_d[0:64, :], onesb[:], t_[:], start=(kb == 0), stop=(kb == nkb - 1))
            p.op("pe", pv, reads=["V", t_.name, "onesb"], writes=["ps_o", "ps_d"])
        r_, o_ = rec.next(), ofo.next()
        p.op("dve", lambda e, r_=r_: e.reciprocal(out=r_[:], in_=ps_d[0:64, :]), reads=["ps_d"], writes=[r_.name])
        p.op("dve", lambda e, r_=r_, o_=o_: e.tensor_tensor(out=o_[:], in0=ps_o[0:64, :], in1=r_[:], op=ALU.mult), reads=["ps_o", r_.name], writes=[o_.name])
        D(lambda e, o_=o_, qs=qs: e.dma_start(out=fo[:, qs], in_=o_[:]), reads=[o_.name])
    run_prog(nc, p)
    return nc


_NC = {}


def _get(name, fn):
    if name not in _NC:
        _NC[name] = fn()
    return _NC[name]


def run_l1(xfull, P, pos):
    nc = _get("l1", build_l1)
    w1 = np.ascontiguousarray(P["w_in"][:, l1_weight_cols()])
    ln1 = np.ascontiguousarray(P["ln1"].reshape(8, 128).T)
    pp = l1_params(P)
    cst = l1_consts()
    maps = []
    for c in range(8):
        t0 = c * NT
        xs = np.zeros((1024, HP + NT), np.float32)
        xs[:, HP:] = xfull[t0:t0 + NT].T
        if c > 0:
            xs[:, 1:HP] = xfull[t0 - 3:t0].T
        maps.append(dict(xT=xs, w1=w1, ln1=ln1, pp=pp, w2=np.ascontiguousarray(P["gla_w2"]),
                         pos=np.ascontiguousarray(np.broadcast_to(pos[t0:t0 + NT], (128, NT))).astype(np.int32), cst=cst))
    res = run_bass_kernel_spmd(nc, maps, core_ids=list(range(8))).results
    OB = np.concatenate([r["ob"] for r in res], axis=1)
    OF = np.concatenate([r["of"] for r in res], axis=1)
    return OB, OF


def _tm(a):
    R = a.shape[0]
    return np.ascontiguousarray(a.T.reshape(NCH, 128, R).transpose(1, 0, 2))


def run_l2(OB, OF):
    nc = _get("l2", build_l2)
    bf = OB.dtype
    idb = np.eye(128, dtype=np.float32).astype(bf)
    s_ = np.arange(128)[:, None]
    mneg = np.where(np.arange(128)[None, :] >= s_, 0.0, -1.0e4).astype(np.float32)
    id8 = np.eye(8, dtype=np.float32)
    lg = np.log(1.0 - 2.0 ** (-5.0 - np.arange(4, dtype=np.float32))).astype(np.float32)
    zeros_b = np.zeros((64, SEQ), np.float32)
    heads = []
    for h in range(4):
        heads.append(dict(q=OB[768 + 64 * h:832 + 64 * h], k=OB[1024 + 64 * h:1088 + 64 * h], v=OB[1280 + 64 * h:1344 + 64 * h],
                          b=OF[4 + 64 * h:68 + 64 * h], c=np.zeros(SEQ, np.float32)))
    for h in range(4):
        cc = np.tile((np.arange(128, dtype=np.float32) + 1.0) * lg[h], NCH).astype(np.float32)
        heads.append(dict(q=OB[1536 + 64 * h:1600 + 64 * h], k=OB[1792 + 64 * h:1856 + 64 * h], v=OB[2048 + 64 * h:2112 + 64 * h], b=zeros_b, c=cc))
    for h in range(8):
        g = h // 4
        heads.append(dict(q=OB[3456 + 64 * g:3520 + 64 * g], k=OB[3328 + 64 * g:3392 + 64 * g], v=OB[2816 + 64 * h:2880 + 64 * h], b=zeros_b, c=OF[260 + 64 * h]))
    maps = []
    for c in range(8):
        hf, par = c // 2, c % 2
        m = {}
        m["fqT"] = np.ascontiguousarray(OB[64 * hf:64 * hf + 64].reshape(64, 32, 512)[:, par::2].reshape(64, 8192))
        m["fkT"] = np.ascontiguousarray(OB[256 + 64 * hf:320 + 64 * hf])
        m["fv"] = _tm(OB[512 + 64 * hf:576 + 64 * hf])
        crow = OF[hf]
        m["cq"] = np.ascontiguousarray(crow.reshape(32, 512)[par::2].reshape(8, 1024))
        m["ck"] = np.ascontiguousarray(crow.reshape(NCH, 128).T)
        T = crow[NT - 1::NT]
        m["Ttm"] = np.ascontiguousarray(np.broadcast_to(T, (128, 8))).astype(np.float32)
        q_ = np.arange(512)[None, None, :]
        jb = np.arange(8)[None, :, None]
        ss = np.arange(128)[:, None, None]
        m["msk"] = np.where(128 * jb + ss <= 512 * par + q_, 0.0, -30000.0).astype(np.float32).astype(bf)
        m["idb"] = idb
        m["mneg"] = mneg
        m["id8"] = id8
        hs = [heads[2 * c], heads[2 * c + 1]]
        m["gq"] = np.stack([h["q"] for h in hs])
        m["gk"] = np.stack([h["k"] for h in hs])
        m["gktm"] = np.stack([_tm(h["k"]) for h in hs])
        m["gvtm"] = np.stack([_tm(h["v"]) for h in hs])
        m["gb"] = np.stack([h["b"] for h in hs]).astype(np.float32)
        m["gbtm"] = np.stack([_tm(h["b"]) for h in hs]).astype(np.float32)
        m["gblast"] = np.stack([np.ascontiguousarray(h["b"][:, 127::128]) for h in hs]).astype(np.float32)
        m["gc128"] = np.stack([np.ascontiguousarray(np.broadcast_to(h["c"], (128, SEQ))) for h in hs]).astype(np.float32)
        m["gctm"] = np.stack([np.ascontiguousarray(h["c"].reshape(NCH, 128).T) for h in hs]).astype(np.float32)
        m["gclast"] = np.stack([np.ascontiguousarray(np.broadcast_to(h["c"][127::128], (128, NCH))) for h in hs]).astype(np.float32)
        maps.append(m)
    res = run_bass_kernel_spmd(nc, maps, core_ids=list(range(8))).results
    FO = np.zeros((256, SEQ), dtype=bf)
    GO = np.zeros((16, 64, SEQ), np.float32)
    for c in range(8):
        hf, par = c // 2, c % 2
        FO[64 * hf:64 * hf + 64].reshape(64, 32, 512)[:, par::2] = res[c]["fo"].reshape(64, 16, 512)
        GO[2 * c:2 * c + 2] = res[c]["go"]
    return FO, GO


def build_l3():
    c = Ctx()
    nc, p = c.nc, c.p
    D = p.dma
    xT = c.dram("xT", [1024, NT], F32, "ExternalInput")
    w3 = c.dram("w3", [1024, 5120], F32, "ExternalInput")
    wup = c.dram("wup", [1280, 1024], F32, "ExternalInput")
    wout = c.dram("wout", [1024, 1024], F32, "ExternalInput")
    wfi = c.dram("wfi", [1024, 5632], F32, "ExternalInput")
    wfo = c.dram("wfo", [2816, 1024], F32, "ExternalInput")
    ln1 = c.dram("ln1", [128, 8], F32, "ExternalInput")
    ln2 = c.dram("ln2", [128, 8], F32, "ExternalInput")
    pp = c.dram("pp", [128, 16], F32, "ExternalInput")
    ya = c.dram("ya", [256, NT], BF16, "ExternalInput")
    goT = c.dram("goT", [1024, NT], F32, "ExternalInput")
    xsT = c.dram("xsT", [512, NT], BF16, "ExternalInput")
    xo = c.dram("xo", [1024, NT], F32, "ExternalOutput")

    xt = c.rot("xt", 2, [128, 8, TT], F32)
    hT = c.sb("hT", [128, 8, TT], BF16)
    Y = c.sb("Y", [128, 10, TT], BF16)
    mg = c.sb("mg", [128, 8, TT], BF16)
    wupt = c.sb("wupt", [128, 10, 1024], BF16)
    woutt = c.sb("woutt", [128, 8, 1024], BF16)
    ws = c.rot("ws", 3, [128, 8, 512], BF16)
    wfs = c.rot("wfs", 2, [128, 4, 1024], BF16)
    act = c.rot("actb", 2, [128, 4, TT], BF16)
    sq = c.rot("sq", 3, [128, TT], BF16)
    rs = c.rot("rs", 2, [128, TT], F32)
    og = c.rot("og", 3, [128, TT], F32)
    xsb = c.rot("xsb", 2, [128, TT], BF16)
    sg = c.rot("sg", 3, [128, TT], F32)
    ta = c.rot("ta", 3, [128, TT], F32)
    u2 = c.rot("u2_", 4, [128, TT], F32)
    mt = c.rot("mt", 2, [128, TT], F32)
    ones = c.sb("ones", [128, 128], BF16)
    bd = c.sb("bd", [128, 128], BF16)
    lnw1 = c.sb("lnw1", [128, 8], F32)
    lnw2 = c.sb("lnw2", [128, 8], F32)
    ppt = c.sb("ppt", [128, 16], F32)
    epsb = c.sb("epsb", [128, 1], F32)
    psA = c.rot("psA", 4, [128, TT], F32, psum=True)
    psB = c.rot("psB", 3, [128, TT], F32, psum=True)

    D(lambda e: e.dma_start(out=lnw1[:], in_=ln1), writes=["lnw1"])
    D(lambda e: e.dma_start(out=lnw2[:], in_=ln2), writes=["lnw2"])
    D(lambda e: e.dma_start(out=ppt[:], in_=pp), writes=["ppt"])
    D(lambda e: e.dma_start(out=wupt[:], in_=wup.rearrange("(kc p) m -> p kc m", p=128)), writes=["wupt"], eng="pool")
    D(lambda e: e.dma_start(out=woutt[:], in_=wout.rearrange("(kc p) m -> p kc m", p=128)), writes=["woutt"], eng="pool")
    p.op("pool", lambda e: e.memset(ones[:], 1.0), writes=["ones"])
    p.op("pool", lambda e: e.memset(bd[:], 0.0), writes=["bd"])
    p.op("pool", lambda e: e.memset(bd[0:64, 0:64], 1.0), reads=["bd"], writes=["bd"])
    p.op("pool", lambda e: e.memset(bd[64:128, 64:128], 1.0), reads=["bd"], writes=["bd"])
    p.op("pool", lambda e: e.memset(epsb[:], EPS), writes=["epsb"])

    def rms(x_, lnw):
        ps = psA.next()
        for kc in range(8):
            s = sq.next()
            p.op("act", lambda e, s=s, kc=kc: e.activation(out=s[:], in_=x_[:, kc, :], func=AF.Square), reads=[x_.name], writes=[s.name])
            p.op("pe", lambda e, s=s, kc=kc: e.matmul(ps[:], ones[:], s[:], start=(kc == 0), stop=(kc == 7)), reads=[s.name, "ones"], writes=[ps.name])
        r = rs.next()
        p.op("act", lambda e: e.activation(out=r[:], in_=ps[:], func=AF.Ln, scale=1.0 / 1024, bias=epsb[:, 0:1]), reads=[ps.name, "epsb"], writes=[r.name])
        p.op("act", lambda e: e.activation(out=r[:], in_=r[:], func=AF.Exp, scale=-0.5), reads=[r.name], writes=[r.name])
        for kc in range(8):
            p.op("dve", lambda e, kc=kc: e.scalar_tensor_tensor(out=hT[:, kc, :], in0=x_[:, kc, :], scalar=lnw[:, kc:kc + 1], in1=r[:], op0=ALU.mult, op1=ALU.mult),
                 reads=[x_.name, lnw.name, r.name], writes=[("hT", kc)])

    def proj(ps, w, c0, m=128):
        def f(e):
            ins = None
            for kc in range(8):
                ins = e.matmul(ps[0:m, :], w[:, kc, c0:c0 + m], hT[:, kc, :], start=(kc == 0), stop=(kc == 7))
            return ins
        p.op("pe", f, reads=[w.name, "hT"], writes=[ps.name])

    def load_w(src, c0, ncols=512):
        w = ws.next()
        D(lambda e: e.dma_start(out=w[:, :, 0:ncols], in_=src[:, c0:c0 + ncols].rearrange("(kc p) m -> p kc m", p=128)), writes=[w.name], eng="pool")
        return w

    for tt in range(NTT):
        ts_ = slice(tt * TT, (tt + 1) * TT)
        x_ = xt.next()
        D(lambda e, x_=x_, ts_=ts_: e.dma_start(out=x_[:], in_=xT[:, ts_].rearrange("(kc p) t -> p kc t", p=128)), writes=[x_.name])
        rms(x_, lnw1)
        D(lambda e, ts_=ts_: e.dma_start(out=Y[:, 0:2, :], in_=ya[:, ts_].rearrange("(c p) t -> p c t", p=128)), writes=[("Y", 0)])
        w = load_w(w3, 0)
        for i4 in range(4):
            o_ = og.next()
            D(lambda e, o_=o_, i4=i4, ts_=ts_: e.dma_start(out=o_[:], in_=goT[128 * i4:128 * (i4 + 1), ts_]), writes=[o_.name])
            s = sq.next()
            p.op("act", lambda e, s=s, o_=o_: e.activation(out=s[:], in_=o_[:], func=AF.Square), reads=[o_.name], writes=[s.name])
            ps2 = psB.next()
            p.op("pe", lambda e, ps2=ps2, s=s: e.matmul(ps2[:], bd[:], s[:], start=True, stop=True), reads=["bd", s.name], writes=[ps2.name])
            r = rs.next()
            p.op("act", lambda e, r=r, ps2=ps2: e.activation(out=r[:], in_=ps2[:], func=AF.Ln, scale=1.0 / 64, bias=epsb[:, 0:1]), reads=[ps2.name, "epsb"], writes=[r.name])
            p.op("act", lambda e, r=r: e.activation(out=r[:], in_=r[:], func=AF.Exp, scale=-0.5), reads=[r.name], writes=[r.name])
            ps = psA.next()
            proj(ps, w, 128 * i4)
            g_ = sg.next()
            p.op("act", lambda e, g_=g_, ps=ps: e.activation(out=g_[:], in_=ps[:], func=AF.Silu), reads=[ps.name], writes=[g_.name])
            t_ = ta.next()
            col = 0 if i4 < 2 else 1
            p.op("dve", lambda e, t_=t_, o_=o_, r=r, col=col: e.scalar_tensor_tensor(out=t_[:], in0=o_[:], scalar=ppt[:, col:col + 1], in1=r[:], op0=ALU.mult, op1=ALU.mult), reads=[o_.name, "ppt", r.name], writes=[t_.name])
            p.op("pool", lambda e, t_=t_, g_=g_, i4=i4: e.tensor_tensor(out=Y[:, 2 + i4, :], in0=t_[:], in1=g_[:], op=ALU.mult), reads=[t_.name, g_.name], writes=[("Y", 2 + i4)])
        w = load_w(w3, 512)
        us = []
        for j in range(4):
            o_ = og.next()
            D(lambda e, o_=o_, j=j, ts_=ts_: e.dma_start(out=o_[:], in_=goT[512 + 128 * j:640 + 128 * j, ts_]), writes=[o_.name])
            xb = xsb.next()
            D(lambda e, xb=xb, j=j, ts_=ts_: e.dma_start(out=xb[:], in_=xsT[128 * j:128 * (j + 1), ts_]), writes=[xb.name])
            t_ = ta.next()
            p.op("dve", lambda e, t_=t_, xb=xb, o_=o_, j=j: e.scalar_tensor_tensor(out=t_[:], in0=xb[:], scalar=ppt[:, 2 + j:3 + j], in1=o_[:], op0=ALU.mult, op1=ALU.add), reads=[xb.name, "ppt", o_.name], writes=[t_.name])
            ps = psA.next()
            proj(ps, w, 128 * j)
            g_ = sg.next()
            p.op("act", lambda e, g_=g_, ps=ps: e.activation(out=g_[:], in_=ps[:], func=AF.Silu), reads=[ps.name], writes=[g_.name])
            u_ = u2.next()
            p.op("pool", lambda e, u_=u_, t_=t_, g_=g_: e.tensor_tensor(out=u_[:], in0=t_[:], in1=g_[:], op=ALU.mult), reads=[t_.name, g_.name], writes=[u_.name])
            us.append(u_)
        for gI in range(2):
            ps2 = psB.next()
            for k2 in range(2):
                s = sq.next()
                u_ = us[2 * gI + k2]
                p.op("act", lambda e, s=s, u_=u_: e.activation(out=s[:], in_=u_[:], func=AF.Square), reads=[u_.name], writes=[s.name])
                p.op("pe", lambda e, ps2=ps2, s=s, k2=k2: e.matmul(ps2[:], ones[:], s[:], start=(k2 == 0), stop=(k2 == 1)), reads=["ones", s.name], writes=[ps2.name])
            r = rs.next()
            p.op("act", lambda e, r=r, ps2=ps2: e.activation(out=r[:], in_=ps2[:], func=AF.Ln, scale=1.0 / 256, bias=epsb[:, 0:1]), reads=[ps2.name, "epsb"], writes=[r.name])
            p.op("act", lambda e, r=r: e.activation(out=r[:], in_=r[:], func=AF.Exp, scale=-0.5), reads=[r.name], writes=[r.name])
            for k2 in range(2):
                j = 2 * gI + k2
                u_ = us[j]
                p.op("dve", lambda e, u_=u_, r=r, j=j: e.scalar_tensor_tensor(out=Y[:, 6 + j, :], in0=u_[:], scalar=ppt[:, 6 + j:7 + j], in1=r[:], op0=ALU.mult, op1=ALU.mult), reads=[u_.name, "ppt", r.name], writes=[("Y", 6 + j)])
        kcs = [(0, 2), (2, 2), (4, 2), (6, 4)]
        for n in range(8):
            w = load_w(w3, 1024 + 512 * n)
            m_ = mt.next()
            for b in range(4):
                psg_ = psA.next()
                proj(psg_, w, 128 * b)
                g_ = sg.next()
                p.op("act", lambda e, g_=g_, psg_=psg_: e.activation(out=g_[:], in_=psg_[:], func=AF.Sigmoid), reads=[psg_.name], writes=[g_.name])
                psu = psB.next()
                k0, nk = kcs[b]

                def fu(e, psu=psu, k0=k0, nk=nk, n=n):
                    ins = None
                    for q in range(nk):
                        ins = e.matmul(psu[:], wupt[:, k0 + q, 128 * n:128 * (n + 1)], Y[:, k0 + q, :], start=(q == 0), stop=(q == nk - 1))
                    return ins
                p.op("pe", fu, reads=["wupt", "Y"], writes=[psu.name])
                if b == 0:
                    p.op("dve", lambda e, m_=m_, psu=psu, g_=g_: e.tensor_tensor(out=m_[:], in0=psu[:], in1=g_[:], op=ALU.mult), reads=[psu.name, g_.name], writes=[m_.name])
                else:
                    t_ = ta.next()
                    p.op("dve", lambda e, t_=t_, psu=psu, g_=g_: e.tensor_tensor(out=t_[:], in0=psu[:], in1=g_[:], op=ALU.mult), reads=[psu.name, g_.name], writes=[t_.name])
                    if b < 3:
                        p.op("pool", lambda e, m_=m_, t_=t_: e.tensor_tensor(out=m_[:], in0=m_[:], in1=t_[:], op=ALU.add), reads=[m_.name, t_.name], writes=[m_.name])
                    else:
                        p.op("pool", lambda e, m_=m_, t_=t_, n=n: e.tensor_tensor(out=mg[:, n, :], in0=m_[:], in1=t_[:], op=ALU.add), reads=[m_.name, t_.name], writes=[("mg", n)])
        for n in range(8):
            ps = psA.next()

            def fo_(e, ps=ps, n=n):
                ins = None
                for kc in range(8):
                    ins = e.matmul(ps[:], woutt[:, kc, 128 * n:128 * (n + 1)], mg[:, kc, :], start=(kc == 0), stop=(kc == 7))
                return ins
            p.op("pe", fo_, reads=["woutt", "mg"], writes=[ps.name])
            p.op("dve", lambda e, ps=ps, n=n, x_=x_: e.tensor_tensor(out=x_[:, n, :], in0=x_[:, n, :], in1=ps[:], op=ALU.add), reads=[x_.name, ps.name], writes=[x_.name])
        rms(x_, lnw2)
        for hg in range(6):
            nh = 4 if hg < 5 else 2
            wg = load_w(wfi, 512 * hg, 128 * nh)
            wu = load_w(wfi, 2816 + 512 * hg, 128 * nh)
            wf = wfs.next()
            D(lambda e, wf=wf, hg=hg, nh=nh: e.dma_start(out=wf[:, 0:nh, :], in_=wfo[512 * hg:512 * hg + 128 * nh, :].rearrange("(kc p) m -> p kc m", p=128)), writes=[wf.name], eng="pool")
            a_ = act.next()
            for hc in range(nh):
                pg, pu = psA.next(), psB.next()
                proj(pg, wg, 128 * hc)
                proj(pu, wu, 128 * hc)
                g_ = sg.next()
                p.op("act", lambda e, g_=g_, pg=pg: e.activation(out=g_[:], in_=pg[:], func=AF.Silu), reads=[pg.name], writes=[g_.name])
                p.op("dve", lambda e, a_=a_, hc=hc, g_=g_, pu=pu: e.tensor_tensor(out=a_[:, hc, :], in0=pu[:], in1=g_[:], op=ALU.mult), reads=[pu.name, g_.name], writes=[(a_.name, hc)])
            for n in range(8):
                ps = psA.next()

                def ff_(e, ps=ps, n=n, wf=wf, a_=a_, nh=nh):
                    ins = None
                    for hc in range(nh):
                        ins = e.matmul(ps[:], wf[:, hc, 128 * n:128 * (n + 1)], a_[:, hc, :], start=(hc == 0), stop=(hc == nh - 1))
                    return ins
                p.op("pe", ff_, reads=[wf.name, a_.name], writes=[ps.name])
                p.op("dve", lambda e, ps=ps, n=n, x_=x_: e.tensor_tensor(out=x_[:, n, :], in0=x_[:, n, :], in1=ps[:], op=ALU.add), reads=[x_.name, ps.name], writes=[x_.name])
        D(lambda e, x_=x_, ts_=ts_: e.dma_start(out=xo[:, ts_].rearrange("(kc p) t -> p kc t", p=128), in_=x_[:]), reads=[x_.name])
    run_prog(nc, p)
    return nc


def l3_weight_cols():
    C = COLS
    r = np.arange
    gates = []
    for n in range(8):
        for b in range(4):
            gates.append(r(C["gates"] + 1024 * b + 128 * n, C["gates"] + 1024 * b + 128 * (n + 1)))
    return np.concatenate([r(C["gr"], C["gr"] + 256), r(C["rg"], C["rg"] + 256), r(C["z"], C["z"] + 512)] + gates)


def run_l3(xfull, P, FO, GO, OB):
    nc = _get("l3", build_l3)
    w3 = np.ascontiguousarray(P["w_in"][:, l3_weight_cols()])
    wup = np.ascontiguousarray(np.concatenate([P["w_up_a"], P["w_up_b"], P["w_up_c"], P["w_up_d"]], axis=0))
    pp = np.zeros((128, 16), np.float32)
    pp[:, 0] = np.tile(P["gla_norm"], 2)
    pp[:, 1] = np.tile(P["ret_norm"], 2)
    for j in range(4):
        pp[:, 2 + j] = np.repeat(P["ssd_d"][2 * j:2 * j + 2], 64)
        pp[:, 6 + j] = P["ssd_norm"][128 * j:128 * (j + 1)]
    ln1 = np.ascontiguousarray(P["ln1"].reshape(8, 128).T)
    ln2 = np.ascontiguousarray(P["ln2"].reshape(8, 128).T)
    GOf = GO.reshape(1024, SEQ)
    maps = []
    for c in range(8):
        sl = slice(c * NT, (c + 1) * NT)
        maps.append(dict(xT=np.ascontiguousarray(xfull[sl].T), w3=w3, wup=wup, wout=np.ascontiguousarray(P["w_out"]),
                         wfi=np.ascontiguousarray(P["w_ffn_in"]), wfo=np.ascontiguousarray(P["w_ffn_out"]), ln1=ln1, ln2=ln2, pp=pp,
                         ya=np.ascontiguousarray(FO[:, sl]), goT=np.ascontiguousarray(GOf[:, sl]), xsT=np.ascontiguousarray(OB[2304:2816, sl])))
    res = run_bass_kernel_spmd(nc, maps, core_ids=list(range(8))).results
    return np.concatenate([r["xo"].T for r in res], axis=0)


def kernel(**inputs):
    x = np.asarray(inputs["x"], np.float32)[0]
    pos = np.asarray(inputs["positions"])[0]
    names = [k for k in inputs if k not in ("x", "positions")]
    for l in range(4):
        P = {k: np.asarray(inputs[k][l], np.float32) for k in names}
        OB, OF = run_l1(x, P, pos)
        FO, GO = run_l2(OB, OF)
        x = run_l3(x, P, FO, GO, OB)
    return x[None].astype(np.float32)
```

```python
import numpy as np
import concourse.bass as bass
import concourse.mybir as mybir

F32 = mybir.dt.float32
BF16 = mybir.dt.bfloat16
I32 = mybir.dt.int32
AF = mybir.ActivationFunctionType
ALU = mybir.AluOpType
AX = mybir.AxisListType

SAME_ENGINE_SYNC = True
N_DMA_CH = 8


class Prog:
    def __init__(self, nc):
        self.nc = nc
        self.ops = []
        self.dma_rr = {}

    def op(self, eng, fn, reads=(), writes=(), dma=False):
        self.ops.append(dict(eng=eng, fn=fn, reads=[_k(k) for k in reads],
                             writes=[_k(k) for k in writes], dma=dma))

    def dma(self, fn, reads=(), writes=(), eng="sp"):
        self.op(eng, fn, reads, writes, dma=True)

    def plan(self):
        ops = self.ops
        state = {}
        eng_cnt = {}
        ch_cnt = {}
        ch_last = {}
        rr = {}
        for i, o in enumerate(ops):
            deps = set()
            for (name, sub) in o["reads"]:
                for rec in state.get(name, []):
                    if rec[0] is None or sub is None or rec[0] == sub:
                        if rec[1] is not None:
                            deps.add(rec[1])
            for (name, sub) in o["writes"]:
                for rec in state.get(name, []):
                    if rec[0] is None or sub is None or rec[0] == sub:
                        if rec[1] is not None:
                            deps.add(rec[1])
                        deps.update(rec[2])
            if o["dma"]:
                e = o["eng"]
                ch = rr.get(e, 0)
                rr[e] = (ch + 1) % N_DMA_CH
                key = ("dma", e, ch)
                if key in ch_last:
                    deps.add(ch_last[key])
                ch_last[key] = i
                ch_cnt[key] = ch_cnt.get(key, 0) + 1
                o["sem"] = key
                o["semval"] = 16 * ch_cnt[key]
            else:
                e = o["eng"]
                eng_cnt[e] = eng_cnt.get(e, 0) + 1
                o["sem"] = ("eng", e)
                o["semval"] = eng_cnt[e]
            deps.discard(i)
            o["deps"] = deps
            for (name, sub) in o["reads"]:
                recs = state.setdefault(name, [])
                hit = False
                for rec in recs:
                    if rec[0] == sub:
                        rec[2].add(i)
                        hit = True
                if not hit:
                    recs.append([sub, None, {i}])
                for rec in recs:
                    if rec[0] != sub and (rec[0] is None or sub is None):
                        rec[2].add(i)
            for (name, sub) in o["writes"]:
                recs = state.setdefault(name, [])
                hit = False
                for rec in recs:
                    if rec[0] == sub:
                        rec[1] = i
                        rec[2] = set()
                        hit = True
                    elif rec[0] is None or sub is None:
                        rec[1] = i
                        rec[2] = set()
                if not hit:
                    recs.append([sub, i, set()])
        waited = {}
        for i, o in enumerate(ops):
            need = {}
            for d in o["deps"]:
                od = ops[d]
                if (not od["dma"]) and od["eng"] == o["eng"] and not o["dma"]:
                    if o["eng"] == "pe" or not SAME_ENGINE_SYNC:
                        continue
                    raw = any(_overlap(r, w) for r in o["reads"] for w in od["writes"])
                    if not raw:
                        continue
                need[od["sem"]] = max(need.get(od["sem"], 0), od["semval"])
            w = []
            for s, v in need.items():
                if waited.get((o["eng"], s), 0) < v:
                    waited[(o["eng"], s)] = v
                    w.append((s, v))
            o["waits"] = w
        self.sem_keys = sorted({o["sem"] for o in ops}, key=str)
        return self

    def emit(self, block_engines, sems):
        raise NotImplementedError

    def emit_engine(self, name, eng, sems):
        for o in self.ops:
            if o["eng"] != name:
                continue
            for (s, v) in o["waits"]:
                eng.wait_ge(sems[s], v)
            ins = o["fn"](eng)
            inc = 16 if o["dma"] else 1
            ins.then_inc(sems[o["sem"]], inc)


def _k(k):
    if isinstance(k, tuple):
        return (k[0], k[1])
    return (k, None)


def _overlap(a, b):
    return a[0] == b[0] and (a[1] is None or b[1] is None or a[1] == b[1])


ENG_ATTR = {"pe": "tensor", "act": "scalar", "dve": "vector", "pool": "gpsimd", "sp": "sync"}


def run_prog(nc, prog, tail_waits=True):
    prog.plan()
    import contextlib
    with contextlib.ExitStack() as st:
        sems = {}
        for k in prog.sem_keys:
            sems[k] = st.enter_context(nc.semaphore("s_" + "_".join(str(x) for x in k)))
        block = st.enter_context(nc.Block())
        used = {o["eng"] for o in prog.ops}
        finals = {}
        for o in prog.ops:
            finals[o["sem"]] = o["semval"]

        def mk(name):
            def body(eng):
                prog.emit_engine(name, eng, sems)
                if name == "sp":
                    for k, v in finals.items():
                        eng.wait_ge(sems[k], v)
            return body

        for name in ["sp", "pe", "act", "dve", "pool"]:
            if name in used or name == "sp":
                getattr(block, ENG_ATTR[name])(mk(name))
    return nc

from concourse.bass_utils import run_bass_kernel_spmd
import ml_dtypes

NT = 2048
TT = 512
NTT = 4
HP = 4
EPS = 1e-6
TWO_PI = 6.283185307179586


class Rot:
    def __init__(self, tiles):
        self.tiles = tiles
        self.i = 0

    def next(self):
        t = self.tiles[self.i % len(self.tiles)]
        self.i += 1
        return t


class Ctx:
    def __init__(self):
        self.nc = bass.Bass("TRN2", target_bir_lowering=False)
        self.p = Prog(self.nc)
        self.names = {}

    def dram(self, name, shape, dt, kind):
        return self.nc.dram_tensor(name, shape, dt, kind=kind).ap()

    def sb(self, name, shape, dt):
        t = self.nc.alloc_sbuf_tensor(name, shape, dt)
        return _Tile(t, name)

    def ps(self, name, shape=(128, 512), dt=F32):
        t = self.nc.alloc_psum_tensor(name, list(shape), dt)
        return _Tile(t, name)

    def rot(self, prefix, n, shape, dt, psum=False):
        return Rot([(self.ps if psum else self.sb)(f"{prefix}{i}", list(shape), dt) for i in range(n)])


class _Tile:
    def __init__(self, t, name):
        self.t = t
        self.name = name

    def __getitem__(self, k):
        return self.t[k]


def build_l1():
    c = Ctx()
    nc, p = c.nc, c.p
    xT = c.dram("xT", [1024, HP + NT], F32, "ExternalInput")
    w1 = c.dram("w1", [1024, 4116], F32, "ExternalInput")
    ln1 = c.dram("ln1", [128, 8], F32, "ExternalInput")
    pp = c.dram("pp", [128, 48], F32, "ExternalInput")
    w2 = c.dram("w2", [16, 256], F32, "ExternalInput")
    pos = c.dram("pos", [128, NT], I32, "ExternalInput")
    cst = c.dram("cst", [128, 4], F32, "ExternalInput")
    ob = c.dram("ob", [3584, NT], BF16, "ExternalOutput")
    of = c.dram("of", [772, NT], F32, "ExternalOutput")

    hT = c.sb("hT", [128, 8, HP + NT], BF16)
    xt = c.rot("xt", 2, [128, 8, TT], F32)
    xh = c.sb("xh", [128, 8, HP], F32)
    sq = c.rot("sq", 2, [128, TT], BF16)
    rs = c.rot("rs", 2, [128, TT], F32)
    ones = c.sb("ones", [128, 128], BF16)
    bd = c.sb("bd", [128, 128], BF16)
    lnw = c.sb("lnw", [128, 8], F32)
    ppt = c.sb("ppt", [128, 48], F32)
    npp = c.sb("npp", [128, 48], F32)
    na = c.sb("na", [128, 4], F32)
    cstt = c.sb("cstt", [128, 4], F32)
    epsb = c.sb("epsb", [128, 1], F32)
    ws = c.rot("ws", 3, [128, 8, 512], BF16)
    wsm = c.sb("wsm", [128, 8, 20], BF16)
    w2t = c.sb("w2t", [16, 256], BF16)
    cos2 = c.sb("cos2", [128, NT], F32)
    sin2 = c.sb("sin2", [128, NT], F32)
    posi = c.sb("posi", [128, NT], I32)
    tr_a = c.sb("tr_a", [128, NT], F32)
    tr_b = c.sb("tr_b", [128, NT], F32)
    tr_i = c.sb("tr_i", [128, NT], I32)
    t1 = c.rot("t1_", 2, [128, TT], F32)
    t2 = c.rot("t2_", 2, [128, TT], F32)
    obuf = c.rot("obuf", 4, [128, TT], BF16)
    fbuf = c.rot("fbuf", 3, [128, TT], F32)
    pre = c.rot("pre", 2, [128, TT + 3], F32)
    acc = c.rot("acc", 2, [128, TT], F32)
    sil = c.rot("sil", 2, [128, TT], F32)
    dtr = c.rot("dtr", 2, [128, TT], F32)
    glrT = c.rot("glrT", 2, [16, TT], BF16)
    psA = c.rot("psA", 4, [128, TT], F32, psum=True)
    psB = c.rot("psB", 3, [128, TT], F32, psum=True)
    psH = c.ps("psH", [128, 8])
    onesf = c.sb("onesf", [128, TT], F32)
    crow = c.sb("crow", [4, NT], F32)
    fsc = c.rot("fsc", 2, [128, TT], F32)

    D = p.dma
    D(lambda e: e.dma_start(out=lnw[:], in_=ln1), writes=["lnw"])
    D(lambda e: e.dma_start(out=ppt[:], in_=pp), writes=["ppt"])
    D(lambda e: e.dma_start(out=cstt[:], in_=cst), writes=["cstt"])
    D(lambda e: e.dma_start(out=posi[:], in_=pos), writes=["posi"])
    D(lambda e: e.dma_start(out=w2t[:], in_=w2), writes=["w2t"], eng="pool")
    D(lambda e: e.dma_start(out=wsm[:], in_=w1[:, 4096:4116].rearrange("(kc p) m -> p kc m", p=128)), writes=["wsm"], eng="pool")
    D(lambda e: e.dma_start(out=xh[:], in_=xT[:, 0:HP].rearrange("(kc p) t -> p kc t", p=128)), writes=["xh"])
    p.op("pool", lambda e: e.memset(ones[:], 1.0), writes=["ones"])
    p.op("pool", lambda e: e.memset(onesf[:], 1.0), writes=["onesf"])
    p.op("pool", lambda e: e.memset(bd[:], 0.0), writes=["bd"])
    p.op("pool", lambda e: e.memset(bd[0:64, 0:64], 1.0), reads=["bd"], writes=["bd"])
    p.op("pool", lambda e: e.memset(bd[64:128, 64:128], 1.0), reads=["bd"], writes=["bd"])
    p.op("pool", lambda e: e.memset(epsb[:], EPS), writes=["epsb"])
    p.op("dve", lambda e: e.tensor_scalar(out=npp[:], in0=ppt[:], scalar1=-1.0, scalar2=None, op0=ALU.mult), reads=["ppt"], writes=["npp"])
    p.op("act", lambda e: e.activation(out=na[:], in_=ppt[:, 9:13], func=AF.Exp), reads=["ppt"], writes=["na"])
    p.op("dve", lambda e: e.tensor_scalar(out=na[:], in0=na[:], scalar1=-1.0, scalar2=None, op0=ALU.mult), reads=["na"], writes=["na"])
    p.op("dve", lambda e: e.tensor_copy(out=tr_a[:], in_=posi[:]), reads=["posi"], writes=["tr_a"])
    p.op("dve", lambda e: e.tensor_scalar(out=tr_a[:], in0=tr_a[:], scalar1=cstt[:, 0:1], scalar2=1.0 / TWO_PI, op0=ALU.mult, op1=ALU.mult), reads=["tr_a", "cstt"], writes=["tr_a"])
    for which, dst, col in (("s", sin2, 1), ("c", cos2, 2)):
        if which == "c":
            p.op("dve", lambda e: e.tensor_scalar(out=tr_a[:], in0=tr_a[:], scalar1=0.25, scalar2=None, op0=ALU.add), reads=["tr_a"], writes=["tr_a"])
        p.op("dve", lambda e: e.tensor_copy(out=tr_i[:], in_=tr_a[:]), reads=["tr_a"], writes=["tr_i"])
        p.op("dve", lambda e: e.tensor_copy(out=tr_b[:], in_=tr_i[:]), reads=["tr_i"], writes=["tr_b"])
        p.op("dve", lambda e: e.tensor_tensor(out=tr_b[:], in0=tr_a[:], in1=tr_b[:], op=ALU.subtract), reads=["tr_a", "tr_b"], writes=["tr_b"])
        p.op("dve", lambda e, dst=dst: e.tensor_scalar(out=dst[:], in0=tr_b[:], scalar1=0.5, scalar2=None, op0=ALU.is_gt), reads=["tr_b"], writes=[dst.name])
        p.op("dve", lambda e, dst=dst: e.tensor_tensor(out=tr_b[:], in0=tr_b[:], in1=dst[:], op=ALU.subtract), reads=["tr_b", dst.name], writes=["tr_b"])
        p.op("act", lambda e, dst=dst: e.activation(out=dst[:], in_=tr_b[:], func=AF.Sin, scale=TWO_PI), reads=["tr_b"], writes=[dst.name])
        p.op("dve", lambda e, dst=dst, col=col: e.tensor_scalar(out=dst[:], in0=dst[:], scalar1=cstt[:, col:col + 1], scalar2=None, op0=ALU.mult), reads=[dst.name, "cstt"], writes=[dst.name])

    def rms(xs_, n, dst_cols):
        ps = psA.next()
        for kc in range(8):
            s = sq.next()
            p.op("act", lambda e, s=s, kc=kc: e.activation(out=s[:, 0:n], in_=xs_[:, kc, 0:n], func=AF.Square), reads=[xs_.name], writes=[s.name])
            p.op("pe", lambda e, s=s, kc=kc: e.matmul(ps[:, 0:n], ones[:], s[:, 0:n], start=(kc == 0), stop=(kc == 7)), reads=[s.name, "ones"], writes=[ps.name])
        r = rs.next()
        p.op("act", lambda e: e.activation(out=r[:, 0:n], in_=ps[:, 0:n], func=AF.Ln, scale=1.0 / 1024, bias=epsb[:, 0:1]), reads=[ps.name, "epsb"], writes=[r.name])
        p.op("act", lambda e: e.activation(out=r[:, 0:n], in_=r[:, 0:n], func=AF.Exp, scale=-0.5), reads=[r.name], writes=[r.name])
        for kc in range(8):
            p.op("dve", lambda e, kc=kc: e.scalar_tensor_tensor(out=hT[:, kc, dst_cols[0]:dst_cols[1]], in0=xs_[:, kc, 0:n], scalar=lnw[:, kc:kc + 1], in1=r[:, 0:n], op0=ALU.mult, op1=ALU.mult),
                 reads=[xs_.name, "lnw", r.name], writes=[("hT", dst_cols[0])])

    rms(xh, HP, (0, HP))
    for tt in range(NTT):
        x_ = xt.next()
        D(lambda e, x_=x_, tt=tt: e.dma_start(out=x_[:], in_=xT[:, HP + tt * TT:HP + (tt + 1) * TT].rearrange("(kc p) t -> p kc t", p=128)), writes=[x_.name])
        rms(x_, TT, (HP + tt * TT, HP + (tt + 1) * TT))

    def proj(ps, w, c0, m, col0, n):
        def f(e):
            ins = None
            for kc in range(8):
                ins = e.matmul(ps[0:m, 0:n], w[:, kc, c0:c0 + m], hT[:, kc, col0:col0 + n], start=(kc == 0), stop=(kc == 7))
            return ins
        p.op("pe", f, reads=[w.name, "hT"], writes=[ps.name])

    def load_group(g):
        w = ws.next()
        D(lambda e: e.dma_start(out=w[:], in_=w1[:, 512 * g:512 * (g + 1)].rearrange("(kc p) m -> p kc m", p=128)), writes=[w.name], eng="pool")
        return w

    def out_b(row0, tt, src, m=128):
        D(lambda e: e.dma_start(out=ob[row0:row0 + m, tt * TT:(tt + 1) * TT], in_=src[0:m, :]), reads=[src.name])

    def out_f(row0, tt, src, p0, m):
        D(lambda e: e.dma_start(out=of[row0:row0 + m, tt * TT:(tt + 1) * TT], in_=src[p0:p0 + m, :]), reads=[src.name])

    def qknorm(ps, gcol, extra_bias, row0, tt):
        s = sq.next()
        p.op("act", lambda e: e.activation(out=s[:], in_=ps[:], func=AF.Square), reads=[ps.name], writes=[s.name])
        ps2 = psB.next()
        p.op("pe", lambda e: e.matmul(ps2[:], bd[:], s[:], start=True, stop=True), reads=["bd", s.name], writes=[ps2.name])
        r = rs.next()
        p.op("act", lambda e: e.activation(out=r[:], in_=ps2[:], func=AF.Ln, scale=1.0 / 64, bias=epsb[:, 0:1]), reads=[ps2.name, "epsb"], writes=[r.name])
        p.op("act", lambda e: e.activation(out=r[:], in_=r[:], func=AF.Exp, scale=-0.5, bias=extra_bias), reads=[r.name], writes=[r.name])
        o = obuf.next()
        p.op("dve", lambda e: e.scalar_tensor_tensor(out=o[:], in0=ps[:], scalar=ppt[:, gcol:gcol + 1], in1=r[:], op0=ALU.mult, op1=ALU.mult), reads=[ps.name, "ppt", r.name], writes=[o.name])
        out_b(row0, tt, o)

    def copy_out(ps, scale, row0, tt):
        o = obuf.next()
        p.op("act", lambda e: e.activation(out=o[:], in_=ps[:], func=AF.Copy, scale=scale), reads=[ps.name], writes=[o.name])
        out_b(row0, tt, o)

    def conv_chunk(w, ci, cidx, T0, wd=None, j=None, tt=0, row0=0):
        psM = psA.next()
        proj(psM, w, ci * 128, 128, T0, TT)
        proj(psH, w, ci * 128, 128, T0 - 3, 3)
        pr = pre.next()
        p.op("act", lambda e: e.activation(out=pr[:, 0:3], in_=psH[:, 0:3], func=AF.Copy), reads=["psH"], writes=[(pr.name, 0)])
        p.op("act", lambda e: e.activation(out=pr[:, 3:TT + 3], in_=psM[:], func=AF.Copy), reads=[psM.name], writes=[(pr.name, 1)])
        a = acc.next()
        p.op("dve", lambda e: e.tensor_scalar(out=a[:], in0=pr[:, 0:TT], scalar1=ppt[:, 13 + 4 * cidx:14 + 4 * cidx], scalar2=ppt[:, 37 + cidx:38 + cidx], op0=ALU.mult, op1=ALU.add), reads=[pr.name, "ppt"], writes=[a.name])
        for jj in range(1, 4):
            p.op("dve", lambda e, jj=jj: e.scalar_tensor_tensor(out=a[:], in0=pr[:, jj:jj + TT], scalar=ppt[:, 13 + 4 * cidx + jj:14 + 4 * cidx + jj], in1=a[:], op0=ALU.mult, op1=ALU.add), reads=[pr.name, "ppt", a.name], writes=[a.name])
        s = sil.next()
        p.op("act", lambda e: e.activation(out=s[:], in_=a[:], func=AF.Silu), reads=[a.name], writes=[s.name])
        o = obuf.next()
        p.op("act", lambda e: e.activation(out=o[:], in_=s[:], func=AF.Copy), reads=[s.name], writes=[o.name])
        out_b(row0, tt, o)
        if wd is not None:
            psD = psB.next()
            proj(psD, wd, j * 128, 128, T0, TT)
            d = dtr.next()
            p.op("act", lambda e: e.activation(out=d[:], in_=psD[:], func=AF.Exp, bias=ppt[:, 5 + j:6 + j]), reads=[psD.name, "ppt"], writes=[d.name])
            p.op("act", lambda e: e.activation(out=d[:], in_=d[:], func=AF.Ln, bias=1.0), reads=[d.name], writes=[d.name])
            o2 = obuf.next()
            p.op("dve", lambda e: e.tensor_tensor(out=o2[:], in0=s[:], in1=d[:], op=ALU.mult), reads=[s.name, d.name], writes=[o2.name])
            out_b(2816 + 128 * j, tt, o2)
            f = fbuf.next()
            p.op("dve", lambda e: e.tensor_scalar(out=f[:], in0=d[:], scalar1=na[:, j:j + 1], scalar2=None, op0=ALU.mult), reads=[d.name, "na"], writes=[f.name])
            f2 = fsc.next()
            for q4 in range(4):
                p.op("dve", lambda e, q4=q4: e.tensor_tensor_scan(out=f2[:, 128 * q4:128 * (q4 + 1)], data0=onesf[:, 0:128], data1=f[:, 128 * q4:128 * (q4 + 1)], initial=0.0, op0=ALU.mult, op1=ALU.add), reads=[f.name, "onesf"], writes=[(f2.name, q4)])
            out_f(260 + 128 * j, tt, f2, 0, 128)

    LN8 = float(np.log(0.125))
    for tt in range(NTT):
        T0 = HP + tt * TT
        w = load_group(0)
        for ci in range(4):
            ps = psA.next()
            proj(ps, w, ci * 128, 128, T0, TT)
            if ci < 2:
                qknorm(ps, 0, LN8, 0 + 128 * ci, tt)
            else:
                qknorm(ps, 1, 0.0, 256 + 128 * (ci - 2), tt)
        w = load_group(1)
        for ci in range(4):
            ps = psA.next()
            proj(ps, w, ci * 128, 128, T0, TT)
            copy_out(ps, 1.0 if ci < 2 else 0.125, (512 + 128 * ci) if ci < 2 else (768 + 128 * (ci - 2)), tt)
        w = load_group(2)
        for ci in range(4):
            ps = psA.next()
            proj(ps, w, ci * 128, 128, T0, TT)
            copy_out(ps, 1.0, 1024 + 128 * ci, tt)
        wa = load_group(3)
        wb = load_group(4)
        for ci in range(4):
            pu = psA.next()
            proj(pu, wa, ci * 128, 128, T0, TT)
            pw = psB.next()
            proj(pw, wb, ci * 128, 128, T0, TT)
            a_, b_ = t1.next(), t2.next()
            p.op("dve", lambda e, pu=pu, a_=a_, tt=tt: e.tensor_tensor(out=a_[:], in0=pu[:], in1=cos2[:, tt * TT:(tt + 1) * TT], op=ALU.mult), reads=[pu.name, "cos2"], writes=[a_.name])
            p.op("dve", lambda e, pw=pw, b_=b_, tt=tt: e.tensor_tensor(out=b_[:], in0=pw[:], in1=sin2[:, tt * TT:(tt + 1) * TT], op=ALU.mult), reads=[pw.name, "sin2"], writes=[b_.name])
            o = obuf.next()
            p.op("pool", lambda e, o=o, a_=a_, b_=b_: e.tensor_tensor(out=o[:], in0=a_[:], in1=b_[:], op=ALU.add), reads=[a_.name, b_.name], writes=[o.name])
            out_b(1536 + 128 * ci, tt, o)
        w = load_group(5)
        for ci in range(2):
            ps = psA.next()
            proj(ps, w, ci * 128, 128, T0, TT)
            copy_out(ps, 1.0, 2048 + 128 * ci, tt)
        conv_chunk(w, 2, 4, T0, tt=tt, row0=3328)
        conv_chunk(w, 3, 5, T0, tt=tt, row0=3456)
        wx = load_group(6)
        wd = load_group(7)
        for j in range(4):
            conv_chunk(wx, j, j, T0, wd=wd, j=j, tt=tt, row0=2304 + 128 * j)
        ps = psA.next()
        proj(ps, wsm, 0, 4, T0, TT)
        f = fbuf.next()
        p.op("act", lambda e, ps=ps, f=f: e.activation(out=f[0:4, :], in_=ps[0:4, :], func=AF.Exp, scale=-1.0, bias=npp[0:4, 2:3]), reads=[ps.name, "npp"], writes=[f.name])
        p.op("act", lambda e, f=f: e.activation(out=f[0:4, :], in_=f[0:4, :], func=AF.Ln, bias=1.0), reads=[f.name], writes=[f.name])
        p.op("dve", lambda e, f=f: e.tensor_scalar(out=f[0:4, :], in0=f[0:4, :], scalar1=-1.0, scalar2=None, op0=ALU.mult), reads=[f.name], writes=[f.name])
        if tt == 0:
            p.op("dve", lambda e, f=f: e.tensor_tensor_scan(out=crow[0:4, 0:TT], data0=onesf[0:4, :], data1=f[0:4, :], initial=0.0, op0=ALU.mult, op1=ALU.add), reads=[f.name, "onesf"], writes=[("crow", 0)])
        else:
            p.op("dve", lambda e, f=f, tt=tt: e.tensor_tensor_scan(out=crow[0:4, tt * TT:(tt + 1) * TT], data0=onesf[0:4, :], data1=f[0:4, :], initial=crow[0:4, tt * TT - 1:tt * TT], op0=ALU.mult, op1=ALU.add), reads=[f.name, "onesf", ("crow", tt - 1)], writes=[("crow", tt)])
        D(lambda e, tt=tt: e.dma_start(out=of[0:4, tt * TT:(tt + 1) * TT], in_=crow[0:4, tt * TT:(tt + 1) * TT]), reads=[("crow", tt)])
        ps = psA.next()
        proj(ps, wsm, 4, 16, T0, TT)
        g_ = glrT.next()
        p.op("act", lambda e, ps=ps, g_=g_: e.activation(out=g_[:], in_=ps[0:16, :], func=AF.Copy), reads=[ps.name], writes=[g_.name])
        for c2 in range(2):
            ps2 = psB.next()
            p.op("pe", lambda e, ps2=ps2, g_=g_, c2=c2: e.matmul(ps2[:], w2t[0:16, c2 * 128:(c2 + 1) * 128], g_[0:16, :], start=True, stop=True), reads=["w2t", g_.name], writes=[ps2.name])
            f = fbuf.next()
            p.op("act", lambda e, ps2=ps2, f=f, c2=c2: e.activation(out=f[:], in_=ps2[:], func=AF.Exp, scale=-1.0, bias=npp[:, 3 + c2:4 + c2]), reads=[ps2.name, "npp"], writes=[f.name])
            p.op("act", lambda e, f=f: e.activation(out=f[:], in_=f[:], func=AF.Ln, bias=1.0), reads=[f.name], writes=[f.name])
            p.op("dve", lambda e, f=f: e.tensor_scalar(out=f[:], in0=f[:], scalar1=-1.0 / 16.0, scalar2=None, op0=ALU.mult), reads=[f.name], writes=[f.name])
            f2 = fsc.next()
            for q4 in range(4):
                p.op("dve", lambda e, q4=q4, f=f, f2=f2: e.tensor_tensor_scan(out=f2[:, 128 * q4:128 * (q4 + 1)], data0=onesf[:, 0:128], data1=f[:, 128 * q4:128 * (q4 + 1)], initial=0.0, op0=ALU.mult, op1=ALU.add), reads=[f.name, "onesf"], writes=[(f2.name, q4)])
            out_f(4 + 128 * c2, tt, f2, 0, 128)
    run_prog(nc, p)
    return nc


COLS = dict(fq=0, fk=256, fv=512, ff=768, gq=772, gk=1028, gv=1284, glr=1540, gr=1556, rq=1812, rk=2068,
            rv=2324, rg=2580, z=2836, xs=3348, B=3860, C=3988, dt=4116, gates=4124)


def _swap_halves(idx):
    idx = idx.reshape(-1, 2, 32)
    return idx[:, ::-1, :].reshape(-1)


def l1_weight_cols():
    C = COLS
    r = np.arange
    rq = r(C["rq"], C["rq"] + 256)
    rk = r(C["rk"], C["rk"] + 256)
    dtrep = np.repeat(r(C["dt"], C["dt"] + 8), 64)
    cols = np.concatenate([
        r(C["fq"], C["fq"] + 256), r(C["fk"], C["fk"] + 256),
        r(C["fv"], C["fv"] + 256), r(C["gq"], C["gq"] + 256),
        r(C["gk"], C["gk"] + 256), r(C["gv"], C["gv"] + 256),
        rq, rk, _swap_halves(rq), _swap_halves(rk),
        r(C["rv"], C["rv"] + 256), r(C["B"], C["B"] + 128), r(C["C"], C["C"] + 128),
        r(C["xs"], C["xs"] + 512), dtrep,
        r(C["ff"], C["ff"] + 4), r(C["glr"], C["glr"] + 16)])
    return cols


def l1_params(P):
    pp = np.zeros((128, 48), np.float32)
    pp[:, 0] = np.tile(P["fox_qn"], 2)
    pp[:, 1] = np.tile(P["fox_kn"], 2)
    pp[0:4, 2] = P["fox_bf"]
    pp[:, 3] = P["gla_b"][0:128]
    pp[:, 4] = P["gla_b"][128:256]
    for j in range(4):
        pp[:, 5 + j] = np.repeat(P["ssd_dt_bias"][2 * j:2 * j + 2], 64)
        pp[:, 9 + j] = np.repeat(P["ssd_a_log"][2 * j:2 * j + 2], 64)
    for cidx in range(6):
        ch = slice(128 * cidx, 128 * (cidx + 1))
        for jj in range(4):
            pp[:, 13 + 4 * cidx + jj] = P["ssd_conv_w"][jj, ch]
        pp[:, 37 + cidx] = P["ssd_conv_b"][ch]
    return pp


def l1_consts():
    half = 32
    inv = (10000.0 ** (-(np.arange(half, dtype=np.float32)) / np.float32(half))).astype(np.float32)
    cst = np.zeros((128, 4), np.float32)
    d = np.arange(128) % 64
    cst[:, 0] = inv[d % 32]
    s = np.float32(np.sqrt(0.125))
    cst[:, 1] = np.where(d < 32, -s, s)
    cst[:, 2] = s
    return cst

SEQ = 16384
NCH = 128


def build_l2():
    c = Ctx()
    nc, p = c.nc, c.p
    D = p.dma
    fqT = c.dram("fqT", [64, 8192], BF16, "ExternalInput")
    fkT = c.dram("fkT", [64, SEQ], BF16, "ExternalInput")
    fv = c.dram("fv", [128, NCH, 64], BF16, "ExternalInput")
    cq = c.dram("cq", [8, 1024], F32, "ExternalInput")
    id8d = c.dram("id8", [8, 8], F32, "ExternalInput")
    ck = c.dram("ck", [128, NCH], F32, "ExternalInput")
    Ttm = c.dram("Ttm", [128, 8], F32, "ExternalInput")
    mskd = c.dram("msk", [128, 8, 512], BF16, "ExternalInput")
    idbd = c.dram("idb", [128, 128], BF16, "ExternalInput")
    mnegd = c.dram("mneg", [128, 128], F32, "ExternalInput")
    fo = c.dram("fo", [64, 8192], BF16, "ExternalOutput")
    fden = c.dram("fden", [1, 8192], F32, "ExternalOutput")
    gq = c.dram("gq", [2, 64, SEQ], BF16, "ExternalInput")
    gk = c.dram("gk", [2, 64, SEQ], BF16, "ExternalInput")
    gktm = c.dram("gktm", [2, 128, NCH, 64], BF16, "ExternalInput")
    gvtm = c.dram("gvtm", [2, 128, NCH, 64], BF16, "ExternalInput")
    gb = c.dram("gb", [2, 64, SEQ], F32, "ExternalInput")
    gbtm = c.dram("gbtm", [2, 128, NCH, 64], F32, "ExternalInput")
    gblast = c.dram("gblast", [2, 64, NCH], F32, "ExternalInput")
    gc128 = c.dram("gc128", [2, 128, SEQ], F32, "ExternalInput")
    gctm = c.dram("gctm", [2, 128, NCH], F32, "ExternalInput")
    gclast = c.dram("gclast", [2, 128, NCH], F32, "ExternalInput")
    go = c.dram("go", [2, 64, SEQ], F32, "ExternalOutput")

    GT = 1024
    CPG = 8
    NG = SEQ // GT
    mneg = c.sb("mneg_s", [128, 128], F32)
    psx = c.rot("psx", 2, [128, 512], F32, psum=True)
    D(lambda e: e.dma_start(out=mneg[:], in_=mnegd), writes=["mneg_s"])

    def gen_generic(h2):
        sfx = f"_{h2}"
        q_g = c.sb("q_g" + sfx, [64, GT], BF16)
        k_g = c.sb("k_g" + sfx, [64, GT], BF16)
        ktm_g = c.sb("ktm_g" + sfx, [128, CPG, 64], BF16)
        vtm_r = c.rot("vtm_g" + sfx, 2, [128, CPG, 64], BF16)
        b_g = c.sb("b_g" + sfx, [64, GT], F32)
        btm_g = c.sb("btm_g" + sfx, [128, CPG, 64], F32)
        c_g = c.sb("c_g" + sfx, [128, GT], F32)
        ost = c.rot("ost" + sfx, 2, [64, GT], F32)
        blast = c.sb("blast" + sfx, [64, NCH], F32)
        ctm = c.sb("ctm" + sfx, [128, NCH], F32)
        clast = c.sb("clast" + sfx, [128, NCH], F32)
        cend = c.sb("cend" + sfx, [128, NCH], F32)
        nctm = c.sb("nctm" + sfx, [128, NCH], F32)
        dtot = c.sb("dtot" + sfx, [64, NCH], F32)
        eb = c.sb("eb" + sfx, [64, NCH], F32)
        S = c.sb("S" + sfx, [64, 64], F32)
        Sb = c.sb("Sb" + sfx, [64, 64], BF16)
        E1 = c.sb("E1" + sfx, [64, GT], F32)
        E2 = c.sb("E2" + sfx, [64, GT], F32)
        EC = c.sb("EC" + sfx, [64, GT], F32)
        QD = c.sb("QD" + sfx, [64, GT], BF16)
        KI = c.sb("KI" + sfx, [64, GT], BF16)
        QTt = c.sb("QTg" + sfx, [64, GT], BF16)
        TD = c.sb("TD" + sfx, [128, GT], F32)
        TK = c.sb("TK" + sfx, [128, CPG, 64], F32)
        KIE = c.sb("KIE" + sfx, [128, CPG, 64], BF16)
        AD = c.rot("AD" + sfx, 2, [128, 128], BF16)
        t3 = c.rot("t3" + sfx, 2, [64, 64], F32)
        pso = c.ps("pso" + sfx, [128, 512])
        n = lambda t: t.name

        D(lambda e: e.dma_start(out=blast[:], in_=gblast[h2]), writes=[n(blast)])
        D(lambda e: e.dma_start(out=ctm[:], in_=gctm[h2]), writes=[n(ctm)])
        D(lambda e: e.dma_start(out=clast[:], in_=gclast[h2]), writes=[n(clast)])
        p.op("dve", lambda e: e.tensor_tensor(out=cend[:], in0=clast[:], in1=ctm[:], op=ALU.subtract), reads=[n(clast), n(ctm)], writes=[n(cend)])
        p.op("dve", lambda e: e.tensor_scalar(out=nctm[:], in0=ctm[:], scalar1=-1.0, scalar2=None, op0=ALU.mult), reads=[n(ctm)], writes=[n(nctm)])
        p.op("dve", lambda e: e.tensor_tensor(out=dtot[:], in0=blast[:], in1=clast[0:64, :], op=ALU.add), reads=[n(blast), n(clast)], writes=[n(dtot)])
        p.op("act", lambda e: e.activation(out=dtot[:], in_=dtot[:], func=AF.Exp), reads=[n(dtot)], writes=[n(dtot)])
        p.op("act", lambda e: e.activation(out=eb[:], in_=blast[:], func=AF.Exp), reads=[n(blast)], writes=[n(eb)])
        p.op("pool", lambda e: e.memset(S[:], 0.0), writes=[n(S)])
        p.op("pool", lambda e: e.memset(Sb[:], 0.0), writes=[n(Sb)])

        def load_group(G):
            cs0 = G * GT
            vt = vtm_r.next()
            D(lambda e: e.dma_start(out=q_g[:], in_=gq[h2, :, cs0:cs0 + GT]), writes=[n(q_g)])
            D(lambda e: e.dma_start(out=k_g[:], in_=gk[h2, :, cs0:cs0 + GT]), writes=[n(k_g)])
            D(lambda e: e.dma_start(out=ktm_g[:], in_=gktm[h2, :, CPG * G:CPG * G + CPG, :]), writes=[n(ktm_g)])
            D(lambda e: e.dma_start(out=vt[:], in_=gvtm[h2, :, CPG * G:CPG * G + CPG, :]), writes=[n(vt)])
            D(lambda e: e.dma_start(out=b_g[:], in_=gb[h2, :, cs0:cs0 + GT]), writes=[n(b_g)])
            D(lambda e: e.dma_start(out=btm_g[:], in_=gbtm[h2, :, CPG * G:CPG * G + CPG, :]), writes=[n(btm_g)])
            D(lambda e: e.dma_start(out=c_g[:], in_=gc128[h2, :, cs0:cs0 + GT]), writes=[n(c_g)])
            return vt

        vt_next = load_group(0)
        for G in range(NG):
            vt = vt_next
            cs0 = G * GT
            chs = slice(CPG * G, CPG * G + CPG)
            p.op("act", lambda e: e.activation(out=E1[:], in_=b_g[:], func=AF.Exp), reads=[n(b_g)], writes=[n(E1)])
            p.op("act", lambda e: e.activation(out=E2[:], in_=b_g[:], func=AF.Exp, scale=-1.0), reads=[n(b_g)], writes=[n(E2)])
            p.op("act", lambda e: e.activation(out=EC[:], in_=c_g[0:64, :], func=AF.Exp), reads=[n(c_g)], writes=[n(EC)])
            p.op("dve", lambda e: e.tensor_tensor(out=QD[:], in0=q_g[:], in1=E1[:], op=ALU.mult), reads=[n(q_g), n(E1)], writes=[n(QD)])
            p.op("dve", lambda e: e.tensor_tensor(out=KI[:], in0=k_g[:], in1=E2[:], op=ALU.mult), reads=[n(k_g), n(E2)], writes=[n(KI)])
            p.op("dve", lambda e: e.tensor_tensor(out=E1[:], in0=E1[:], in1=EC[:], op=ALU.mult), reads=[n(E1), n(EC)], writes=[n(E1)])
            p.op("dve", lambda e: e.tensor_tensor(out=QTt[:], in0=q_g[:], in1=E1[:], op=ALU.mult), reads=[n(q_g), n(E1)], writes=[n(QTt)])
            p.op("dve", lambda e, chs=chs: e.tensor_tensor(out=TD[:].rearrange("p (c t) -> p c t", c=CPG), in0=c_g[:].rearrange("p (c t) -> p c t", c=CPG),
                                                        in1=nctm[:, chs].unsqueeze(2).to_broadcast([128, CPG, 128]), op=ALU.add), reads=[n(c_g), n(nctm)], writes=[n(TD)])
            p.op("pool", lambda e: e.tensor_tensor(out=TD[:].rearrange("p (c t) -> p c t", c=CPG), in0=TD[:].rearrange("p (c t) -> p c t", c=CPG),
                                                   in1=mneg[:].unsqueeze(1).to_broadcast([128, CPG, 128]), op=ALU.add), reads=[n(TD), "mneg_s"], writes=[n(TD)])
            p.op("act", lambda e: e.activation(out=TD[:], in_=TD[:], func=AF.Exp), reads=[n(TD)], writes=[n(TD)])
            p.op("dve", lambda e, chs=chs: e.tensor_tensor(out=TK[:], in0=cend[:, chs].unsqueeze(2).to_broadcast([128, CPG, 64]), in1=btm_g[:], op=ALU.subtract), reads=[n(cend), n(btm_g)], writes=[n(TK)])
            p.op("act", lambda e: e.activation(out=TK[:], in_=TK[:], func=AF.Exp), reads=[n(TK)], writes=[n(TK)])
            p.op("dve", lambda e: e.tensor_tensor(out=KIE[:], in0=ktm_g[:], in1=TK[:], op=ALU.mult), reads=[n(ktm_g), n(TK)], writes=[n(KIE)])
            if G + 1 < NG:
                vt_next = load_group(G + 1)
            og = ost.next()
            for j in range(CPG):
                jj = CPG * G + j
                sl = slice(128 * j, 128 * (j + 1))
                oc = slice(128 * (j % 4), 128 * (j % 4 + 1))
                ps = psx.next()
                ad = AD.next()
                t3_ = t3.next()
                p.op("pe", lambda e, ps=ps, sl=sl: e.matmul(ps[:, 0:128], KI[:, sl], QD[:, sl], start=True, stop=True), reads=[n(KI), n(QD)], writes=[(ps.name, 0)])
                p.op("dve", lambda e, ad=ad, ps=ps, sl=sl: e.tensor_tensor(out=ad[:], in0=ps[:, 0:128], in1=TD[:, sl], op=ALU.mult), reads=[(ps.name, 0), n(TD)], writes=[ad.name])

                def mo(e, vt=vt, j=j, ad=ad, sl=sl, oc=oc):
                    e.matmul(pso[0:64, oc], vt[:, j, :], ad[:], start=True, stop=False)
                    return e.matmul(pso[0:64, oc], Sb[:], QTt[:, sl], start=False, stop=True)
                p.op("pe", mo, reads=[vt.name, ad.name, n(Sb), n(QTt)], writes=[(n(pso), j % 4)])
                p.op("pe", lambda e, ps=ps, vt=vt, j=j: e.matmul(ps[0:64, 256:320], KIE[:, j, :], vt[:, j, :], start=True, stop=True), reads=[n(KIE), vt.name], writes=[(ps.name, 2)])
                p.op("dve", lambda e, t3_=t3_, ps=ps, jj=jj: e.tensor_scalar(out=t3_[:], in0=ps[0:64, 256:320], scalar1=eb[:, jj:jj + 1], scalar2=None, op0=ALU.mult), reads=[(ps.name, 2), n(eb)], writes=[t3_.name])
                p.op("dve", lambda e, t3_=t3_, jj=jj: e.scalar_tensor_tensor(out=S[:], in0=S[:], scalar=dtot[:, jj:jj + 1], in1=t3_[:], op0=ALU.mult, op1=ALU.add), reads=[n(S), n(dtot), t3_.name], writes=[n(S)])
                p.op("act", lambda e: e.activation(out=Sb[:], in_=S[:], func=AF.Copy), reads=[n(S)], writes=[n(Sb)])
                if j % 4 == 3:
                    q4 = j // 4
                    p.op("act", lambda e, og=og, q4=q4: e.activation(out=og[:, 512 * q4:512 * (q4 + 1)], in_=pso[0:64, :], func=AF.Copy), reads=[n(pso)], writes=[(og.name, q4)])
                yield
            D(lambda e, og=og, cs0=cs0: e.dma_start(out=go[h2, :, cs0:cs0 + GT], in_=og[:]), reads=[og.name])

    def gen_fox():
        QT = c.sb("QT", [67, 8192], BF16)
        KT = c.sb("KT", [67, SEQ], BF16)
        V = c.sb("V", [128, NCH, 65], BF16)
        msk = c.sb("msk_s", [128, 8, 512], BF16)
        idb = c.sb("idb_s", [128, 128], BF16)
        cqr = c.sb("cqr", [8, 1024], F32)
        r1 = c.sb("r1", [8, 1024], F32)
        hi = c.sb("hi", [8, 1024], BF16)
        mid = c.sb("mid", [8, 1024], BF16)
        lo = c.sb("lo", [8, 1024], BF16)
        id8 = c.sb("id8_s", [8, 8], F32)
        od8 = c.sb("od8", [8, 8], F32)
        offd = c.sb("offd", [8, 1], F32)
        negc = c.sb("negc", [128, NCH], F32)
        Tt = c.sb("Tt", [128, 8], F32)
        offk = c.sb("offk", [128, 8], F32)
        pt = c.rot("pt", 3, [128, 512], BF16)
        oun = c.rot("oun", 2, [64, 512], BF16)
        den = c.rot("den", 2, [65, 512], F32)
        ps_s = c.rot("ps_s", 3, [128, 512], F32, psum=True)
        ps_o = c.ps("ps_o")

        D(lambda e: e.dma_start(out=QT[0:64, :], in_=fqT), writes=[("QT", 0)])
        D(lambda e: e.dma_start(out=KT[0:64, :], in_=fkT), writes=[("KT", 0)])
        D(lambda e: e.dma_start(out=V[:, :, 0:64], in_=fv), writes=[("V", 0)])
        D(lambda e: e.dma_start(out=msk[:], in_=mskd), writes=["msk_s"])
        D(lambda e: e.dma_start(out=idb[:], in_=idbd), writes=["idb_s"])
        D(lambda e: e.dma_start(out=cqr[:], in_=cq), writes=["cqr"])
        D(lambda e: e.dma_start(out=id8[:], in_=id8d), writes=["id8_s"])
        D(lambda e: e.dma_start(out=negc[:], in_=ck), writes=["negc"])
        D(lambda e: e.dma_start(out=Tt[:], in_=Ttm), writes=["Tt"])
        p.op("pool", lambda e: e.memset(V[:, :, 64:65], 1.0), writes=[("V", 1)])
        p.op("pool", lambda e: e.memset(KT[64:67, :], 1.0), writes=[("KT", 1)])
        p.op("pool", lambda e: e.memset(offk[:], 0.0), writes=["offk"])
        for j in range(1, 8):
            p.op("dve", lambda e, j=j: e.tensor_tensor(out=offk[:, j:j + 1], in0=offk[:, j - 1:j], in1=Tt[:, j - 1:j], op=ALU.add), reads=["offk", "Tt"], writes=["offk"])
        for j in range(1, 8):
            p.op("dve", lambda e, j=j: e.tensor_scalar(out=negc[:, 16 * j:16 * j + 16], in0=negc[:, 16 * j:16 * j + 16], scalar1=offk[:, j:j + 1], scalar2=None, op0=ALU.add), reads=["negc", "offk"], writes=["negc"])
        p.op("dve", lambda e: e.tensor_scalar(out=negc[:], in0=negc[:], scalar1=-1.0, scalar2=None, op0=ALU.mult), reads=["negc"], writes=["negc"])
        p.op("dve", lambda e: e.tensor_tensor(out=od8[:], in0=offk[0:8, 0:8], in1=id8[:], op=ALU.mult), reads=["offk", "id8_s"], writes=["od8"])
        p.op("dve", lambda e: e.reduce_sum(out=offd[:], in_=od8[:], axis=AX.X), reads=["od8"], writes=["offd"])
        p.op("dve", lambda e: e.tensor_scalar(out=cqr[:], in0=cqr[:], scalar1=offd[:, 0:1], scalar2=None, op0=ALU.add), reads=["cqr", "offd"], writes=["cqr"])
        p.op("dve", lambda e: e.tensor_copy(out=hi[:], in_=cqr[:]), reads=["cqr"], writes=["hi"])
        p.op("dve", lambda e: e.tensor_tensor(out=r1[:], in0=cqr[:], in1=hi[:], op=ALU.subtract), reads=["cqr", "hi"], writes=["r1"])
        p.op("dve", lambda e: e.tensor_copy(out=mid[:], in_=r1[:]), reads=["r1"], writes=["mid"])
        p.op("dve", lambda e: e.tensor_tensor(out=r1[:], in0=r1[:], in1=mid[:], op=ALU.subtract), reads=["r1", "mid"], writes=["r1"])
        p.op("dve", lambda e: e.tensor_copy(out=lo[:], in_=r1[:]), reads=["r1"], writes=["lo"])
        for j in range(8):
            D(lambda e, j=j: e.dma_start(out=QT[64:65, 1024 * j:1024 * (j + 1)], in_=hi[j:j + 1, :]), reads=["hi"], writes=[("QT", 10 + j)])
            D(lambda e, j=j: e.dma_start(out=QT[65:66, 1024 * j:1024 * (j + 1)], in_=mid[j:j + 1, :]), reads=["mid"], writes=[("QT", 20 + j)])
            D(lambda e, j=j: e.dma_start(out=QT[66:67, 1024 * j:1024 * (j + 1)], in_=lo[j:j + 1, :]), reads=["lo"], writes=[("QT", 30 + j)])
        yield
        for i in range(16):
            nkb = 8 * i + 8
            qs = slice(512 * i, 512 * (i + 1))
            pend = {}

            def mmS(kb, i=i, qs=qs, pend=pend):
                ps = ps_s.next()
                pend[kb] = ps

                def f(e, ps=ps, kb=kb):
                    ins = e.matmul(ps[:], KT[0:67, 128 * kb:128 * (kb + 1)], QT[0:67, qs], start=True, stop=(kb < 8 * i))
                    if kb >= 8 * i:
                        ins = e.matmul(ps[:], idb[:], msk[:, kb - 8 * i, :], start=False, stop=True)
                    return ins
                p.op("pe", f, reads=["KT", "QT", "idb_s", "msk_s"], writes=[ps.name])

            mmS(0)
            for kb in range(nkb):
                if kb + 1 < nkb:
                    mmS(kb + 1)
                ps = pend.pop(kb)
                t_ = pt.next()
                p.op("act", lambda e, t_=t_, ps=ps, kb=kb: e.activation(out=t_[:], in_=ps[:], func=AF.Exp, bias=negc[:, kb:kb + 1]), reads=[ps.name, "negc"], writes=[t_.name])
                p.op("pe", lambda e, t_=t_, kb=kb, nkb=nkb: e.matmul(ps_o[0:65, :], V[:, kb, :], t_[:], start=(kb == 0), stop=(kb == nkb - 1)), reads=["V", t_.name], writes=["ps_o"])
                if kb == nkb - 1:
                    o_, d_ = oun.next(), den.next()
                    p.op("act", lambda e, o_=o_: e.activation(out=o_[:], in_=ps_o[0:64, :], func=AF.Copy), reads=["ps_o"], writes=[o_.name])
                    p.op("act", lambda e, d_=d_: e.activation(out=d_[64:65, :], in_=ps_o[64:65, :], func=AF.Copy), reads=["ps_o"], writes=[d_.name])
                    D(lambda e, o_=o_, qs=qs: e.dma_start(out=fo[:, qs], in_=o_[:]), reads=[o_.name])
                    D(lambda e, d_=d_, qs=qs: e.dma_start(out=fden[0:1, qs], in_=d_[64:65, :]), reads=[d_.name])
                yield

    gf, g0, g1 = gen_fox(), gen_generic(0), gen_generic(1)
    next(gf)
    alive = [True, True, True]
    while any(alive):
        for gi, g in ((0, g0), (1, g1)):
            if alive[gi]:
                try:
                    next(g)
                except StopIteration:
                    alive[gi] = False
        for _ in range(9):
            if alive[2]:
                try:
                    next(gf)
                except StopIteration:
                    alive[2] = False
    run_prog(nc, p)
    return nc


_NC = {}


def _get(name, fn):
    if name not in _NC:
        _NC[name] = fn()
    return _NC[name]


def run_l1(xfull, P, pos):
    nc = _get("l1", build_l1)
    w1 = np.ascontiguousarray(P["w_in"][:, l1_weight_cols()])
    ln1 = np.ascontiguousarray(P["ln1"].reshape(8, 128).T)
    pp = l1_params(P)
    cst = l1_consts()
    maps = []
    for c in range(8):
        t0 = c * NT
        xs = np.zeros((1024, HP + NT), np.float32)
        xs[:, HP:] = xfull[t0:t0 + NT].T
        if c > 0:
            xs[:, 1:HP] = xfull[t0 - 3:t0].T
        maps.append(dict(xT=xs, w1=w1, ln1=ln1, pp=pp, w2=np.ascontiguousarray(P["gla_w2"]),
                         pos=np.ascontiguousarray(np.broadcast_to(pos[t0:t0 + NT], (128, NT))).astype(np.int32), cst=cst))
    res = run_bass_kernel_spmd(nc, maps, core_ids=list(range(8))).results
    OB = np.concatenate([r["ob"] for r in res], axis=1)
    OF = np.concatenate([r["of"] for r in res], axis=1)
    return OB, OF


def _tm(a):
    R = a.shape[0]
    return np.ascontiguousarray(a.T.reshape(NCH, 128, R).transpose(1, 0, 2))


def run_l2(OB, OF):
    nc = _get("l2", build_l2)
    bf = OB.dtype
    idb = np.eye(128, dtype=np.float32).astype(bf)
    s_ = np.arange(128)[:, None]
    mneg = np.where(np.arange(128)[None, :] >= s_, 0.0, -1.0e4).astype(np.float32)
    id8 = np.eye(8, dtype=np.float32)
    lg = np.log(1.0 - 2.0 ** (-5.0 - np.arange(4, dtype=np.float32))).astype(np.float32)
    zeros_b = np.zeros((64, SEQ), np.float32)
    heads = []
    for h in range(4):
        heads.append(dict(q=OB[768 + 64 * h:832 + 64 * h], k=OB[1024 + 64 * h:1088 + 64 * h], v=OB[1280 + 64 * h:1344 + 64 * h],
                          b=OF[4 + 64 * h:68 + 64 * h], c=np.zeros(SEQ, np.float32)))
    for h in range(4):
        cc = np.tile((np.arange(128, dtype=np.float32) + 1.0) * lg[h], NCH).astype(np.float32)
        heads.append(dict(q=OB[1536 + 64 * h:1600 + 64 * h], k=OB[1792 + 64 * h:1856 + 64 * h], v=OB[2048 + 64 * h:2112 + 64 * h], b=zeros_b, c=cc))
    for h in range(8):
        g = h // 4
        heads.append(dict(q=OB[3456 + 64 * g:3520 + 64 * g], k=OB[3328 + 64 * g:3392 + 64 * g], v=OB[2816 + 64 * h:2880 + 64 * h], b=zeros_b, c=OF[260 + 64 * h]))
    maps = []
    for c in range(8):
        hf, par = c // 2, c % 2
        m = {}
        m["fqT"] = np.ascontiguousarray(OB[64 * hf:64 * hf + 64].reshape(64, 32, 512)[:, par::2].reshape(64, 8192))
        m["fkT"] = np.ascontiguousarray(OB[256 + 64 * hf:320 + 64 * hf])
        m["fv"] = _tm(OB[512 + 64 * hf:576 + 64 * hf])
        crow = OF[hf]
        m["cq"] = np.ascontiguousarray(crow.reshape(32, 512)[par::2].reshape(8, 1024))
        m["ck"] = np.ascontiguousarray(crow.reshape(NCH, 128).T)
        T = crow[NT - 1::NT]
        m["Ttm"] = np.ascontiguousarray(np.broadcast_to(T, (128, 8))).astype(np.float32)
        q_ = np.arange(512)[None, None, :]
        jb = np.arange(8)[None, :, None]
        ss = np.arange(128)[:, None, None]
        m["msk"] = np.where(128 * jb + ss <= 512 * par + q_, 0.0, -30000.0).astype(np.float32).astype(bf)
        m["idb"] = idb
        m["mneg"] = mneg
        m["id8"] = id8
        hs = [heads[2 * c], heads[2 * c + 1]]
        m["gq"] = np.stack([h["q"] for h in hs])
        m["gk"] = np.stack([h["k"] for h in hs])
        m["gktm"] = np.stack([_tm(h["k"]) for h in hs])
        m["gvtm"] = np.stack([_tm(h["v"]) for h in hs])
        m["gb"] = np.stack([h["b"] for h in hs]).astype(np.float32)
        m["gbtm"] = np.stack([_tm(h["b"]) for h in hs]).astype(np.float32)
        m["gblast"] = np.stack([np.ascontiguousarray(h["b"][:, 127::128]) for h in hs]).astype(np.float32)
        m["gc128"] = np.stack([np.ascontiguousarray(np.broadcast_to(h["c"], (128, SEQ))) for h in hs]).astype(np.float32)
        m["gctm"] = np.stack([np.ascontiguousarray(h["c"].reshape(NCH, 128).T) for h in hs]).astype(np.float32)
        m["gclast"] = np.stack([np.ascontiguousarray(np.broadcast_to(h["c"][127::128], (128, NCH))) for h in hs]).astype(np.float32)
        maps.append(m)
    res = run_bass_kernel_spmd(nc, maps, core_ids=list(range(8))).results
    FO = np.zeros((256, SEQ), dtype=bf)
    FD = np.zeros((4, SEQ), np.float32)
    GO = np.zeros((16, 64, SEQ), np.float32)
    for c in range(8):
        hf, par = c // 2, c % 2
        FO[64 * hf:64 * hf + 64].reshape(64, 32, 512)[:, par::2] = res[c]["fo"].reshape(64, 16, 512)
        FD[hf].reshape(32, 512)[par::2] = res[c]["fden"].reshape(16, 512)
        GO[2 * c:2 * c + 2] = res[c]["go"]
    return FO, GO, FD


def build_l3():
    c = Ctx()
    nc, p = c.nc, c.p
    D = p.dma
    xT = c.dram("xT", [1024, NT], F32, "ExternalInput")
    w3 = c.dram("w3", [1024, 5120], F32, "ExternalInput")
    wup = c.dram("wup", [1280, 1024], F32, "ExternalInput")
    wout = c.dram("wout", [1024, 1024], F32, "ExternalInput")
    wfi = c.dram("wfi", [1024, 5632], F32, "ExternalInput")
    wfo = c.dram("wfo", [2816, 1024], F32, "ExternalInput")
    ln1 = c.dram("ln1", [128, 8], F32, "ExternalInput")
    ln2 = c.dram("ln2", [128, 8], F32, "ExternalInput")
    pp = c.dram("pp", [128, 16], F32, "ExternalInput")
    ya = c.dram("ya", [256, NT], BF16, "ExternalInput")
    yden = c.dram("yden", [256, NT], F32, "ExternalInput")
    goT = c.dram("goT", [1024, NT], F32, "ExternalInput")
    xsT = c.dram("xsT", [512, NT], BF16, "ExternalInput")
    xo = c.dram("xo", [1024, NT], F32, "ExternalOutput")

    xt = c.rot("xt", 2, [128, 8, TT], F32)
    hT = c.sb("hT", [128, 8, TT], BF16)
    Y = c.sb("Y", [128, 10, TT], BF16)
    mg = c.sb("mg", [128, 8, TT], BF16)
    wupt = c.sb("wupt", [128, 10, 1024], BF16)
    woutt = c.sb("woutt", [128, 8, 1024], BF16)
    ws = c.rot("ws", 3, [128, 8, 512], BF16)
    wfs = c.rot("wfs", 2, [128, 4, 1024], BF16)
    act = c.rot("actb", 2, [128, 4, TT], BF16)
    sq = c.rot("sq", 3, [128, TT], BF16)
    rs = c.rot("rs", 2, [128, TT], F32)
    og = c.rot("og", 3, [128, TT], F32)
    yun = c.sb("yun", [128, 2, TT], BF16)
    ydn = c.sb("ydn", [128, 2, TT], F32)
    xsb = c.rot("xsb", 2, [128, TT], BF16)
    sg = c.rot("sg", 3, [128, TT], F32)
    ta = c.rot("ta", 3, [128, TT], F32)
    u2 = c.rot("u2_", 4, [128, TT], F32)
    mt = c.rot("mt", 2, [128, TT], F32)
    ones = c.sb("ones", [128, 128], BF16)
    bd = c.sb("bd", [128, 128], BF16)
    lnw1 = c.sb("lnw1", [128, 8], F32)
    lnw2 = c.sb("lnw2", [128, 8], F32)
    ppt = c.sb("ppt", [128, 16], F32)
    epsb = c.sb("epsb", [128, 1], F32)
    psA = c.rot("psA", 4, [128, TT], F32, psum=True)
    psB = c.rot("psB", 3, [128, TT], F32, psum=True)

    D(lambda e: e.dma_start(out=lnw1[:], in_=ln1), writes=["lnw1"])
    D(lambda e: e.dma_start(out=lnw2[:], in_=ln2), writes=["lnw2"])
    D(lambda e: e.dma_start(out=ppt[:], in_=pp), writes=["ppt"])
    D(lambda e: e.dma_start(out=wupt[:], in_=wup.rearrange("(kc p) m -> p kc m", p=128)), writes=["wupt"], eng="pool")
    D(lambda e: e.dma_start(out=woutt[:], in_=wout.rearrange("(kc p) m -> p kc m", p=128)), writes=["woutt"], eng="pool")
    p.op("pool", lambda e: e.memset(ones[:], 1.0), writes=["ones"])
    p.op("pool", lambda e: e.memset(bd[:], 0.0), writes=["bd"])
    p.op("pool", lambda e: e.memset(bd[0:64, 0:64], 1.0), reads=["bd"], writes=["bd"])
    p.op("pool", lambda e: e.memset(bd[64:128, 64:128], 1.0), reads=["bd"], writes=["bd"])
    p.op("pool", lambda e: e.memset(epsb[:], EPS), writes=["epsb"])

    def rms(x_, lnw):
        ps = psA.next()
        for kc in range(8):
            s = sq.next()
            p.op("act", lambda e, s=s, kc=kc: e.activation(out=s[:], in_=x_[:, kc, :], func=AF.Square), reads=[x_.name], writes=[s.name])
            p.op("pe", lambda e, s=s, kc=kc: e.matmul(ps[:], ones[:], s[:], start=(kc == 0), stop=(kc == 7)), reads=[s.name, "ones"], writes=[ps.name])
        r = rs.next()
        p.op("act", lambda e: e.activation(out=r[:], in_=ps[:], func=AF.Ln, scale=1.0 / 1024, bias=epsb[:, 0:1]), reads=[ps.name, "epsb"], writes=[r.name])
        p.op("act", lambda e: e.activation(out=r[:], in_=r[:], func=AF.Exp, scale=-0.5), reads=[r.name], writes=[r.name])
        for kc in range(8):
            p.op("dve", lambda e, kc=kc: e.scalar_tensor_tensor(out=hT[:, kc, :], in0=x_[:, kc, :], scalar=lnw[:, kc:kc + 1], in1=r[:], op0=ALU.mult, op1=ALU.mult),
                 reads=[x_.name, lnw.name, r.name], writes=[("hT", kc)])

    def proj(ps, w, c0, m=128):
        def f(e):
            ins = None
            for kc in range(8):
                ins = e.matmul(ps[0:m, :], w[:, kc, c0:c0 + m], hT[:, kc, :], start=(kc == 0), stop=(kc == 7))
            return ins
        p.op("pe", f, reads=[w.name, "hT"], writes=[ps.name])

    def load_w(src, c0, ncols=512):
        w = ws.next()
        D(lambda e: e.dma_start(out=w[:, :, 0:ncols], in_=src[:, c0:c0 + ncols].rearrange("(kc p) m -> p kc m", p=128)), writes=[w.name], eng="pool")
        return w

    for tt in range(NTT):
        ts_ = slice(tt * TT, (tt + 1) * TT)
        x_ = xt.next()
        D(lambda e, x_=x_, ts_=ts_: e.dma_start(out=x_[:], in_=xT[:, ts_].rearrange("(kc p) t -> p kc t", p=128)), writes=[x_.name])
        rms(x_, lnw1)
        D(lambda e, ts_=ts_: e.dma_start(out=yun[:], in_=ya[:, ts_].rearrange("(c p) t -> p c t", p=128)), writes=["yun"])
        D(lambda e, ts_=ts_: e.dma_start(out=ydn[:], in_=yden[:, ts_].rearrange("(c p) t -> p c t", p=128)), writes=["ydn"])
        p.op("dve", lambda e: e.reciprocal(out=ydn[:], in_=ydn[:]), reads=["ydn"], writes=["ydn"])
        p.op("dve", lambda e: e.tensor_tensor(out=Y[:, 0:2, :], in0=yun[:], in1=ydn[:], op=ALU.mult), reads=["yun", "ydn"], writes=[("Y", 0), ("Y", 1)])
        w = load_w(w3, 0)
        for i4 in range(4):
            o_ = og.next()
            D(lambda e, o_=o_, i4=i4, ts_=ts_: e.dma_start(out=o_[:], in_=goT[128 * i4:128 * (i4 + 1), ts_]), writes=[o_.name])
            s = sq.next()
            p.op("act", lambda e, s=s, o_=o_: e.activation(out=s[:], in_=o_[:], func=AF.Square), reads=[o_.name], writes=[s.name])
            ps2 = psB.next()
            p.op("pe", lambda e, ps2=ps2, s=s: e.matmul(ps2[:], bd[:], s[:], start=True, stop=True), reads=["bd", s.name], writes=[ps2.name])
            r = rs.next()
            p.op("act", lambda e, r=r, ps2=ps2: e.activation(out=r[:], in_=ps2[:], func=AF.Ln, scale=1.0 / 64, bias=epsb[:, 0:1]), reads=[ps2.name, "epsb"], writes=[r.name])
            p.op("act", lambda e, r=r: e.activation(out=r[:], in_=r[:], func=AF.Exp, scale=-0.5), reads=[r.name], writes=[r.name])
            ps = psA.next()
            proj(ps, w, 128 * i4)
            g_ = sg.next()
            p.op("act", lambda e, g_=g_, ps=ps: e.activation(out=g_[:], in_=ps[:], func=AF.Silu), reads=[ps.name], writes=[g_.name])
            t_ = ta.next()
            col = 0 if i4 < 2 else 1
            p.op("dve", lambda e, t_=t_, o_=o_, r=r, col=col: e.scalar_tensor_tensor(out=t_[:], in0=o_[:], scalar=ppt[:, col:col + 1], in1=r[:], op0=ALU.mult, op1=ALU.mult), reads=[o_.name, "ppt", r.name], writes=[t_.name])
            p.op("pool", lambda e, t_=t_, g_=g_, i4=i4: e.tensor_tensor(out=Y[:, 2 + i4, :], in0=t_[:], in1=g_[:], op=ALU.mult), reads=[t_.name, g_.name], writes=[("Y", 2 + i4)])
        w = load_w(w3, 512)
        us = []
        for j in range(4):
            o_ = og.next()
            D(lambda e, o_=o_, j=j, ts_=ts_: e.dma_start(out=o_[:], in_=goT[512 + 128 * j:640 + 128 * j, ts_]), writes=[o_.name])
            xb = xsb.next()
            D(lambda e, xb=xb, j=j, ts_=ts_: e.dma_start(out=xb[:], in_=xsT[128 * j:128 * (j + 1), ts_]), writes=[xb.name])
            t_ = ta.next()
            p.op("dve", lambda e, t_=t_, xb=xb, o_=o_, j=j: e.scalar_tensor_tensor(out=t_[:], in0=xb[:], scalar=ppt[:, 2 + j:3 + j], in1=o_[:], op0=ALU.mult, op1=ALU.add), reads=[xb.name, "ppt", o_.name], writes=[t_.name])
            ps = psA.next()
            proj(ps, w, 128 * j)
            g_ = sg.next()
            p.op("act", lambda e, g_=g_, ps=ps: e.activation(out=g_[:], in_=ps[:], func=AF.Silu), reads=[ps.name], writes=[g_.name])
            u_ = u2.next()
            p.op("pool", lambda e, u_=u_, t_=t_, g_=g_: e.tensor_tensor(out=u_[:], in0=t_[:], in1=g_[:], op=ALU.mult), reads=[t_.name, g_.name], writes=[u_.name])
            us.append(u_)
        for gI in range(2):
            ps2 = psB.next()
            for k2 in range(2):
                s = sq.next()
                u_ = us[2 * gI + k2]
                p.op("act", lambda e, s=s, u_=u_: e.activation(out=s[:], in_=u_[:], func=AF.Square), reads=[u_.name], writes=[s.name])
                p.op("pe", lambda e, ps2=ps2, s=s, k2=k2: e.matmul(ps2[:], ones[:], s[:], start=(k2 == 0), stop=(k2 == 1)), reads=["ones", s.name], writes=[ps2.name])
            r = rs.next()
            p.op("act", lambda e, r=r, ps2=ps2: e.activation(out=r[:], in_=ps2[:], func=AF.Ln, scale=1.0 / 256, bias=epsb[:, 0:1]), reads=[ps2.name, "epsb"], writes=[r.name])
            p.op("act", lambda e, r=r: e.activation(out=r[:], in_=r[:], func=AF.Exp, scale=-0.5), reads=[r.name], writes=[r.name])
            for k2 in range(2):
                j = 2 * gI + k2
                u_ = us[j]
                p.op("dve", lambda e, u_=u_, r=r, j=j: e.scalar_tensor_tensor(out=Y[:, 6 + j, :], in0=u_[:], scalar=ppt[:, 6 + j:7 + j], in1=r[:], op0=ALU.mult, op1=ALU.mult), reads=[u_.name, "ppt", r.name], writes=[("Y", 6 + j)])
        kcs = [(0, 2), (2, 2), (4, 2), (6, 4)]
        for n in range(8):
            w = load_w(w3, 1024 + 512 * n)
            m_ = mt.next()
            for b in range(4):
                psg_ = psA.next()
                proj(psg_, w, 128 * b)
                g_ = sg.next()
                p.op("act", lambda e, g_=g_, psg_=psg_: e.activation(out=g_[:], in_=psg_[:], func=AF.Sigmoid), reads=[psg_.name], writes=[g_.name])
                psu = psB.next()
                k0, nk = kcs[b]

                def fu(e, psu=psu, k0=k0, nk=nk, n=n):
                    ins = None
                    for q in range(nk):
                        ins = e.matmul(psu[:], wupt[:, k0 + q, 128 * n:128 * (n + 1)], Y[:, k0 + q, :], start=(q == 0), stop=(q == nk - 1))
                    return ins
                p.op("pe", fu, reads=["wupt", "Y"], writes=[psu.name])
                if b == 0:
                    p.op("dve", lambda e, m_=m_, psu=psu, g_=g_: e.tensor_tensor(out=m_[:], in0=psu[:], in1=g_[:], op=ALU.mult), reads=[psu.name, g_.name], writes=[m_.name])
                else:
                    t_ = ta.next()
                    p.op("dve", lambda e, t_=t_, psu=psu, g_=g_: e.tensor_tensor(out=t_[:], in0=psu[:], in1=g_[:], op=ALU.mult), reads=[psu.name, g_.name], writes=[t_.name])
                    if b < 3:
                        p.op("pool", lambda e, m_=m_, t_=t_: e.tensor_tensor(out=m_[:], in0=m_[:], in1=t_[:], op=ALU.add), reads=[m_.name, t_.name], writes=[m_.name])
                    else:
                        p.op("pool", lambda e, m_=m_, t_=t_, n=n: e.tensor_tensor(out=mg[:, n, :], in0=m_[:], in1=t_[:], op=ALU.add), reads=[m_.name, t_.name], writes=[("mg", n)])
        for n in range(8):
            ps = psA.next()

            def fo_(e, ps=ps, n=n):
                ins = None
                for kc in range(8):
                    ins = e.matmul(ps[:], woutt[:, kc, 128 * n:128 * (n + 1)], mg[:, kc, :], start=(kc == 0), stop=(kc == 7))
                return ins
            p.op("pe", fo_, reads=["woutt", "mg"], writes=[ps.name])
            p.op("dve", lambda e, ps=ps, n=n, x_=x_: e.tensor_tensor(out=x_[:, n, :], in0=x_[:, n, :], in1=ps[:], op=ALU.add), reads=[x_.name, ps.name], writes=[x_.name])
        rms(x_, lnw2)
        for hg in range(6):
            nh = 4 if hg < 5 else 2
            wg = load_w(wfi, 512 * hg, 128 * nh)
            wu = load_w(wfi, 2816 + 512 * hg, 128 * nh)
            wf = wfs.next()
            D(lambda e, wf=wf, hg=hg, nh=nh: e.dma_start(out=wf[:, 0:nh, :], in_=wfo[512 * hg:512 * hg + 128 * nh, :].rearrange("(kc p) m -> p kc m", p=128)), writes=[wf.name], eng="pool")
            a_ = act.next()
            for hc in range(nh):
                pg, pu = psA.next(), psB.next()
                proj(pg, wg, 128 * hc)
                proj(pu, wu, 128 * hc)
                g_ = sg.next()
                p.op("act", lambda e, g_=g_, pg=pg: e.activation(out=g_[:], in_=pg[:], func=AF.Silu), reads=[pg.name], writes=[g_.name])
                p.op("dve", lambda e, a_=a_, hc=hc, g_=g_, pu=pu: e.tensor_tensor(out=a_[:, hc, :], in0=pu[:], in1=g_[:], op=ALU.mult), reads=[pu.name, g_.name], writes=[(a_.name, hc)])
            for n in range(8):
                ps = psA.next()

                def ff_(e, ps=ps, n=n, wf=wf, a_=a_, nh=nh):
                    ins = None
                    for hc in range(nh):
                        ins = e.matmul(ps[:], wf[:, hc, 128 * n:128 * (n + 1)], a_[:, hc, :], start=(hc == 0), stop=(hc == nh - 1))
                    return ins
                p.op("pe", ff_, reads=[wf.name, a_.name], writes=[ps.name])
                p.op("dve", lambda e, ps=ps, n=n, x_=x_: e.tensor_tensor(out=x_[:, n, :], in0=x_[:, n, :], in1=ps[:], op=ALU.add), reads=[x_.name, ps.name], writes=[x_.name])
        D(lambda e, x_=x_, ts_=ts_: e.dma_start(out=xo[:, ts_].rearrange("(kc p) t -> p kc t", p=128), in_=x_[:]), reads=[x_.name])
    run_prog(nc, p)
    return nc


def l3_weight_cols():
    C = COLS
    r = np.arange
    gates = []
    for n in range(8):
        for b in range(4):
            gates.append(r(C["gates"] + 1024 * b + 128 * n, C["gates"] + 1024 * b + 128 * (n + 1)))
    return np.concatenate([r(C["gr"], C["gr"] + 256), r(C["rg"], C["rg"] + 256), r(C["z"], C["z"] + 512)] + gates)


def run_l3(xfull, P, FO, GO, OB, FD):
    nc = _get("l3", build_l3)
    w3 = np.ascontiguousarray(P["w_in"][:, l3_weight_cols()])
    wup = np.ascontiguousarray(np.concatenate([P["w_up_a"], P["w_up_b"], P["w_up_c"], P["w_up_d"]], axis=0))
    pp = np.zeros((128, 16), np.float32)
    pp[:, 0] = np.tile(P["gla_norm"], 2)
    pp[:, 1] = np.tile(P["ret_norm"], 2)
    for j in range(4):
        pp[:, 2 + j] = np.repeat(P["ssd_d"][2 * j:2 * j + 2], 64)
        pp[:, 6 + j] = P["ssd_norm"][128 * j:128 * (j + 1)]
    ln1 = np.ascontiguousarray(P["ln1"].reshape(8, 128).T)
    ln2 = np.ascontiguousarray(P["ln2"].reshape(8, 128).T)
    GOf = GO.reshape(1024, SEQ)
    FDr = np.repeat(FD, 64, axis=0)
    maps = []
    for c in range(8):
        sl = slice(c * NT, (c + 1) * NT)
        maps.append(dict(xT=np.ascontiguousarray(xfull[sl].T), w3=w3, wup=wup, wout=np.ascontiguousarray(P["w_out"]),
                         wfi=np.ascontiguousarray(P["w_ffn_in"]), wfo=np.ascontiguousarray(P["w_ffn_out"]), ln1=ln1, ln2=ln2, pp=pp,
                         ya=np.ascontiguousarray(FO[:, sl]), yden=np.ascontiguousarray(FDr[:, sl]), goT=np.ascontiguousarray(GOf[:, sl]), xsT=np.ascontiguousarray(OB[2304:2816, sl])))
    res = run_bass_kernel_spmd(nc, maps, core_ids=list(range(8))).results
    return np.concatenate([r["xo"].T for r in res], axis=0)


def kernel(**inputs):
    x = np.asarray(inputs["x"], np.float32)[0]
    pos = np.asarray(inputs["positions"])[0]
    names = [k for k in inputs if k not in ("x", "positions")]
    for l in range(4):
        P = {k: np.asarray(inputs[k][l], np.float32) for k in names}
        OB, OF = run_l1(x, P, pos)
        FO, GO, FD = run_l2(OB, OF)
        x = run_l3(x, P, FO, GO, OB, FD)
    return x[None].astype(np.float32)
```

```python
import numpy as np
import concourse.bass as bass
import concourse.mybir as mybir

F32 = mybir.dt.float32
BF16 = mybir.dt.bfloat16
I32 = mybir.dt.int32
AF = mybir.ActivationFunctionType
ALU = mybir.AluOpType
AX = mybir.AxisListType

SAME_ENGINE_SYNC = True
N_DMA_CH = 8


class Prog:
    def __init__(self, nc):
        self.nc = nc
        self.ops = []
        self.dma_rr = {}

    def op(self, eng, fn, reads=(), writes=(), dma=False):
        self.ops.append(dict(eng=eng, fn=fn, reads=[_k(k) for k in reads],
                             writes=[_k(k) for k in writes], dma=dma))

    def dma(self, fn, reads=(), writes=(), eng="sp"):
        self.op(eng, fn, reads, writes, dma=True)

    def plan(self):
        ops = self.ops
        state = {}
        eng_cnt = {}
        ch_cnt = {}
        ch_last = {}
        rr = {}
        for i, o in enumerate(ops):
            deps = set()
            for (name, sub) in o["reads"]:
                for rec in state.get(name, []):
                    if rec[0] is None or sub is None or rec[0] == sub:
                        if rec[1] is not None:
                            deps.add(rec[1])
            for (name, sub) in o["writes"]:
                for rec in state.get(name, []):
                    if rec[0] is None or sub is None or rec[0] == sub:
                        if rec[1] is not None:
                            deps.add(rec[1])
                        deps.update(rec[2])
            if o["dma"]:
                e = o["eng"]
                ch = rr.get(e, 0)
                rr[e] = (ch + 1) % N_DMA_CH
                key = ("dma", e, ch)
                if key in ch_last:
                    deps.add(ch_last[key])
                ch_last[key] = i
                ch_cnt[key] = ch_cnt.get(key, 0) + 1
                o["sem"] = key
                o["semval"] = 16 * ch_cnt[key]
            else:
                e = o["eng"]
                eng_cnt[e] = eng_cnt.get(e, 0) + 1
                o["sem"] = ("eng", e)
                o["semval"] = eng_cnt[e]
            deps.discard(i)
            o["deps"] = deps
            for (name, sub) in o["reads"]:
                recs = state.setdefault(name, [])
                hit = False
                for rec in recs:
                    if rec[0] == sub:
                        rec[2].add(i)
                        hit = True
                if not hit:
                    recs.append([sub, None, {i}])
                for rec in recs:
                    if rec[0] != sub and (rec[0] is None or sub is None):
                        rec[2].add(i)
            for (name, sub) in o["writes"]:
                recs = state.setdefault(name, [])
                hit = False
                for rec in recs:
                    if rec[0] == sub:
                        rec[1] = i
                        rec[2] = set()
                        hit = True
                    elif rec[0] is None or sub is None:
                        rec[1] = i
                        rec[2] = set()
                if not hit:
                    recs.append([sub, i, set()])
        waited = {}
        for i, o in enumerate(ops):
            need = {}
            for d in o["deps"]:
                od = ops[d]
                if (not od["dma"]) and od["eng"] == o["eng"] and not o["dma"]:
                    if o["eng"] == "pe" or not SAME_ENGINE_SYNC:
                        continue
                    raw = any(_overlap(r, w) for r in o["reads"] for w in od["writes"])
                    if not raw:
                        continue
                need[od["sem"]] = max(need.get(od["sem"], 0), od["semval"])
            w = []
            for s, v in need.items():
                if waited.get((o["eng"], s), 0) < v:
                    waited[(o["eng"], s)] = v
                    w.append((s, v))
            o["waits"] = w
        self.sem_keys = sorted({o["sem"] for o in ops}, key=str)
        return self

    def emit(self, block_engines, sems):
        raise NotImplementedError

    def emit_engine(self, name, eng, sems):
        for o in self.ops:
            if o["eng"] != name:
                continue
            for (s, v) in o["waits"]:
                eng.wait_ge(sems[s], v)
            ins = o["fn"](eng)
            inc = 16 if o["dma"] else 1
            ins.then_inc(sems[o["sem"]], inc)


def _k(k):
    if isinstance(k, tuple):
        return (k[0], k[1])
    return (k, None)


def _overlap(a, b):
    return a[0] == b[0] and (a[1] is None or b[1] is None or a[1] == b[1])


ENG_ATTR = {"pe": "tensor", "act": "scalar", "dve": "vector", "pool": "gpsimd", "sp": "sync"}


def run_prog(nc, prog, tail_waits=True):
    prog.plan()
    import contextlib
    with contextlib.ExitStack() as st:
        sems = {}
        for k in prog.sem_keys:
            sems[k] = st.enter_context(nc.semaphore("s_" + "_".join(str(x) for x in k)))
        block = st.enter_context(nc.Block())
        used = {o["eng"] for o in prog.ops}
        finals = {}
        for o in prog.ops:
            finals[o["sem"]] = o["semval"]

        def mk(name):
            def body(eng):
                prog.emit_engine(name, eng, sems)
                if name == "sp":
                    for k, v in finals.items():
                        eng.wait_ge(sems[k], v)
            return body

        for name in ["sp", "pe", "act", "dve", "pool"]:
            if name in used or name == "sp":
                getattr(block, ENG_ATTR[name])(mk(name))
    return nc

from concourse.bass_utils import run_bass_kernel_spmd
import ml_dtypes

NT = 2048
TT = 512
NTT = 4
HP = 4
EPS = 1e-6
TWO_PI = 6.283185307179586


class Rot:
    def __init__(self, tiles):
        self.tiles = tiles
        self.i = 0

    def next(self):
        t = self.tiles[self.i % len(self.tiles)]
        self.i += 1
        return t


class Ctx:
    def __init__(self):
        self.nc = bass.Bass("TRN2", target_bir_lowering=False)
        self.p = Prog(self.nc)
        self.names = {}

    def dram(self, name, shape, dt, kind):
        return self.nc.dram_tensor(name, shape, dt, kind=kind).ap()

    def sb(self, name, shape, dt):
        t = self.nc.alloc_sbuf_tensor(name, shape, dt)
        return _Tile(t, name)

    def ps(self, name, shape=(128, 512), dt=F32):
        t = self.nc.alloc_psum_tensor(name, list(shape), dt)
        return _Tile(t, name)

    def rot(self, prefix, n, shape, dt, psum=False):
        return Rot([(self.ps if psum else self.sb)(f"{prefix}{i}", list(shape), dt) for i in range(n)])


class _Tile:
    def __init__(self, t, name):
        self.t = t
        self.name = name

    def __getitem__(self, k):
        return self.t[k]


def build_l1():
    c = Ctx()
    nc, p = c.nc, c.p
    xT = c.dram("xT", [1024, HP + NT], F32, "ExternalInput")
    w1 = c.dram("w1", [1024, 4116], F32, "ExternalInput")
    ln1 = c.dram("ln1", [128, 8], F32, "ExternalInput")
    pp = c.dram("pp", [128, 48], F32, "ExternalInput")
    w2 = c.dram("w2", [16, 256], F32, "ExternalInput")
    pos = c.dram("pos", [128, NT], I32, "ExternalInput")
    cst = c.dram("cst", [128, 4], F32, "ExternalInput")
    ob = c.dram("ob", [3584, NT], BF16, "ExternalOutput")
    of = c.dram("of", [772, NT], F32, "ExternalOutput")

    hT = c.sb("hT", [128, 8, HP + NT], BF16)
    xt = c.rot("xt", 2, [128, 8, TT], F32)
    xh = c.sb("xh", [128, 8, HP], F32)
    sq = c.rot("sq", 2, [128, TT], BF16)
    rs = c.rot("rs", 2, [128, TT], F32)
    ones = c.sb("ones", [128, 128], BF16)
    bd = c.sb("bd", [128, 128], BF16)
    lnw = c.sb("lnw", [128, 8], F32)
    ppt = c.sb("ppt", [128, 48], F32)
    npp = c.sb("npp", [128, 48], F32)
    na = c.sb("na", [128, 4], F32)
    cstt = c.sb("cstt", [128, 4], F32)
    epsb = c.sb("epsb", [128, 1], F32)
    ws = c.rot("ws", 3, [128, 8, 512], BF16)
    wsm = c.sb("wsm", [128, 8, 20], BF16)
    w2t = c.sb("w2t", [16, 256], BF16)
    cos2 = c.sb("cos2", [128, NT], F32)
    sin2 = c.sb("sin2", [128, NT], F32)
    posi = c.sb("posi", [128, NT], I32)
    tr_a = c.sb("tr_a", [128, NT], F32)
    tr_b = c.sb("tr_b", [128, NT], F32)
    tr_i = c.sb("tr_i", [128, NT], I32)
    t1 = c.rot("t1_", 2, [128, TT], F32)
    t2 = c.rot("t2_", 2, [128, TT], F32)
    obuf = c.rot("obuf", 4, [128, TT], BF16)
    fbuf = c.rot("fbuf", 3, [128, TT], F32)
    pre = c.rot("pre", 2, [128, TT + 3], F32)
    acc = c.rot("acc", 2, [128, TT], F32)
    sil = c.rot("sil", 2, [128, TT], F32)
    dtr = c.rot("dtr", 2, [128, TT], F32)
    glrT = c.rot("glrT", 2, [16, TT], BF16)
    psA = c.rot("psA", 4, [128, TT], F32, psum=True)
    psB = c.rot("psB", 3, [128, TT], F32, psum=True)
    psH = c.ps("psH", [128, 8])
    onesf = c.sb("onesf", [128, TT], F32)
    crow = c.sb("crow", [4, NT], F32)
    fsc = c.rot("fsc", 2, [128, TT], F32)

    D = p.dma
    D(lambda e: e.dma_start(out=lnw[:], in_=ln1), writes=["lnw"])
    D(lambda e: e.dma_start(out=ppt[:], in_=pp), writes=["ppt"])
    D(lambda e: e.dma_start(out=cstt[:], in_=cst), writes=["cstt"])
    D(lambda e: e.dma_start(out=posi[:], in_=pos), writes=["posi"])
    D(lambda e: e.dma_start(out=w2t[:], in_=w2), writes=["w2t"], eng="pool")
    D(lambda e: e.dma_start(out=wsm[:], in_=w1[:, 4096:4116].rearrange("(kc p) m -> p kc m", p=128)), writes=["wsm"], eng="pool")
    D(lambda e: e.dma_start(out=xh[:], in_=xT[:, 0:HP].rearrange("(kc p) t -> p kc t", p=128)), writes=["xh"])
    p.op("pool", lambda e: e.memset(ones[:], 1.0), writes=["ones"])
    p.op("pool", lambda e: e.memset(onesf[:], 1.0), writes=["onesf"])
    p.op("pool", lambda e: e.memset(bd[:], 0.0), writes=["bd"])
    p.op("pool", lambda e: e.memset(bd[0:64, 0:64], 1.0), reads=["bd"], writes=["bd"])
    p.op("pool", lambda e: e.memset(bd[64:128, 64:128], 1.0), reads=["bd"], writes=["bd"])
    p.op("pool", lambda e: e.memset(epsb[:], EPS), writes=["epsb"])
    p.op("dve", lambda e: e.tensor_scalar(out=npp[:], in0=ppt[:], scalar1=-1.0, scalar2=None, op0=ALU.mult), reads=["ppt"], writes=["npp"])
    p.op("act", lambda e: e.activation(out=na[:], in_=ppt[:, 9:13], func=AF.Exp), reads=["ppt"], writes=["na"])
    p.op("dve", lambda e: e.tensor_scalar(out=na[:], in0=na[:], scalar1=-1.0, scalar2=None, op0=ALU.mult), reads=["na"], writes=["na"])
    p.op("dve", lambda e: e.tensor_copy(out=tr_a[:], in_=posi[:]), reads=["posi"], writes=["tr_a"])
    p.op("dve", lambda e: e.tensor_scalar(out=tr_a[:], in0=tr_a[:], scalar1=cstt[:, 0:1], scalar2=1.0 / TWO_PI, op0=ALU.mult, op1=ALU.mult), reads=["tr_a", "cstt"], writes=["tr_a"])
    for which, dst, col in (("s", sin2, 1), ("c", cos2, 2)):
        if which == "c":
            p.op("dve", lambda e: e.tensor_scalar(out=tr_a[:], in0=tr_a[:], scalar1=0.25, scalar2=None, op0=ALU.add), reads=["tr_a"], writes=["tr_a"])
        p.op("dve", lambda e: e.tensor_copy(out=tr_i[:], in_=tr_a[:]), reads=["tr_a"], writes=["tr_i"])
        p.op("dve", lambda e: e.tensor_copy(out=tr_b[:], in_=tr_i[:]), reads=["tr_i"], writes=["tr_b"])
        p.op("dve", lambda e: e.tensor_tensor(out=tr_b[:], in0=tr_a[:], in1=tr_b[:], op=ALU.subtract), reads=["tr_a", "tr_b"], writes=["tr_b"])
        p.op("dve", lambda e, dst=dst: e.tensor_scalar(out=dst[:], in0=tr_b[:], scalar1=0.5, scalar2=None, op0=ALU.is_gt), reads=["tr_b"], writes=[dst.name])
        p.op("dve", lambda e, dst=dst: e.tensor_tensor(out=tr_b[:], in0=tr_b[:], in1=dst[:], op=ALU.subtract), reads=["tr_b", dst.name], writes=["tr_b"])
        p.op("act", lambda e, dst=dst: e.activation(out=dst[:], in_=tr_b[:], func=AF.Sin, scale=TWO_PI), reads=["tr_b"], writes=[dst.name])
        p.op("dve", lambda e, dst=dst, col=col: e.tensor_scalar(out=dst[:], in0=dst[:], scalar1=cstt[:, col:col + 1], scalar2=None, op0=ALU.mult), reads=[dst.name, "cstt"], writes=[dst.name])

    def rms(xs_, n, dst_cols):
        ps = psA.next()
        for kc in range(8):
            s = sq.next()
            p.op("act", lambda e, s=s, kc=kc: e.activation(out=s[:, 0:n], in_=xs_[:, kc, 0:n], func=AF.Square), reads=[xs_.name], writes=[s.name])
            p.op("pe", lambda e, s=s, kc=kc: e.matmul(ps[:, 0:n], ones[:], s[:, 0:n], start=(kc == 0), stop=(kc == 7)), reads=[s.name, "ones"], writes=[ps.name])
        r = rs.next()
        p.op("act", lambda e: e.activation(out=r[:, 0:n], in_=ps[:, 0:n], func=AF.Ln, scale=1.0 / 1024, bias=epsb[:, 0:1]), reads=[ps.name, "epsb"], writes=[r.name])
        p.op("act", lambda e: e.activation(out=r[:, 0:n], in_=r[:, 0:n], func=AF.Exp, scale=-0.5), reads=[r.name], writes=[r.name])
        for kc in range(8):
            p.op("dve", lambda e, kc=kc: e.scalar_tensor_tensor(out=hT[:, kc, dst_cols[0]:dst_cols[1]], in0=xs_[:, kc, 0:n], scalar=lnw[:, kc:kc + 1], in1=r[:, 0:n], op0=ALU.mult, op1=ALU.mult),
                 reads=[xs_.name, "lnw", r.name], writes=[("hT", dst_cols[0])])

    rms(xh, HP, (0, HP))
    for tt in range(NTT):
        x_ = xt.next()
        D(lambda e, x_=x_, tt=tt: e.dma_start(out=x_[:], in_=xT[:, HP + tt * TT:HP + (tt + 1) * TT].rearrange("(kc p) t -> p kc t", p=128)), writes=[x_.name])
        rms(x_, TT, (HP + tt * TT, HP + (tt + 1) * TT))

    def proj(ps, w, c0, m, col0, n):
        def f(e):
            ins = None
            for kc in range(8):
                ins = e.matmul(ps[0:m, 0:n], w[:, kc, c0:c0 + m], hT[:, kc, col0:col0 + n], start=(kc == 0), stop=(kc == 7))
            return ins
        p.op("pe", f, reads=[w.name, "hT"], writes=[ps.name])

    def load_group(g):
        w = ws.next()
        D(lambda e: e.dma_start(out=w[:], in_=w1[:, 512 * g:512 * (g + 1)].rearrange("(kc p) m -> p kc m", p=128)), writes=[w.name], eng="pool")
        return w

    def out_b(row0, tt, src, m=128):
        D(lambda e: e.dma_start(out=ob[row0:row0 + m, tt * TT:(tt + 1) * TT], in_=src[0:m, :]), reads=[src.name])

    def out_f(row0, tt, src, p0, m):
        D(lambda e: e.dma_start(out=of[row0:row0 + m, tt * TT:(tt + 1) * TT], in_=src[p0:p0 + m, :]), reads=[src.name])

    def qknorm(ps, gcol, extra_bias, row0, tt):
        s = sq.next()
        p.op("act", lambda e: e.activation(out=s[:], in_=ps[:], func=AF.Square), reads=[ps.name], writes=[s.name])
        ps2 = psB.next()
        p.op("pe", lambda e: e.matmul(ps2[:], bd[:], s[:], start=True, stop=True), reads=["bd", s.name], writes=[ps2.name])
        r = rs.next()
        p.op("act", lambda e: e.activation(out=r[:], in_=ps2[:], func=AF.Ln, scale=1.0 / 64, bias=epsb[:, 0:1]), reads=[ps2.name, "epsb"], writes=[r.name])
        p.op("act", lambda e: e.activation(out=r[:], in_=r[:], func=AF.Exp, scale=-0.5, bias=extra_bias), reads=[r.name], writes=[r.name])
        o = obuf.next()
        p.op("dve", lambda e: e.scalar_tensor_tensor(out=o[:], in0=ps[:], scalar=ppt[:, gcol:gcol + 1], in1=r[:], op0=ALU.mult, op1=ALU.mult), reads=[ps.name, "ppt", r.name], writes=[o.name])
        out_b(row0, tt, o)

    def copy_out(ps, scale, row0, tt):
        o = obuf.next()
        p.op("act", lambda e: e.activation(out=o[:], in_=ps[:], func=AF.Copy, scale=scale), reads=[ps.name], writes=[o.name])
        out_b(row0, tt, o)

    def conv_chunk(w, ci, cidx, T0, wd=None, j=None, tt=0, row0=0):
        psM = psA.next()
        proj(psM, w, ci * 128, 128, T0, TT)
        proj(psH, w, ci * 128, 128, T0 - 3, 3)
        pr = pre.next()
        p.op("act", lambda e: e.activation(out=pr[:, 0:3], in_=psH[:, 0:3], func=AF.Copy), reads=["psH"], writes=[(pr.name, 0)])
        p.op("act", lambda e: e.activation(out=pr[:, 3:TT + 3], in_=psM[:], func=AF.Copy), reads=[psM.name], writes=[(pr.name, 1)])
        a = acc.next()
        p.op("dve", lambda e: e.tensor_scalar(out=a[:], in0=pr[:, 0:TT], scalar1=ppt[:, 13 + 4 * cidx:14 + 4 * cidx], scalar2=ppt[:, 37 + cidx:38 + cidx], op0=ALU.mult, op1=ALU.add), reads=[pr.name, "ppt"], writes=[a.name])
        for jj in range(1, 4):
            p.op("dve", lambda e, jj=jj: e.scalar_tensor_tensor(out=a[:], in0=pr[:, jj:jj + TT], scalar=ppt[:, 13 + 4 * cidx + jj:14 + 4 * cidx + jj], in1=a[:], op0=ALU.mult, op1=ALU.add), reads=[pr.name, "ppt", a.name], writes=[a.name])
        s = sil.next()
        p.op("act", lambda e: e.activation(out=s[:], in_=a[:], func=AF.Silu), reads=[a.name], writes=[s.name])
        o = obuf.next()
        p.op("act", lambda e: e.activation(out=o[:], in_=s[:], func=AF.Copy), reads=[s.name], writes=[o.name])
        out_b(row0, tt, o)
        if wd is not None:
            psD = psB.next()
            proj(psD, wd, j * 128, 128, T0, TT)
            d = dtr.next()
            p.op("act", lambda e: e.activation(out=d[:], in_=psD[:], func=AF.Exp, bias=ppt[:, 5 + j:6 + j]), reads=[psD.name, "ppt"], writes=[d.name])
            p.op("act", lambda e: e.activation(out=d[:], in_=d[:], func=AF.Ln, bias=1.0), reads=[d.name], writes=[d.name])
            o2 = obuf.next()
            p.op("dve", lambda e: e.tensor_tensor(out=o2[:], in0=s[:], in1=d[:], op=ALU.mult), reads=[s.name, d.name], writes=[o2.name])
            out_b(2816 + 128 * j, tt, o2)
            f = fbuf.next()
            p.op("dve", lambda e: e.tensor_scalar(out=f[:], in0=d[:], scalar1=na[:, j:j + 1], scalar2=None, op0=ALU.mult), reads=[d.name, "na"], writes=[f.name])
            f2 = fsc.next()
            for q4 in range(4):
                p.op("dve", lambda e, q4=q4: e.tensor_tensor_scan(out=f2[:, 128 * q4:128 * (q4 + 1)], data0=onesf[:, 0:128], data1=f[:, 128 * q4:128 * (q4 + 1)], initial=0.0, op0=ALU.mult, op1=ALU.add), reads=[f.name, "onesf"], writes=[(f2.name, q4)])
            out_f(260 + 128 * j, tt, f2, 0, 128)

    LN8 = float(np.log(0.125))
    T0s = [HP + tt * TT for tt in range(NTT)]
    w = load_group(0)
    for tt in range(NTT):
        for ci in range(4):
            ps = psA.next()
            proj(ps, w, ci * 128, 128, T0s[tt], TT)
            if ci < 2:
                qknorm(ps, 0, LN8, 0 + 128 * ci, tt)
            else:
                qknorm(ps, 1, 0.0, 256 + 128 * (ci - 2), tt)
    w = load_group(1)
    for tt in range(NTT):
        for ci in range(4):
            ps = psA.next()
            proj(ps, w, ci * 128, 128, T0s[tt], TT)
            copy_out(ps, 1.0 if ci < 2 else 0.125, (512 + 128 * ci) if ci < 2 else (768 + 128 * (ci - 2)), tt)
    w = load_group(2)
    for tt in range(NTT):
        for ci in range(4):
            ps = psA.next()
            proj(ps, w, ci * 128, 128, T0s[tt], TT)
            copy_out(ps, 1.0, 1024 + 128 * ci, tt)
    wa = load_group(3)
    wb = load_group(4)
    for tt in range(NTT):
        for ci in range(4):
            pu = psA.next()
            proj(pu, wa, ci * 128, 128, T0s[tt], TT)
            pw = psB.next()
            proj(pw, wb, ci * 128, 128, T0s[tt], TT)
            a_, b_ = t1.next(), t2.next()
            p.op("dve", lambda e, pu=pu, a_=a_, tt=tt: e.tensor_tensor(out=a_[:], in0=pu[:], in1=cos2[:, tt * TT:(tt + 1) * TT], op=ALU.mult), reads=[pu.name, "cos2"], writes=[a_.name])
            p.op("dve", lambda e, pw=pw, b_=b_, tt=tt: e.tensor_tensor(out=b_[:], in0=pw[:], in1=sin2[:, tt * TT:(tt + 1) * TT], op=ALU.mult), reads=[pw.name, "sin2"], writes=[b_.name])
            o = obuf.next()
            p.op("pool", lambda e, o=o, a_=a_, b_=b_: e.tensor_tensor(out=o[:], in0=a_[:], in1=b_[:], op=ALU.add), reads=[a_.name, b_.name], writes=[o.name])
            out_b(1536 + 128 * ci, tt, o)
    w = load_group(5)
    for tt in range(NTT):
        for ci in range(2):
            ps = psA.next()
            proj(ps, w, ci * 128, 128, T0s[tt], TT)
            copy_out(ps, 1.0, 2048 + 128 * ci, tt)
        conv_chunk(w, 2, 4, T0s[tt], tt=tt, row0=3328)
        conv_chunk(w, 3, 5, T0s[tt], tt=tt, row0=3456)
    wx = load_group(6)
    wd = load_group(7)
    for tt in range(NTT):
        for j in range(4):
            conv_chunk(wx, j, j, T0s[tt], wd=wd, j=j, tt=tt, row0=2304 + 128 * j)
    for tt in range(NTT):
        T0 = T0s[tt]
        ps = psA.next()
        proj(ps, wsm, 0, 4, T0, TT)
        f = fbuf.next()
        p.op("act", lambda e, ps=ps, f=f: e.activation(out=f[0:4, :], in_=ps[0:4, :], func=AF.Exp, scale=-1.0, bias=npp[0:4, 2:3]), reads=[ps.name, "npp"], writes=[f.name])
        p.op("act", lambda e, f=f: e.activation(out=f[0:4, :], in_=f[0:4, :], func=AF.Ln, bias=1.0), reads=[f.name], writes=[f.name])
        p.op("dve", lambda e, f=f: e.tensor_scalar(out=f[0:4, :], in0=f[0:4, :], scalar1=-1.0, scalar2=None, op0=ALU.mult), reads=[f.name], writes=[f.name])
        if tt == 0:
            p.op("dve", lambda e, f=f: e.tensor_tensor_scan(out=crow[0:4, 0:TT], data0=onesf[0:4, :], data1=f[0:4, :], initial=0.0, op0=ALU.mult, op1=ALU.add), reads=[f.name, "onesf"], writes=[("crow", 0)])
        else:
            p.op("dve", lambda e, f=f, tt=tt: e.tensor_tensor_scan(out=crow[0:4, tt * TT:(tt + 1) * TT], data0=onesf[0:4, :], data1=f[0:4, :], initial=crow[0:4, tt * TT - 1:tt * TT], op0=ALU.mult, op1=ALU.add), reads=[f.name, "onesf", ("crow", tt - 1)], writes=[("crow", tt)])
        D(lambda e, tt=tt: e.dma_start(out=of[0:4, tt * TT:(tt + 1) * TT], in_=crow[0:4, tt * TT:(tt + 1) * TT]), reads=[("crow", tt)])
        ps = psA.next()
        proj(ps, wsm, 4, 16, T0, TT)
        g_ = glrT.next()
        p.op("act", lambda e, ps=ps, g_=g_: e.activation(out=g_[:], in_=ps[0:16, :], func=AF.Copy), reads=[ps.name], writes=[g_.name])
        for c2 in range(2):
            ps2 = psB.next()
            p.op("pe", lambda e, ps2=ps2, g_=g_, c2=c2: e.matmul(ps2[:], w2t[0:16, c2 * 128:(c2 + 1) * 128], g_[0:16, :], start=True, stop=True), reads=["w2t", g_.name], writes=[ps2.name])
            f = fbuf.next()
            p.op("act", lambda e, ps2=ps2, f=f, c2=c2: e.activation(out=f[:], in_=ps2[:], func=AF.Exp, scale=-1.0, bias=npp[:, 3 + c2:4 + c2]), reads=[ps2.name, "npp"], writes=[f.name])
            p.op("act", lambda e, f=f: e.activation(out=f[:], in_=f[:], func=AF.Ln, bias=1.0), reads=[f.name], writes=[f.name])
            p.op("dve", lambda e, f=f: e.tensor_scalar(out=f[:], in0=f[:], scalar1=-1.0 / 16.0, scalar2=None, op0=ALU.mult), reads=[f.name], writes=[f.name])
            f2 = fsc.next()
            for q4 in range(4):
                p.op("dve", lambda e, q4=q4, f=f, f2=f2: e.tensor_tensor_scan(out=f2[:, 128 * q4:128 * (q4 + 1)], data0=onesf[:, 0:128], data1=f[:, 128 * q4:128 * (q4 + 1)], initial=0.0, op0=ALU.mult, op1=ALU.add), reads=[f.name, "onesf"], writes=[(f2.name, q4)])
            out_f(4 + 128 * c2, tt, f2, 0, 128)
    run_prog(nc, p)
    return nc


COLS = dict(fq=0, fk=256, fv=512, ff=768, gq=772, gk=1028, gv=1284, glr=1540, gr=1556, rq=1812, rk=2068,
            rv=2324, rg=2580, z=2836, xs=3348, B=3860, C=3988, dt=4116, gates=4124)


def _swap_halves(idx):
    idx = idx.reshape(-1, 2, 32)
    return idx[:, ::-1, :].reshape(-1)


def l1_weight_cols():
    C = COLS
    r = np.arange
    rq = r(C["rq"], C["rq"] + 256)
    rk = r(C["rk"], C["rk"] + 256)
    dtrep = np.repeat(r(C["dt"], C["dt"] + 8), 64)
    cols = np.concatenate([
        r(C["fq"], C["fq"] + 256), r(C["fk"], C["fk"] + 256),
        r(C["fv"], C["fv"] + 256), r(C["gq"], C["gq"] + 256),
        r(C["gk"], C["gk"] + 256), r(C["gv"], C["gv"] + 256),
        rq, rk, _swap_halves(rq), _swap_halves(rk),
        r(C["rv"], C["rv"] + 256), r(C["B"], C["B"] + 128), r(C["C"], C["C"] + 128),
        r(C["xs"], C["xs"] + 512), dtrep,
        r(C["ff"], C["ff"] + 4), r(C["glr"], C["glr"] + 16)])
    return cols


def l1_params(P):
    pp = np.zeros((128, 48), np.float32)
    pp[:, 0] = np.tile(P["fox_qn"], 2)
    pp[:, 1] = np.tile(P["fox_kn"], 2)
    pp[0:4, 2] = P["fox_bf"]
    pp[:, 3] = P["gla_b"][0:128]
    pp[:, 4] = P["gla_b"][128:256]
    for j in range(4):
        pp[:, 5 + j] = np.repeat(P["ssd_dt_bias"][2 * j:2 * j + 2], 64)
        pp[:, 9 + j] = np.repeat(P["ssd_a_log"][2 * j:2 * j + 2], 64)
    for cidx in range(6):
        ch = slice(128 * cidx, 128 * (cidx + 1))
        for jj in range(4):
            pp[:, 13 + 4 * cidx + jj] = P["ssd_conv_w"][jj, ch]
        pp[:, 37 + cidx] = P["ssd_conv_b"][ch]
    return pp


def l1_consts():
    half = 32
    inv = (10000.0 ** (-(np.arange(half, dtype=np.float32)) / np.float32(half))).astype(np.float32)
    cst = np.zeros((128, 4), np.float32)
    d = np.arange(128) % 64
    cst[:, 0] = inv[d % 32]
    s = np.float32(np.sqrt(0.125))
    cst[:, 1] = np.where(d < 32, -s, s)
    cst[:, 2] = s
    return cst

SEQ = 16384
NCH = 128


def build_l2():
    c = Ctx()
    nc, p = c.nc, c.p
    D = p.dma
    fqT = c.dram("fqT", [64, 8192], BF16, "ExternalInput")
    fkT = c.dram("fkT", [64, SEQ], BF16, "ExternalInput")
    fv = c.dram("fv", [128, NCH, 64], BF16, "ExternalInput")
    cq = c.dram("cq", [8, 1024], F32, "ExternalInput")
    id8d = c.dram("id8", [8, 8], F32, "ExternalInput")
    ck = c.dram("ck", [128, NCH], F32, "ExternalInput")
    Ttm = c.dram("Ttm", [128, 8], F32, "ExternalInput")
    mskd = c.dram("msk", [128, 8, 512], BF16, "ExternalInput")
    idbd = c.dram("idb", [128, 128], BF16, "ExternalInput")
    mnegd = c.dram("mneg", [128, 128], F32, "ExternalInput")
    fo = c.dram("fo", [64, 8192], BF16, "ExternalOutput")
    fden = c.dram("fden", [1, 8192], F32, "ExternalOutput")
    gq = c.dram("gq", [2, 64, SEQ], BF16, "ExternalInput")
    gk = c.dram("gk", [2, 64, SEQ], BF16, "ExternalInput")
    gktm = c.dram("gktm", [2, 128, NCH, 64], BF16, "ExternalInput")
    gvtm = c.dram("gvtm", [2, 128, NCH, 64], BF16, "ExternalInput")
    gb = c.dram("gb", [2, 64, SEQ], F32, "ExternalInput")
    gbtm = c.dram("gbtm", [2, 128, NCH, 64], F32, "ExternalInput")
    gblast = c.dram("gblast", [2, 64, NCH], F32, "ExternalInput")
    gc128 = c.dram("gc128", [2, 128, SEQ], F32, "ExternalInput")
    gctm = c.dram("gctm", [2, 128, NCH], F32, "ExternalInput")
    gclast = c.dram("gclast", [2, 128, NCH], F32, "ExternalInput")
    go = c.dram("go", [2, 64, SEQ], F32, "ExternalOutput")

    GT = 1024
    CPG = 8
    NG = SEQ // GT
    mneg = c.sb("mneg_s", [128, 128], F32)
    psx = c.rot("psx", 2, [128, 512], F32, psum=True)
    D(lambda e: e.dma_start(out=mneg[:], in_=mnegd), writes=["mneg_s"])

    shared_tmp = (c.sb("E1", [64, GT], F32), c.sb("E2", [64, GT], F32), c.sb("EC", [64, GT], F32), c.sb("TK", [128, CPG, 64], F32))

    def gen_generic(h2):
        sfx = f"_{h2}"
        q_g = c.sb("q_g" + sfx, [64, GT], BF16)
        k_g = c.sb("k_g" + sfx, [64, GT], BF16)
        ktm_g = c.sb("ktm_g" + sfx, [128, CPG, 64], BF16)
        vtm_r = c.rot("vtm_g" + sfx, 3, [128, CPG, 64], BF16)
        b_g = c.sb("b_g" + sfx, [64, GT], F32)
        btm_g = c.sb("btm_g" + sfx, [128, CPG, 64], F32)
        c_g = c.sb("c_g" + sfx, [128, GT], F32)
        ost = c.rot("ost" + sfx, 1, [64, GT], F32)
        blast = c.sb("blast" + sfx, [64, NCH], F32)
        ctm = c.sb("ctm" + sfx, [128, NCH], F32)
        clast = c.sb("clast" + sfx, [128, NCH], F32)
        cend = c.sb("cend" + sfx, [128, NCH], F32)
        nctm = c.sb("nctm" + sfx, [128, NCH], F32)
        dtot = c.sb("dtot" + sfx, [64, NCH], F32)
        eb = c.sb("eb" + sfx, [64, NCH], F32)
        S = c.sb("S" + sfx, [64, 64], F32)
        Sb = c.sb("Sb" + sfx, [64, 64], BF16)
        E1, E2, EC, TK = shared_tmp
        QDs = [c.sb("QD%d" % b + sfx, [64, GT], BF16) for b in range(2)]
        KIs = [c.sb("KI%d" % b + sfx, [64, GT], BF16) for b in range(2)]
        QTs = [c.sb("QTg%d" % b + sfx, [64, GT], BF16) for b in range(2)]
        TDs = [c.sb("TD%d" % b + sfx, [128, GT], F32) for b in range(2)]
        KIEs = [c.sb("KIE%d" % b + sfx, [128, CPG, 64], BF16) for b in range(2)]
        AD = c.rot("AD" + sfx, 2, [128, 128], BF16)
        t3 = c.rot("t3" + sfx, 2, [64, 64], F32)
        pso = c.ps("pso" + sfx, [128, 512])
        n = lambda t: t.name

        D(lambda e: e.dma_start(out=blast[:], in_=gblast[h2]), writes=[n(blast)])
        D(lambda e: e.dma_start(out=ctm[:], in_=gctm[h2]), writes=[n(ctm)])
        D(lambda e: e.dma_start(out=clast[:], in_=gclast[h2]), writes=[n(clast)])
        p.op("dve", lambda e: e.tensor_tensor(out=cend[:], in0=clast[:], in1=ctm[:], op=ALU.subtract), reads=[n(clast), n(ctm)], writes=[n(cend)])
        p.op("dve", lambda e: e.tensor_scalar(out=nctm[:], in0=ctm[:], scalar1=-1.0, scalar2=None, op0=ALU.mult), reads=[n(ctm)], writes=[n(nctm)])
        p.op("dve", lambda e: e.tensor_tensor(out=dtot[:], in0=blast[:], in1=clast[0:64, :], op=ALU.add), reads=[n(blast), n(clast)], writes=[n(dtot)])
        p.op("act", lambda e: e.activation(out=dtot[:], in_=dtot[:], func=AF.Exp), reads=[n(dtot)], writes=[n(dtot)])
        p.op("act", lambda e: e.activation(out=eb[:], in_=blast[:], func=AF.Exp), reads=[n(blast)], writes=[n(eb)])
        p.op("pool", lambda e: e.memset(S[:], 0.0), writes=[n(S)])
        p.op("pool", lambda e: e.memset(Sb[:], 0.0), writes=[n(Sb)])
        vts = {}

        def load_group(G):
            cs0 = G * GT
            vt = vtm_r.next()
            vts[G] = vt
            D(lambda e: e.dma_start(out=q_g[:], in_=gq[h2, :, cs0:cs0 + GT]), writes=[n(q_g)])
            D(lambda e: e.dma_start(out=k_g[:], in_=gk[h2, :, cs0:cs0 + GT]), writes=[n(k_g)])
            D(lambda e: e.dma_start(out=ktm_g[:], in_=gktm[h2, :, CPG * G:CPG * G + CPG, :]), writes=[n(ktm_g)])
            D(lambda e: e.dma_start(out=vt[:], in_=gvtm[h2, :, CPG * G:CPG * G + CPG, :]), writes=[n(vt)])
            D(lambda e: e.dma_start(out=b_g[:], in_=gb[h2, :, cs0:cs0 + GT]), writes=[n(b_g)])
            D(lambda e: e.dma_start(out=btm_g[:], in_=gbtm[h2, :, CPG * G:CPG * G + CPG, :]), writes=[n(btm_g)])
            D(lambda e: e.dma_start(out=c_g[:], in_=gc128[h2, :, cs0:cs0 + GT]), writes=[n(c_g)])

        def prologue(G):
            b = G % 2
            QD, KI, QTt, TD, KIE = QDs[b], KIs[b], QTs[b], TDs[b], KIEs[b]
            chs = slice(CPG * G, CPG * G + CPG)
            v3 = lambda t: t[:].rearrange("p (c t) -> p c t", c=CPG)
            return [
                lambda: p.op("act", lambda e: e.activation(out=E1[:], in_=b_g[:], func=AF.Exp), reads=[n(b_g)], writes=[n(E1)]),
                lambda: p.op("act", lambda e: e.activation(out=E2[:], in_=b_g[:], func=AF.Exp, scale=-1.0), reads=[n(b_g)], writes=[n(E2)]),
                lambda: p.op("act", lambda e: e.activation(out=EC[:], in_=c_g[0:64, :], func=AF.Exp), reads=[n(c_g)], writes=[n(EC)]),
                lambda: p.op("dve", lambda e: e.tensor_tensor(out=QD[:], in0=q_g[:], in1=E1[:], op=ALU.mult), reads=[n(q_g), n(E1)], writes=[n(QD)]),
                lambda: p.op("dve", lambda e: e.tensor_tensor(out=KI[:], in0=k_g[:], in1=E2[:], op=ALU.mult), reads=[n(k_g), n(E2)], writes=[n(KI)]),
                lambda: p.op("dve", lambda e: e.tensor_tensor(out=E1[:], in0=E1[:], in1=EC[:], op=ALU.mult), reads=[n(E1), n(EC)], writes=[n(E1)]),
                lambda: p.op("dve", lambda e: e.tensor_tensor(out=QTt[:], in0=q_g[:], in1=E1[:], op=ALU.mult), reads=[n(q_g), n(E1)], writes=[n(QTt)]),
                lambda: p.op("dve", lambda e: e.tensor_tensor(out=v3(TD), in0=v3(c_g), in1=nctm[:, chs].unsqueeze(2).to_broadcast([128, CPG, 128]), op=ALU.add), reads=[n(c_g), n(nctm)], writes=[n(TD)]),
                lambda: p.op("pool", lambda e: e.tensor_tensor(out=v3(TD), in0=v3(TD), in1=mneg[:].unsqueeze(1).to_broadcast([128, CPG, 128]), op=ALU.add), reads=[n(TD), "mneg_s"], writes=[n(TD)]),
                lambda: p.op("act", lambda e: e.activation(out=TD[:], in_=TD[:], func=AF.Exp), reads=[n(TD)], writes=[n(TD)]),
                lambda: p.op("dve", lambda e: e.tensor_tensor(out=TK[:], in0=cend[:, chs].unsqueeze(2).to_broadcast([128, CPG, 64]), in1=btm_g[:], op=ALU.subtract), reads=[n(cend), n(btm_g)], writes=[n(TK)]),
                lambda: p.op("act", lambda e: e.activation(out=TK[:], in_=TK[:], func=AF.Exp), reads=[n(TK)], writes=[n(TK)]),
                lambda: p.op("dve", lambda e: e.tensor_tensor(out=KIE[:], in0=ktm_g[:], in1=TK[:], op=ALU.mult), reads=[n(ktm_g), n(TK)], writes=[n(KIE)]),
            ]

        load_group(0)
        for th in prologue(0):
            th()
        load_group(1)
        for G in range(NG):
            b = G % 2
            QD, KI, QTt, TD, KIE = QDs[b], KIs[b], QTs[b], TDs[b], KIEs[b]
            vt = vts.pop(G)
            cs0 = G * GT
            nxt = prologue(G + 1) if G + 1 < NG else []
            og = ost.next()
            for j in range(CPG):
                jj = CPG * G + j
                sl = slice(128 * j, 128 * (j + 1))
                oc = slice(128 * (j % 4), 128 * (j % 4 + 1))
                ps = psx.next()
                ad = AD.next()
                t3_ = t3.next()
                p.op("pe", lambda e, ps=ps, sl=sl, KI=KI, QD=QD: e.matmul(ps[:, 0:128], KI[:, sl], QD[:, sl], start=True, stop=True), reads=[n(KI), n(QD)], writes=[(ps.name, 0)])
                p.op("dve", lambda e, ad=ad, ps=ps, sl=sl, TD=TD: e.tensor_tensor(out=ad[:], in0=ps[:, 0:128], in1=TD[:, sl], op=ALU.mult), reads=[(ps.name, 0), n(TD)], writes=[ad.name])

                def mo(e, vt=vt, j=j, ad=ad, sl=sl, oc=oc, QTt=QTt):
                    e.matmul(pso[0:64, oc], vt[:, j, :], ad[:], start=True, stop=False)
                    return e.matmul(pso[0:64, oc], Sb[:], QTt[:, sl], start=False, stop=True)
                p.op("pe", mo, reads=[vt.name, ad.name, n(Sb), n(QTt)], writes=[(n(pso), j % 4)])
                p.op("pe", lambda e, ps=ps, vt=vt, j=j, KIE=KIE: e.matmul(ps[0:64, 256:320], KIE[:, j, :], vt[:, j, :], start=True, stop=True), reads=[n(KIE), vt.name], writes=[(ps.name, 2)])
                p.op("dve", lambda e, t3_=t3_, ps=ps, jj=jj: e.tensor_scalar(out=t3_[:], in0=ps[0:64, 256:320], scalar1=eb[:, jj:jj + 1], scalar2=None, op0=ALU.mult), reads=[(ps.name, 2), n(eb)], writes=[t3_.name])
                p.op("dve", lambda e, t3_=t3_, jj=jj: e.scalar_tensor_tensor(out=S[:], in0=S[:], scalar=dtot[:, jj:jj + 1], in1=t3_[:], op0=ALU.mult, op1=ALU.add), reads=[n(S), n(dtot), t3_.name], writes=[n(S)])
                p.op("act", lambda e: e.activation(out=Sb[:], in_=S[:], func=AF.Copy), reads=[n(S)], writes=[n(Sb)])
                if j % 4 == 3:
                    q4 = j // 4
                    p.op("act", lambda e, og=og, q4=q4: e.activation(out=og[:, 512 * q4:512 * (q4 + 1)], in_=pso[0:64, :], func=AF.Copy), reads=[n(pso)], writes=[(og.name, q4)])
                if nxt and j in (0, 2, 4):
                    k_ = {0: 7, 2: 3, 4: 3}[j]
                    for _ in range(k_):
                        nxt.pop(0)()
                yield
            while nxt:
                nxt.pop(0)()
            if G + 2 < NG:
                load_group(G + 2)
            D(lambda e, og=og, cs0=cs0: e.dma_start(out=go[h2, :, cs0:cs0 + GT], in_=og[:]), reads=[og.name])

    def gen_fox():
        QT = c.sb("QT", [67, 8192], BF16)
        KT = c.sb("KT", [67, SEQ], BF16)
        V = c.sb("V", [128, NCH, 65], BF16)
        msk = c.sb("msk_s", [128, 8, 512], BF16)
        idb = c.sb("idb_s", [128, 128], BF16)
        cqr = c.sb("cqr", [8, 1024], F32)
        r1 = c.sb("r1", [8, 1024], F32)
        hi = c.sb("hi", [8, 1024], BF16)
        mid = c.sb("mid", [8, 1024], BF16)
        lo = c.sb("lo", [8, 1024], BF16)
        id8 = c.sb("id8_s", [8, 8], F32)
        od8 = c.sb("od8", [8, 8], F32)
        offd = c.sb("offd", [8, 1], F32)
        negc = c.sb("negc", [128, NCH], F32)
        Tt = c.sb("Tt", [128, 8], F32)
        offk = c.sb("offk", [128, 8], F32)
        pt = c.rot("pt", 3, [128, 512], BF16)
        oun = c.rot("oun", 2, [64, 512], BF16)
        den = c.rot("den", 1, [65, 512], F32)
        ps_s = c.rot("ps_s", 3, [128, 512], F32, psum=True)
        ps_o = c.ps("ps_o")

        D(lambda e: e.dma_start(out=QT[0:64, :], in_=fqT), writes=[("QT", 0)])
        D(lambda e: e.dma_start(out=KT[0:64, :], in_=fkT), writes=[("KT", 0)])
        D(lambda e: e.dma_start(out=V[:, :, 0:64], in_=fv), writes=[("V", 0)])
        D(lambda e: e.dma_start(out=msk[:], in_=mskd), writes=["msk_s"])
        D(lambda e: e.dma_start(out=idb[:], in_=idbd), writes=["idb_s"])
        D(lambda e: e.dma_start(out=cqr[:], in_=cq), writes=["cqr"])
        D(lambda e: e.dma_start(out=id8[:], in_=id8d), writes=["id8_s"])
        D(lambda e: e.dma_start(out=negc[:], in_=ck), writes=["negc"])
        D(lambda e: e.dma_start(out=Tt[:], in_=Ttm), writes=["Tt"])
        p.op("pool", lambda e: e.memset(V[:, :, 64:65], 1.0), writes=[("V", 1)])
        p.op("pool", lambda e: e.memset(KT[64:67, :], 1.0), writes=[("KT", 1)])
        p.op("pool", lambda e: e.memset(offk[:], 0.0), writes=["offk"])
        for j in range(1, 8):
            p.op("dve", lambda e, j=j: e.tensor_tensor(out=offk[:, j:j + 1], in0=offk[:, j - 1:j], in1=Tt[:, j - 1:j], op=ALU.add), reads=["offk", "Tt"], writes=["offk"])
        for j in range(1, 8):
            p.op("dve", lambda e, j=j: e.tensor_scalar(out=negc[:, 16 * j:16 * j + 16], in0=negc[:, 16 * j:16 * j + 16], scalar1=offk[:, j:j + 1], scalar2=None, op0=ALU.add), reads=["negc", "offk"], writes=["negc"])
        p.op("dve", lambda e: e.tensor_scalar(out=negc[:], in0=negc[:], scalar1=-1.0, scalar2=None, op0=ALU.mult), reads=["negc"], writes=["negc"])
        p.op("dve", lambda e: e.tensor_tensor(out=od8[:], in0=offk[0:8, 0:8], in1=id8[:], op=ALU.mult), reads=["offk", "id8_s"], writes=["od8"])
        p.op("dve", lambda e: e.reduce_sum(out=offd[:], in_=od8[:], axis=AX.X), reads=["od8"], writes=["offd"])
        p.op("dve", lambda e: e.tensor_scalar(out=cqr[:], in0=cqr[:], scalar1=offd[:, 0:1], scalar2=None, op0=ALU.add), reads=["cqr", "offd"], writes=["cqr"])
        p.op("dve", lambda e: e.tensor_copy(out=hi[:], in_=cqr[:]), reads=["cqr"], writes=["hi"])
        p.op("dve", lambda e: e.tensor_tensor(out=r1[:], in0=cqr[:], in1=hi[:], op=ALU.subtract), reads=["cqr", "hi"], writes=["r1"])
        p.op("dve", lambda e: e.tensor_copy(out=mid[:], in_=r1[:]), reads=["r1"], writes=["mid"])
        p.op("dve", lambda e: e.tensor_tensor(out=r1[:], in0=r1[:], in1=mid[:], op=ALU.subtract), reads=["r1", "mid"], writes=["r1"])
        p.op("dve", lambda e: e.tensor_copy(out=lo[:], in_=r1[:]), reads=["r1"], writes=["lo"])
        for j in range(8):
            D(lambda e, j=j: e.dma_start(out=QT[64:65, 1024 * j:1024 * (j + 1)], in_=hi[j:j + 1, :]), reads=["hi"], writes=[("QT", 10 + j)])
            D(lambda e, j=j: e.dma_start(out=QT[65:66, 1024 * j:1024 * (j + 1)], in_=mid[j:j + 1, :]), reads=["mid"], writes=[("QT", 20 + j)])
            D(lambda e, j=j: e.dma_start(out=QT[66:67, 1024 * j:1024 * (j + 1)], in_=lo[j:j + 1, :]), reads=["lo"], writes=[("QT", 30 + j)])
        yield
        for i in range(16):
            nkb = 8 * i + 8
            qs = slice(512 * i, 512 * (i + 1))
            pend = {}

            def mmS(kb, i=i, qs=qs, pend=pend):
                ps = ps_s.next()
                pend[kb] = ps

                def f(e, ps=ps, kb=kb):
                    ins = e.matmul(ps[:], KT[0:67, 128 * kb:128 * (kb + 1)], QT[0:67, qs], start=True, stop=(kb < 8 * i))
                    if kb >= 8 * i:
                        ins = e.matmul(ps[:], idb[:], msk[:, kb - 8 * i, :], start=False, stop=True)
                    return ins
                p.op("pe", f, reads=["KT", "QT", "idb_s", "msk_s"], writes=[ps.name])

            mmS(0)
            mmS(1)
            for kb in range(nkb):
                if kb + 2 < nkb:
                    mmS(kb + 2)
                ps = pend.pop(kb)
                t_ = pt.next()
                p.op("act", lambda e, t_=t_, ps=ps, kb=kb: e.activation(out=t_[:], in_=ps[:], func=AF.Exp, bias=negc[:, kb:kb + 1]), reads=[ps.name, "negc"], writes=[t_.name])
                p.op("pe", lambda e, t_=t_, kb=kb, nkb=nkb: e.matmul(ps_o[0:65, :], V[:, kb, :], t_[:], start=(kb == 0), stop=(kb == nkb - 1)), reads=["V", t_.name], writes=["ps_o"])
                if kb == nkb - 1:
                    o_, d_ = oun.next(), den.next()
                    p.op("act", lambda e, o_=o_: e.activation(out=o_[:], in_=ps_o[0:64, :], func=AF.Copy), reads=["ps_o"], writes=[o_.name])
                    p.op("act", lambda e, d_=d_: e.activation(out=d_[64:65, :], in_=ps_o[64:65, :], func=AF.Copy), reads=["ps_o"], writes=[d_.name])
                    D(lambda e, o_=o_, qs=qs: e.dma_start(out=fo[:, qs], in_=o_[:]), reads=[o_.name])
                    D(lambda e, d_=d_, qs=qs: e.dma_start(out=fden[0:1, qs], in_=d_[64:65, :]), reads=[d_.name])
                yield

    gf, g0, g1 = gen_fox(), gen_generic(0), gen_generic(1)
    next(gf)
    alive = [True, True, True]
    while any(alive):
        for gi, g in ((0, g0), (1, g1)):
            if alive[gi]:
                try:
                    next(g)
                except StopIteration:
                    alive[gi] = False
        for _ in range(9):
            if alive[2]:
                try:
                    next(gf)
                except StopIteration:
                    alive[2] = False
    run_prog(nc, p)
    return nc


_NC = {}


def _get(name, fn):
    if name not in _NC:
        _NC[name] = fn()
    return _NC[name]


def run_l1(xfull, P, pos):
    nc = _get("l1", build_l1)
    w1 = np.ascontiguousarray(P["w_in"][:, l1_weight_cols()])
    ln1 = np.ascontiguousarray(P["ln1"].reshape(8, 128).T)
    pp = l1_params(P)
    cst = l1_consts()
    maps = []
    for c in range(8):
        t0 = c * NT
        xs = np.zeros((1024, HP + NT), np.float32)
        xs[:, HP:] = xfull[t0:t0 + NT].T
        if c > 0:
            xs[:, 1:HP] = xfull[t0 - 3:t0].T
        maps.append(dict(xT=xs, w1=w1, ln1=ln1, pp=pp, w2=np.ascontiguousarray(P["gla_w2"]),
                         pos=np.ascontiguousarray(np.broadcast_to(pos[t0:t0 + NT], (128, NT))).astype(np.int32), cst=cst))
    res = run_bass_kernel_spmd(nc, maps, core_ids=list(range(8))).results
    OB = np.concatenate([r["ob"] for r in res], axis=1)
    OF = np.concatenate([r["of"] for r in res], axis=1)
    return OB, OF


def _tm(a):
    R = a.shape[0]
    return np.ascontiguousarray(a.T.reshape(NCH, 128, R).transpose(1, 0, 2))


def run_l2(OB, OF):
    nc = _get("l2", build_l2)
    bf = OB.dtype
    idb = np.eye(128, dtype=np.float32).astype(bf)
    s_ = np.arange(128)[:, None]
    mneg = np.where(np.arange(128)[None, :] >= s_, 0.0, -1.0e4).astype(np.float32)
    id8 = np.eye(8, dtype=np.float32)
    lg = np.log(1.0 - 2.0 ** (-5.0 - np.arange(4, dtype=np.float32))).astype(np.float32)
    zeros_b = np.zeros((64, SEQ), np.float32)
    heads = []
    for h in range(4):
        heads.append(dict(q=OB[768 + 64 * h:832 + 64 * h], k=OB[1024 + 64 * h:1088 + 64 * h], v=OB[1280 + 64 * h:1344 + 64 * h],
                          b=OF[4 + 64 * h:68 + 64 * h], c=np.zeros(SEQ, np.float32)))
    for h in range(4):
        cc = np.tile((np.arange(128, dtype=np.float32) + 1.0) * lg[h], NCH).astype(np.float32)
        heads.append(dict(q=OB[1536 + 64 * h:1600 + 64 * h], k=OB[1792 + 64 * h:1856 + 64 * h], v=OB[2048 + 64 * h:2112 + 64 * h], b=zeros_b, c=cc))
    for h in range(8):
        g = h // 4
        heads.append(dict(q=OB[3456 + 64 * g:3520 + 64 * g], k=OB[3328 + 64 * g:3392 + 64 * g], v=OB[2816 + 64 * h:2880 + 64 * h], b=zeros_b, c=OF[260 + 64 * h]))
    maps = []
    for c in range(8):
        hf, par = c // 2, c % 2
        m = {}
        m["fqT"] = np.ascontiguousarray(OB[64 * hf:64 * hf + 64].reshape(64, 32, 512)[:, par::2].reshape(64, 8192))
        m["fkT"] = np.ascontiguousarray(OB[256 + 64 * hf:320 + 64 * hf])
        m["fv"] = _tm(OB[512 + 64 * hf:576 + 64 * hf])
        crow = OF[hf]
        m["cq"] = np.ascontiguousarray(crow.reshape(32, 512)[par::2].reshape(8, 1024))
        m["ck"] = np.ascontiguousarray(crow.reshape(NCH, 128).T)
        T = crow[NT - 1::NT]
        m["Ttm"] = np.ascontiguousarray(np.broadcast_to(T, (128, 8))).astype(np.float32)
        q_ = np.arange(512)[None, None, :]
        jb = np.arange(8)[None, :, None]
        ss = np.arange(128)[:, None, None]
        m["msk"] = np.where(128 * jb + ss <= 512 * par + q_, 0.0, -30000.0).astype(np.float32).astype(bf)
        m["idb"] = idb
        m["mneg"] = mneg
        m["id8"] = id8
        hs = [heads[2 * c], heads[2 * c + 1]]
        m["gq"] = np.stack([h["q"] for h in hs])
        m["gk"] = np.stack([h["k"] for h in hs])
        m["gktm"] = np.stack([_tm(h["k"]) for h in hs])
        m["gvtm"] = np.stack([_tm(h["v"]) for h in hs])
        m["gb"] = np.stack([h["b"] for h in hs]).astype(np.float32)
        m["gbtm"] = np.stack([_tm(h["b"]) for h in hs]).astype(np.float32)
        m["gblast"] = np.stack([np.ascontiguousarray(h["b"][:, 127::128]) for h in hs]).astype(np.float32)
        m["gc128"] = np.stack([np.ascontiguousarray(np.broadcast_to(h["c"], (128, SEQ))) for h in hs]).astype(np.float32)
        m["gctm"] = np.stack([np.ascontiguousarray(h["c"].reshape(NCH, 128).T) for h in hs]).astype(np.float32)
        m["gclast"] = np.stack([np.ascontiguousarray(np.broadcast_to(h["c"][127::128], (128, NCH))) for h in hs]).astype(np.float32)
        maps.append(m)
    res = run_bass_kernel_spmd(nc, maps, core_ids=list(range(8))).results
    FO = np.zeros((256, SEQ), dtype=bf)
    FD = np.zeros((4, SEQ), np.float32)
    GO = np.zeros((16, 64, SEQ), np.float32)
    for c in range(8):
        hf, par = c // 2, c % 2
        FO[64 * hf:64 * hf + 64].reshape(64, 32, 512)[:, par::2] = res[c]["fo"].reshape(64, 16, 512)
        FD[hf].reshape(32, 512)[par::2] = res[c]["fden"].reshape(16, 512)
        GO[2 * c:2 * c + 2] = res[c]["go"]
    return FO, GO, FD


def build_l3():
    c = Ctx()
    nc, p = c.nc, c.p
    D = p.dma
    xT = c.dram("xT", [1024, NT], F32, "ExternalInput")
    w3 = c.dram("w3", [1024, 5120], F32, "ExternalInput")
    wup = c.dram("wup", [1280, 1024], F32, "ExternalInput")
    wout = c.dram("wout", [1024, 1024], F32, "ExternalInput")
    wfi = c.dram("wfi", [1024, 5632], F32, "ExternalInput")
    wfo = c.dram("wfo", [2816, 1024], F32, "ExternalInput")
    ln1 = c.dram("ln1", [128, 8], F32, "ExternalInput")
    ln2 = c.dram("ln2", [128, 8], F32, "ExternalInput")
    pp = c.dram("pp", [128, 16], F32, "ExternalInput")
    ya = c.dram("ya", [256, NT], BF16, "ExternalInput")
    yden = c.dram("yden", [256, NT], F32, "ExternalInput")
    goT = c.dram("goT", [1024, NT], F32, "ExternalInput")
    xsT = c.dram("xsT", [512, NT], BF16, "ExternalInput")
    xo = c.dram("xo", [1024, NT], F32, "ExternalOutput")

    XR = c.sb("XR", [128, 8, NT], F32)
    HT = c.sb("HT", [128, 8, NT], BF16)
    Y = c.sb("Y", [128, 10, TT], BF16)
    mg = c.sb("mg", [128, 8, TT], BF16)
    wupt = c.sb("wupt", [128, 10, 1024], BF16)
    woutt = c.sb("woutt", [128, 8, 1024], BF16)
    ws = c.rot("ws", 3, [128, 8, 512], BF16)
    sq = c.rot("sq", 2, [128, TT], BF16)
    rs = c.rot("rs", 2, [128, TT], F32)
    og = c.rot("og", 2, [128, TT], F32)
    yun = c.sb("yun", [128, 2, TT], BF16)
    ydn = c.sb("ydn", [128, 2, TT], F32)
    xsb = c.rot("xsb", 1, [128, TT], BF16)
    sg = c.rot("sg", 2, [128, TT], F32)
    ta = c.rot("ta", 2, [128, TT], F32)
    u2 = c.rot("u2_", 4, [128, TT], BF16)
    mt = c.rot("mt", 1, [128, TT], F32)
    ones = c.sb("ones", [128, 128], BF16)
    bd = c.sb("bd", [128, 128], BF16)
    lnw1 = c.sb("lnw1", [128, 8], F32)
    lnw2 = c.sb("lnw2", [128, 8], F32)
    ppt = c.sb("ppt", [128, 16], F32)
    epsb = c.sb("epsb", [128, 1], F32)
    psA = c.rot("psA", 4, [128, TT], F32, psum=True)
    psB = c.rot("psB", 3, [128, TT], F32, psum=True)

    D(lambda e: e.dma_start(out=lnw1[:], in_=ln1), writes=["lnw1"])
    D(lambda e: e.dma_start(out=lnw2[:], in_=ln2), writes=["lnw2"])
    D(lambda e: e.dma_start(out=ppt[:], in_=pp), writes=["ppt"])
    D(lambda e: e.dma_start(out=wupt[:], in_=wup.rearrange("(kc p) m -> p kc m", p=128)), writes=["wupt"], eng="pool")
    D(lambda e: e.dma_start(out=woutt[:], in_=wout.rearrange("(kc p) m -> p kc m", p=128)), writes=["woutt"], eng="pool")
    p.op("pool", lambda e: e.memset(ones[:], 1.0), writes=["ones"])
    p.op("pool", lambda e: e.memset(bd[:], 0.0), writes=["bd"])
    p.op("pool", lambda e: e.memset(bd[0:64, 0:64], 1.0), reads=["bd"], writes=["bd"])
    p.op("pool", lambda e: e.memset(bd[64:128, 64:128], 1.0), reads=["bd"], writes=["bd"])
    p.op("pool", lambda e: e.memset(epsb[:], EPS), writes=["epsb"])

    def rms(tt, lnw):
        ts_ = slice(tt * TT, (tt + 1) * TT)
        ps = psA.next()
        for kc in range(8):
            s = sq.next()
            p.op("act", lambda e, s=s, kc=kc: e.activation(out=s[:], in_=XR[:, kc, ts_], func=AF.Square), reads=[("XR", tt)], writes=[s.name])
            p.op("pe", lambda e, s=s, kc=kc: e.matmul(ps[:], ones[:], s[:], start=(kc == 0), stop=(kc == 7)), reads=[s.name, "ones"], writes=[ps.name])
        r = rs.next()
        p.op("act", lambda e: e.activation(out=r[:], in_=ps[:], func=AF.Ln, scale=1.0 / 1024, bias=epsb[:, 0:1]), reads=[ps.name, "epsb"], writes=[r.name])
        p.op("act", lambda e: e.activation(out=r[:], in_=r[:], func=AF.Exp, scale=-0.5), reads=[r.name], writes=[r.name])
        for kc in range(8):
            p.op("dve", lambda e, kc=kc: e.scalar_tensor_tensor(out=HT[:, kc, ts_], in0=XR[:, kc, ts_], scalar=lnw[:, kc:kc + 1], in1=r[:], op0=ALU.mult, op1=ALU.mult),
                 reads=[("XR", tt), lnw.name, r.name], writes=[("HT", tt)])

    def proj(ps, w, c0, tt, m=128):
        ts_ = slice(tt * TT, (tt + 1) * TT)

        def f(e):
            ins = None
            for kc in range(8):
                ins = e.matmul(ps[0:m, :], w[:, kc, c0:c0 + m], HT[:, kc, ts_], start=(kc == 0), stop=(kc == 7))
            return ins
        p.op("pe", f, reads=[w.name, ("HT", tt)], writes=[ps.name])

    def load_w(src, c0, ncols=512):
        w = ws.next()
        D(lambda e: e.dma_start(out=w[:, :, 0:ncols], in_=src[:, c0:c0 + ncols].rearrange("(kc p) m -> p kc m", p=128)), writes=[w.name], eng="pool")
        return w

    for tt in range(NTT):
        ts_ = slice(tt * TT, (tt + 1) * TT)
        D(lambda e, ts_=ts_: e.dma_start(out=XR[:, :, ts_], in_=xT[:, ts_].rearrange("(kc p) t -> p kc t", p=128)), writes=[("XR", tt)])
        rms(tt, lnw1)
        D(lambda e, ts_=ts_: e.dma_start(out=yun[:], in_=ya[:, ts_].rearrange("(c p) t -> p c t", p=128)), writes=["yun"])
        D(lambda e, ts_=ts_: e.dma_start(out=ydn[:], in_=yden[:, ts_].rearrange("(c p) t -> p c t", p=128)), writes=["ydn"])
        p.op("dve", lambda e: e.reciprocal(out=ydn[:], in_=ydn[:]), reads=["ydn"], writes=["ydn"])
        p.op("dve", lambda e: e.tensor_tensor(out=Y[:, 0:2, :], in0=yun[:], in1=ydn[:], op=ALU.mult), reads=["yun", "ydn"], writes=[("Y", 0), ("Y", 1)])
        w = load_w(w3, 0)
        for i4 in range(4):
            o_ = og.next()
            D(lambda e, o_=o_, i4=i4, ts_=ts_: e.dma_start(out=o_[:], in_=goT[128 * i4:128 * (i4 + 1), ts_]), writes=[o_.name])
            s = sq.next()
            p.op("act", lambda e, s=s, o_=o_: e.activation(out=s[:], in_=o_[:], func=AF.Square), reads=[o_.name], writes=[s.name])
            ps2 = psB.next()
            p.op("pe", lambda e, ps2=ps2, s=s: e.matmul(ps2[:], bd[:], s[:], start=True, stop=True), reads=["bd", s.name], writes=[ps2.name])
            r = rs.next()
            p.op("act", lambda e, r=r, ps2=ps2: e.activation(out=r[:], in_=ps2[:], func=AF.Ln, scale=1.0 / 64, bias=epsb[:, 0:1]), reads=[ps2.name, "epsb"], writes=[r.name])
            p.op("act", lambda e, r=r: e.activation(out=r[:], in_=r[:], func=AF.Exp, scale=-0.5), reads=[r.name], writes=[r.name])
            ps = psA.next()
            proj(ps, w, 128 * i4, tt)
            g_ = sg.next()
            p.op("act", lambda e, g_=g_, ps=ps: e.activation(out=g_[:], in_=ps[:], func=AF.Silu), reads=[ps.name], writes=[g_.name])
            t_ = ta.next()
            col = 0 if i4 < 2 else 1
            p.op("dve", lambda e, t_=t_, o_=o_, r=r, col=col: e.scalar_tensor_tensor(out=t_[:], in0=o_[:], scalar=ppt[:, col:col + 1], in1=r[:], op0=ALU.mult, op1=ALU.mult), reads=[o_.name, "ppt", r.name], writes=[t_.name])
            p.op("pool", lambda e, t_=t_, g_=g_, i4=i4: e.tensor_tensor(out=Y[:, 2 + i4, :], in0=t_[:], in1=g_[:], op=ALU.mult), reads=[t_.name, g_.name], writes=[("Y", 2 + i4)])
        w = load_w(w3, 512)
        us = []
        for j in range(4):
            o_ = og.next()
            D(lambda e, o_=o_, j=j, ts_=ts_: e.dma_start(out=o_[:], in_=goT[512 + 128 * j:640 + 128 * j, ts_]), writes=[o_.name])
            xb = xsb.next()
            D(lambda e, xb=xb, j=j, ts_=ts_: e.dma_start(out=xb[:], in_=xsT[128 * j:128 * (j + 1), ts_]), writes=[xb.name])
            t_ = ta.next()
            p.op("dve", lambda e, t_=t_, xb=xb, o_=o_, j=j: e.scalar_tensor_tensor(out=t_[:], in0=xb[:], scalar=ppt[:, 2 + j:3 + j], in1=o_[:], op0=ALU.mult, op1=ALU.add), reads=[xb.name, "ppt", o_.name], writes=[t_.name])
            ps = psA.next()
            proj(ps, w, 128 * j, tt)
            g_ = sg.next()
            p.op("act", lambda e, g_=g_, ps=ps: e.activation(out=g_[:], in_=ps[:], func=AF.Silu), reads=[ps.name], writes=[g_.name])
            u_ = u2.next()
            p.op("pool", lambda e, u_=u_, t_=t_, g_=g_: e.tensor_tensor(out=u_[:], in0=t_[:], in1=g_[:], op=ALU.mult), reads=[t_.name, g_.name], writes=[u_.name])
            us.append(u_)
        for gI in range(2):
            ps2 = psB.next()
            for k2 in range(2):
                s = sq.next()
                u_ = us[2 * gI + k2]
                p.op("act", lambda e, s=s, u_=u_: e.activation(out=s[:], in_=u_[:], func=AF.Square), reads=[u_.name], writes=[s.name])
                p.op("pe", lambda e, ps2=ps2, s=s, k2=k2: e.matmul(ps2[:], ones[:], s[:], start=(k2 == 0), stop=(k2 == 1)), reads=["ones", s.name], writes=[ps2.name])
            r = rs.next()
            p.op("act", lambda e, r=r, ps2=ps2: e.activation(out=r[:], in_=ps2[:], func=AF.Ln, scale=1.0 / 256, bias=epsb[:, 0:1]), reads=[ps2.name, "epsb"], writes=[r.name])
            p.op("act", lambda e, r=r: e.activation(out=r[:], in_=r[:], func=AF.Exp, scale=-0.5), reads=[r.name], writes=[r.name])
            for k2 in range(2):
                j = 2 * gI + k2
                u_ = us[j]
                p.op("dve", lambda e, u_=u_, r=r, j=j: e.scalar_tensor_tensor(out=Y[:, 6 + j, :], in0=u_[:], scalar=ppt[:, 6 + j:7 + j], in1=r[:], op0=ALU.mult, op1=ALU.mult), reads=[u_.name, "ppt", r.name], writes=[("Y", 6 + j)])
        kcs = [(0, 2), (2, 2), (4, 2), (6, 4)]
        for n in range(8):
            w = load_w(w3, 1024 + 512 * n)
            m_ = mt.next()
            for b in range(4):
                psg_ = psA.next()
                proj(psg_, w, 128 * b, tt)
                g_ = sg.next()
                p.op("act", lambda e, g_=g_, psg_=psg_: e.activation(out=g_[:], in_=psg_[:], func=AF.Sigmoid), reads=[psg_.name], writes=[g_.name])
                psu = psB.next()
                k0, nk = kcs[b]

                def fu(e, psu=psu, k0=k0, nk=nk, n=n):
                    ins = None
                    for q in range(nk):
                        ins = e.matmul(psu[:], wupt[:, k0 + q, 128 * n:128 * (n + 1)], Y[:, k0 + q, :], start=(q == 0), stop=(q == nk - 1))
                    return ins
                p.op("pe", fu, reads=["wupt", "Y"], writes=[psu.name])
                if b == 0:
                    p.op("dve", lambda e, m_=m_, psu=psu, g_=g_: e.tensor_tensor(out=m_[:], in0=psu[:], in1=g_[:], op=ALU.mult), reads=[psu.name, g_.name], writes=[m_.name])
                else:
                    t_ = ta.next()
                    p.op("dve", lambda e, t_=t_, psu=psu, g_=g_: e.tensor_tensor(out=t_[:], in0=psu[:], in1=g_[:], op=ALU.mult), reads=[psu.name, g_.name], writes=[t_.name])
                    if b < 3:
                        p.op("pool", lambda e, m_=m_, t_=t_: e.tensor_tensor(out=m_[:], in0=m_[:], in1=t_[:], op=ALU.add), reads=[m_.name, t_.name], writes=[m_.name])
                    else:
                        p.op("pool", lambda e, m_=m_, t_=t_, n=n: e.tensor_tensor(out=mg[:, n, :], in0=m_[:], in1=t_[:], op=ALU.add), reads=[m_.name, t_.name], writes=[("mg", n)])
        for n in range(8):
            ps = psA.next()

            def fo_(e, ps=ps, n=n):
                ins = None
                for kc in range(8):
                    ins = e.matmul(ps[:], woutt[:, kc, 128 * n:128 * (n + 1)], mg[:, kc, :], start=(kc == 0), stop=(kc == 7))
                return ins
            p.op("pe", fo_, reads=["woutt", "mg"], writes=[ps.name])
            p.op("dve", lambda e, ps=ps, n=n, ts_=ts_: e.tensor_tensor(out=XR[:, n, ts_], in0=XR[:, n, ts_], in1=ps[:], op=ALU.add), reads=[("XR", tt), ps.name], writes=[("XR", tt)])
        rms(tt, lnw2)
    slot = 0
    for hg in range(6):
        nh = 4 if hg < 5 else 2
        wg = load_w(wfi, 512 * hg, 128 * nh)
        wu = load_w(wfi, 2816 + 512 * hg, 128 * nh)
        half = hg % 2
        wkey = ("woutt", "h%d" % half)
        D(lambda e, half=half, hg=hg, nh=nh: e.dma_start(out=woutt[:, 4 * half:4 * half + nh, :], in_=wfo[512 * hg:512 * hg + 128 * nh, :].rearrange("(kc p) m -> p kc m", p=128)), writes=[wkey], eng="pool")
        for tt in range(NTT):
            ts_ = slice(tt * TT, (tt + 1) * TT)
            a0 = 4 * (slot % 2)
            slot += 1
            akeys = [("mg", a0 + q) for q in range(4)]
            for hc in range(nh):
                pg, pu = psA.next(), psB.next()
                proj(pg, wg, 128 * hc, tt)
                proj(pu, wu, 128 * hc, tt)
                g_ = sg.next()
                p.op("act", lambda e, g_=g_, pg=pg: e.activation(out=g_[:], in_=pg[:], func=AF.Silu), reads=[pg.name], writes=[g_.name])
                p.op("dve", lambda e, a0=a0, hc=hc, g_=g_, pu=pu: e.tensor_tensor(out=mg[:, a0 + hc, :], in0=pu[:], in1=g_[:], op=ALU.mult), reads=[pu.name, g_.name], writes=[("mg", a0 + hc)])
            for n in range(8):
                ps = psA.next()

                def ff_(e, ps=ps, n=n, half=half, a0=a0, nh=nh):
                    ins = None
                    for hc in range(nh):
                        ins = e.matmul(ps[:], woutt[:, 4 * half + hc, 128 * n:128 * (n + 1)], mg[:, a0 + hc, :], start=(hc == 0), stop=(hc == nh - 1))
                    return ins
                p.op("pe", ff_, reads=[wkey] + akeys, writes=[ps.name])
                p.op("dve", lambda e, ps=ps, n=n, ts_=ts_: e.tensor_tensor(out=XR[:, n, ts_], in0=XR[:, n, ts_], in1=ps[:], op=ALU.add), reads=[("XR", tt), ps.name], writes=[("XR", tt)])
    for tt in range(NTT):
        ts_ = slice(tt * TT, (tt + 1) * TT)
        D(lambda e, ts_=ts_: e.dma_start(out=xo[:, ts_].rearrange("(kc p) t -> p kc t", p=128), in_=XR[:, :, ts_]), reads=[("XR", tt)])
    run_prog(nc, p)
    return nc


def l3_weight_cols():
    C = COLS
    r = np.arange
    gates = []
    for n in range(8):
        for b in range(4):
            gates.append(r(C["gates"] + 1024 * b + 128 * n, C["gates"] + 1024 * b + 128 * (n + 1)))
    return np.concatenate([r(C["gr"], C["gr"] + 256), r(C["rg"], C["rg"] + 256), r(C["z"], C["z"] + 512)] + gates)


def run_l3(xfull, P, FO, GO, OB, FD):
    nc = _get("l3", build_l3)
    w3 = np.ascontiguousarray(P["w_in"][:, l3_weight_cols()])
    wup = np.ascontiguousarray(np.concatenate([P["w_up_a"], P["w_up_b"], P["w_up_c"], P["w_up_d"]], axis=0))
    pp = np.zeros((128, 16), np.float32)
    pp[:, 0] = np.tile(P["gla_norm"], 2)
    pp[:, 1] = np.tile(P["ret_norm"], 2)
    for j in range(4):
        pp[:, 2 + j] = np.repeat(P["ssd_d"][2 * j:2 * j + 2], 64)
        pp[:, 6 + j] = P["ssd_norm"][128 * j:128 * (j + 1)]
    ln1 = np.ascontiguousarray(P["ln1"].reshape(8, 128).T)
    ln2 = np.ascontiguousarray(P["ln2"].reshape(8, 128).T)
    GOf = GO.reshape(1024, SEQ)
    FDr = np.repeat(FD, 64, axis=0)
    maps = []
    for c in range(8):
        sl = slice(c * NT, (c + 1) * NT)
        maps.append(dict(xT=np.ascontiguousarray(xfull[sl].T), w3=w3, wup=wup, wout=np.ascontiguousarray(P["w_out"]),
                         wfi=np.ascontiguousarray(P["w_ffn_in"]), wfo=np.ascontiguousarray(P["w_ffn_out"]), ln1=ln1, ln2=ln2, pp=pp,
                         ya=np.ascontiguousarray(FO[:, sl]), yden=np.ascontiguousarray(FDr[:, sl]), goT=np.ascontiguousarray(GOf[:, sl]), xsT=np.ascontiguousarray(OB[2304:2816, sl])))
    res = run_bass_kernel_spmd(nc, maps, core_ids=list(range(8))).results
    return np.concatenate([r["xo"].T for r in res], axis=0)


def kernel(**inputs):
    x = np.asarray(inputs["x"], np.float32)[0]
    pos = np.asarray(inputs["positions"])[0]
    names = [k for k in inputs if k not in ("x", "positions")]
    for l in range(4):
        P = {k: np.asarray(inputs[k][l], np.float32) for k in names}
        OB, OF = run_l1(x, P, pos)
        FO, GO, FD = run_l2(OB, OF)
        x = run_l3(x, P, FO, GO, OB, FD)
    return x[None].astype(np.float32)
```

```python
import numpy as np
import concourse.bass as bass
import concourse.mybir as mybir

F32 = mybir.dt.float32
BF16 = mybir.dt.bfloat16
I32 = mybir.dt.int32
AF = mybir.ActivationFunctionType
ALU = mybir.AluOpType
AX = mybir.AxisListType

SAME_ENGINE_SYNC = True
N_DMA_CH = 8


class Prog:
    def __init__(self, nc):
        self.nc = nc
        self.ops = []
        self.dma_rr = {}

    def op(self, eng, fn, reads=(), writes=(), dma=False):
        self.ops.append(dict(eng=eng, fn=fn, reads=[_k(k) for k in reads],
                             writes=[_k(k) for k in writes], dma=dma))

    def dma(self, fn, reads=(), writes=(), eng="sp"):
        self.op(eng, fn, reads, writes, dma=True)

    def plan(self):
        ops = self.ops
        state = {}
        eng_cnt = {}
        ch_cnt = {}
        ch_last = {}
        rr = {}
        for i, o in enumerate(ops):
            deps = set()
            for (name, sub) in o["reads"]:
                for rec in state.get(name, []):
                    if rec[0] is None or sub is None or rec[0] == sub:
                        if rec[1] is not None:
                            deps.add(rec[1])
            for (name, sub) in o["writes"]:
                for rec in state.get(name, []):
                    if rec[0] is None or sub is None or rec[0] == sub:
                        if rec[1] is not None:
                            deps.add(rec[1])
                        deps.update(rec[2])
            if o["dma"]:
                e = o["eng"]
                ch = rr.get(e, 0)
                rr[e] = (ch + 1) % N_DMA_CH
                key = ("dma", e, ch)
                if key in ch_last:
                    deps.add(ch_last[key])
                ch_last[key] = i
                ch_cnt[key] = ch_cnt.get(key, 0) + 1
                o["sem"] = key
                o["semval"] = 16 * ch_cnt[key]
            else:
                e = o["eng"]
                eng_cnt[e] = eng_cnt.get(e, 0) + 1
                o["sem"] = ("eng", e)
                o["semval"] = eng_cnt[e]
            deps.discard(i)
            o["deps"] = deps
            for (name, sub) in o["reads"]:
                recs = state.setdefault(name, [])
                hit = False
                for rec in recs:
                    if rec[0] == sub:
                        rec[2].add(i)
                        hit = True
                if not hit:
                    recs.append([sub, None, {i}])
                for rec in recs:
                    if rec[0] != sub and (rec[0] is None or sub is None):
                        rec[2].add(i)
            for (name, sub) in o["writes"]:
                recs = state.setdefault(name, [])
                hit = False
                for rec in recs:
                    if rec[0] == sub:
                        rec[1] = i
                        rec[2] = set()
                        hit = True
                    elif rec[0] is None or sub is None:
                        rec[1] = i
                        rec[2] = set()
                if not hit:
                    recs.append([sub, i, set()])
        waited = {}
        for i, o in enumerate(ops):
            need = {}
            for d in o["deps"]:
                od = ops[d]
                if (not od["dma"]) and od["eng"] == o["eng"] and not o["dma"]:
                    if o["eng"] == "pe" or not SAME_ENGINE_SYNC:
                        continue
                    raw = any(_overlap(r, w) for r in o["reads"] for w in od["writes"])
                    if not raw:
                        continue
                need[od["sem"]] = max(need.get(od["sem"], 0), od["semval"])
            w = []
            for s, v in need.items():
                if waited.get((o["eng"], s), 0) < v:
                    waited[(o["eng"], s)] = v
                    w.append((s, v))
            o["waits"] = w
        self.sem_keys = sorted({o["sem"] for o in ops}, key=str)
        return self

    def emit(self, block_engines, sems):
        raise NotImplementedError

    def emit_engine(self, name, eng, sems):
        for o in self.ops:
            if o["eng"] != name:
                continue
            for (s, v) in o["waits"]:
                eng.wait_ge(sems[s], v)
            ins = o["fn"](eng)
            inc = 16 if o["dma"] else 1
            ins.then_inc(sems[o["sem"]], inc)


def _k(k):
    if isinstance(k, tuple):
        return (k[0], k[1])
    return (k, None)


def _overlap(a, b):
    return a[0] == b[0] and (a[1] is None or b[1] is None or a[1] == b[1])


ENG_ATTR = {"pe": "tensor", "act": "scalar", "dve": "vector", "pool": "gpsimd", "sp": "sync"}


def run_prog(nc, prog, tail_waits=True):
    prog.plan()
    import contextlib
    with contextlib.ExitStack() as st:
        sems = {}
        for k in prog.sem_keys:
            sems[k] = st.enter_context(nc.semaphore("s_" + "_".join(str(x) for x in k)))
        block = st.enter_context(nc.Block())
        used = {o["eng"] for o in prog.ops}
        finals = {}
        for o in prog.ops:
            finals[o["sem"]] = o["semval"]

        def mk(name):
            def body(eng):
                prog.emit_engine(name, eng, sems)
                if name == "sp":
                    for k, v in finals.items():
                        eng.wait_ge(sems[k], v)
            return body

        for name in ["sp", "pe", "act", "dve", "pool"]:
            if name in used or name == "sp":
                getattr(block, ENG_ATTR[name])(mk(name))
    return nc

from concourse.bass_utils import run_bass_kernel_spmd
import ml_dtypes

NT = 2048
TT = 512
NTT = 4
HP = 4
EPS = 1e-6
TWO_PI = 6.283185307179586


class Rot:
    def __init__(self, tiles):
        self.tiles = tiles
        self.i = 0

    def next(self):
        t = self.tiles[self.i % len(self.tiles)]
        self.i += 1
        return t


class Ctx:
    def __init__(self):
        self.nc = bass.Bass("TRN2", target_bir_lowering=False)
        self.p = Prog(self.nc)
        self.names = {}

    def dram(self, name, shape, dt, kind):
        return self.nc.dram_tensor(name, shape, dt, kind=kind).ap()

    def sb(self, name, shape, dt):
        t = self.nc.alloc_sbuf_tensor(name, shape, dt)
        return _Tile(t, name)

    def ps(self, name, shape=(128, 512), dt=F32):
        t = self.nc.alloc_psum_tensor(name, list(shape), dt)
        return _Tile(t, name)

    def rot(self, prefix, n, shape, dt, psum=False):
        return Rot([(self.ps if psum else self.sb)(f"{prefix}{i}", list(shape), dt) for i in range(n)])


class _Tile:
    def __init__(self, t, name):
        self.t = t
        self.name = name

    def __getitem__(self, k):
        return self.t[k]


def build_l1():
    c = Ctx()
    nc, p = c.nc, c.p
    xT = c.dram("xT", [1024, HP + NT], F32, "ExternalInput")
    w1 = c.dram("w1", [1024, 4116], F32, "ExternalInput")
    ln1 = c.dram("ln1", [128, 8], F32, "ExternalInput")
    pp = c.dram("pp", [128, 48], F32, "ExternalInput")
    w2 = c.dram("w2", [16, 256], F32, "ExternalInput")
    pos = c.dram("pos", [128, NT], I32, "ExternalInput")
    cst = c.dram("cst", [128, 4], F32, "ExternalInput")
    ob = c.dram("ob", [3584, NT], BF16, "ExternalOutput")
    of = c.dram("of", [772, NT], F32, "ExternalOutput")

    hT = c.sb("hT", [128, 8, HP + NT], BF16)
    xt = c.rot("xt", 2, [128, 8, TT], F32)
    xh = c.sb("xh", [128, 8, HP], F32)
    sq = c.rot("sq", 2, [128, TT], BF16)
    rs = c.rot("rs", 2, [128, TT], F32)
    ones = c.sb("ones", [128, 128], BF16)
    bd = c.sb("bd", [128, 128], BF16)
    lnw = c.sb("lnw", [128, 8], F32)
    ppt = c.sb("ppt", [128, 48], F32)
    npp = c.sb("npp", [128, 48], F32)
    na = c.sb("na", [128, 4], F32)
    cstt = c.sb("cstt", [128, 4], F32)
    epsb = c.sb("epsb", [128, 1], F32)
    ws = c.rot("ws", 3, [128, 8, 512], BF16)
    wsm = c.sb("wsm", [128, 8, 20], BF16)
    w2t = c.sb("w2t", [16, 256], BF16)
    cos2 = c.sb("cos2", [128, NT], F32)
    sin2 = c.sb("sin2", [128, NT], F32)
    posi = c.sb("posi", [128, NT], I32)
    tr_a = c.sb("tr_a", [128, NT], F32)
    tr_b = c.sb("tr_b", [128, NT], F32)
    tr_i = c.sb("tr_i", [128, NT], I32)
    t1 = c.rot("t1_", 2, [128, TT], F32)
    t2 = c.rot("t2_", 2, [128, TT], F32)
    obuf = c.rot("obuf", 4, [128, TT], BF16)
    fbuf = c.rot("fbuf", 3, [128, TT], F32)
    pre = c.rot("pre", 2, [128, TT + 3], F32)
    acc = c.rot("acc", 2, [128, TT], F32)
    sil = c.rot("sil", 2, [128, TT], F32)
    dtr = c.rot("dtr", 2, [128, TT], F32)
    glrT = c.rot("glrT", 2, [16, TT], BF16)
    psA = c.rot("psA", 4, [128, TT], F32, psum=True)
    psB = c.rot("psB", 3, [128, TT], F32, psum=True)
    psH = c.ps("psH", [128, 8])
    onesf = c.sb("onesf", [128, TT], F32)
    crow = c.sb("crow", [4, NT], F32)
    fsc = c.rot("fsc", 2, [128, TT], F32)

    D = p.dma
    D(lambda e: e.dma_start(out=lnw[:], in_=ln1), writes=["lnw"])
    D(lambda e: e.dma_start(out=ppt[:], in_=pp), writes=["ppt"])
    D(lambda e: e.dma_start(out=cstt[:], in_=cst), writes=["cstt"])
    D(lambda e: e.dma_start(out=posi[:], in_=pos), writes=["posi"])
    D(lambda e: e.dma_start(out=w2t[:], in_=w2), writes=["w2t"], eng="pool")
    D(lambda e: e.dma_start(out=wsm[:], in_=w1[:, 4096:4116].rearrange("(kc p) m -> p kc m", p=128)), writes=["wsm"], eng="pool")
    D(lambda e: e.dma_start(out=xh[:], in_=xT[:, 0:HP].rearrange("(kc p) t -> p kc t", p=128)), writes=["xh"])
    p.op("pool", lambda e: e.memset(ones[:], 1.0), writes=["ones"])
    p.op("pool", lambda e: e.memset(onesf[:], 1.0), writes=["onesf"])
    p.op("pool", lambda e: e.memset(bd[:], 0.0), writes=["bd"])
    p.op("pool", lambda e: e.memset(bd[0:64, 0:64], 1.0), reads=["bd"], writes=["bd"])
    p.op("pool", lambda e: e.memset(bd[64:128, 64:128], 1.0), reads=["bd"], writes=["bd"])
    p.op("pool", lambda e: e.memset(epsb[:], EPS), writes=["epsb"])
    p.op("dve", lambda e: e.tensor_scalar(out=npp[:], in0=ppt[:], scalar1=-1.0, scalar2=None, op0=ALU.mult), reads=["ppt"], writes=["npp"])
    p.op("act", lambda e: e.activation(out=na[:], in_=ppt[:, 9:13], func=AF.Exp), reads=["ppt"], writes=["na"])
    p.op("dve", lambda e: e.tensor_scalar(out=na[:], in0=na[:], scalar1=-1.0, scalar2=None, op0=ALU.mult), reads=["na"], writes=["na"])
    p.op("dve", lambda e: e.tensor_copy(out=tr_a[:], in_=posi[:]), reads=["posi"], writes=["tr_a"])
    p.op("dve", lambda e: e.tensor_scalar(out=tr_a[:], in0=tr_a[:], scalar1=cstt[:, 0:1], scalar2=1.0 / TWO_PI, op0=ALU.mult, op1=ALU.mult), reads=["tr_a", "cstt"], writes=["tr_a"])
    for which, dst, col in (("s", sin2, 1), ("c", cos2, 2)):
        if which == "c":
            p.op("dve", lambda e: e.tensor_scalar(out=tr_a[:], in0=tr_a[:], scalar1=0.25, scalar2=None, op0=ALU.add), reads=["tr_a"], writes=["tr_a"])
        p.op("dve", lambda e: e.tensor_copy(out=tr_i[:], in_=tr_a[:]), reads=["tr_a"], writes=["tr_i"])
        p.op("dve", lambda e: e.tensor_copy(out=tr_b[:], in_=tr_i[:]), reads=["tr_i"], writes=["tr_b"])
        p.op("dve", lambda e: e.tensor_tensor(out=tr_b[:], in0=tr_a[:], in1=tr_b[:], op=ALU.subtract), reads=["tr_a", "tr_b"], writes=["tr_b"])
        p.op("dve", lambda e, dst=dst: e.tensor_scalar(out=dst[:], in0=tr_b[:], scalar1=0.5, scalar2=None, op0=ALU.is_gt), reads=["tr_b"], writes=[dst.name])
        p.op("dve", lambda e, dst=dst: e.tensor_tensor(out=tr_b[:], in0=tr_b[:], in1=dst[:], op=ALU.subtract), reads=["tr_b", dst.name], writes=["tr_b"])
        p.op("act", lambda e, dst=dst: e.activation(out=dst[:], in_=tr_b[:], func=AF.Sin, scale=TWO_PI), reads=["tr_b"], writes=[dst.name])
        p.op("dve", lambda e, dst=dst, col=col: e.tensor_scalar(out=dst[:], in0=dst[:], scalar1=cstt[:, col:col + 1], scalar2=None, op0=ALU.mult), reads=[dst.name, "cstt"], writes=[dst.name])

    def rms(xs_, n, dst_cols):
        ps = psA.next()
        for kc in range(8):
            s = sq.next()
            p.op("act", lambda e, s=s, kc=kc: e.activation(out=s[:, 0:n], in_=xs_[:, kc, 0:n], func=AF.Square), reads=[xs_.name], writes=[s.name])
            p.op("pe", lambda e, s=s, kc=kc: e.matmul(ps[:, 0:n], ones[:], s[:, 0:n], start=(kc == 0), stop=(kc == 7)), reads=[s.name, "ones"], writes=[ps.name])
        r = rs.next()
        p.op("act", lambda e: e.activation(out=r[:, 0:n], in_=ps[:, 0:n], func=AF.Ln, scale=1.0 / 1024, bias=epsb[:, 0:1]), reads=[ps.name, "epsb"], writes=[r.name])
        p.op("act", lambda e: e.activation(out=r[:, 0:n], in_=r[:, 0:n], func=AF.Exp, scale=-0.5), reads=[r.name], writes=[r.name])
        for kc in range(8):
            p.op("dve", lambda e, kc=kc: e.scalar_tensor_tensor(out=hT[:, kc, dst_cols[0]:dst_cols[1]], in0=xs_[:, kc, 0:n], scalar=lnw[:, kc:kc + 1], in1=r[:, 0:n], op0=ALU.mult, op1=ALU.mult),
                 reads=[xs_.name, "lnw", r.name], writes=[("hT", dst_cols[0])])

    rms(xh, HP, (0, HP))
    for tt in range(NTT):
        x_ = xt.next()
        D(lambda e, x_=x_, tt=tt: e.dma_start(out=x_[:], in_=xT[:, HP + tt * TT:HP + (tt + 1) * TT].rearrange("(kc p) t -> p kc t", p=128)), writes=[x_.name])
        rms(x_, TT, (HP + tt * TT, HP + (tt + 1) * TT))

    def proj(ps, w, c0, m, col0, n):
        def f(e):
            ins = None
            for kc in range(8):
                ins = e.matmul(ps[0:m, 0:n], w[:, kc, c0:c0 + m], hT[:, kc, col0:col0 + n], start=(kc == 0), stop=(kc == 7))
            return ins
        p.op("pe", f, reads=[w.name, "hT"], writes=[ps.name])

    def load_group(g):
        w = ws.next()
        D(lambda e: e.dma_start(out=w[:], in_=w1[:, 512 * g:512 * (g + 1)].rearrange("(kc p) m -> p kc m", p=128)), writes=[w.name], eng="pool")
        return w

    def out_b(row0, tt, src, m=128):
        D(lambda e: e.dma_start(out=ob[row0:row0 + m, tt * TT:(tt + 1) * TT], in_=src[0:m, :]), reads=[src.name])

    def out_f(row0, tt, src, p0, m):
        D(lambda e: e.dma_start(out=of[row0:row0 + m, tt * TT:(tt + 1) * TT], in_=src[p0:p0 + m, :]), reads=[src.name])

    def qknorm(ps, gcol, extra_bias, row0, tt):
        s = sq.next()
        p.op("act", lambda e: e.activation(out=s[:], in_=ps[:], func=AF.Square), reads=[ps.name], writes=[s.name])
        ps2 = psB.next()
        p.op("pe", lambda e: e.matmul(ps2[:], bd[:], s[:], start=True, stop=True), reads=["bd", s.name], writes=[ps2.name])
        r = rs.next()
        p.op("act", lambda e: e.activation(out=r[:], in_=ps2[:], func=AF.Ln, scale=1.0 / 64, bias=epsb[:, 0:1]), reads=[ps2.name, "epsb"], writes=[r.name])
        p.op("act", lambda e: e.activation(out=r[:], in_=r[:], func=AF.Exp, scale=-0.5, bias=extra_bias), reads=[r.name], writes=[r.name])
        o = obuf.next()
        p.op("dve", lambda e: e.scalar_tensor_tensor(out=o[:], in0=ps[:], scalar=ppt[:, gcol:gcol + 1], in1=r[:], op0=ALU.mult, op1=ALU.mult), reads=[ps.name, "ppt", r.name], writes=[o.name])
        out_b(row0, tt, o)

    def copy_out(ps, scale, row0, tt):
        o = obuf.next()
        p.op("act", lambda e: e.activation(out=o[:], in_=ps[:], func=AF.Copy, scale=scale), reads=[ps.name], writes=[o.name])
        out_b(row0, tt, o)

    def conv_chunk(w, ci, cidx, T0, wd=None, j=None, tt=0, row0=0):
        psM = psA.next()
        proj(psM, w, ci * 128, 128, T0, TT)
        proj(psH, w, ci * 128, 128, T0 - 3, 3)
        pr = pre.next()
        p.op("act", lambda e: e.activation(out=pr[:, 0:3], in_=psH[:, 0:3], func=AF.Copy), reads=["psH"], writes=[(pr.name, 0)])
        p.op("act", lambda e: e.activation(out=pr[:, 3:TT + 3], in_=psM[:], func=AF.Copy), reads=[psM.name], writes=[(pr.name, 1)])
        a = acc.next()
        p.op("dve", lambda e: e.tensor_scalar(out=a[:], in0=pr[:, 0:TT], scalar1=ppt[:, 13 + 4 * cidx:14 + 4 * cidx], scalar2=ppt[:, 37 + cidx:38 + cidx], op0=ALU.mult, op1=ALU.add), reads=[pr.name, "ppt"], writes=[a.name])
        for jj in range(1, 4):
            p.op("dve", lambda e, jj=jj: e.scalar_tensor_tensor(out=a[:], in0=pr[:, jj:jj + TT], scalar=ppt[:, 13 + 4 * cidx + jj:14 + 4 * cidx + jj], in1=a[:], op0=ALU.mult, op1=ALU.add), reads=[pr.name, "ppt", a.name], writes=[a.name])
        s = sil.next()
        p.op("act", lambda e: e.activation(out=s[:], in_=a[:], func=AF.Silu), reads=[a.name], writes=[s.name])
        o = obuf.next()
        p.op("act", lambda e: e.activation(out=o[:], in_=s[:], func=AF.Copy), reads=[s.name], writes=[o.name])
        out_b(row0, tt, o)
        if wd is not None:
            psD = psB.next()
            proj(psD, wd, j * 128, 128, T0, TT)
            d = dtr.next()
            p.op("act", lambda e: e.activation(out=d[:], in_=psD[:], func=AF.Exp, bias=ppt[:, 5 + j:6 + j]), reads=[psD.name, "ppt"], writes=[d.name])
            p.op("act", lambda e: e.activation(out=d[:], in_=d[:], func=AF.Ln, bias=1.0), reads=[d.name], writes=[d.name])
            o2 = obuf.next()
            p.op("dve", lambda e: e.tensor_tensor(out=o2[:], in0=s[:], in1=d[:], op=ALU.mult), reads=[s.name, d.name], writes=[o2.name])
            out_b(2816 + 128 * j, tt, o2)
            f = fbuf.next()
            p.op("dve", lambda e: e.tensor_scalar(out=f[:], in0=d[:], scalar1=na[:, j:j + 1], scalar2=None, op0=ALU.mult), reads=[d.name, "na"], writes=[f.name])
            f2 = fsc.next()
            for q4 in range(4):
                p.op("dve", lambda e, q4=q4: e.tensor_tensor_scan(out=f2[:, 128 * q4:128 * (q4 + 1)], data0=onesf[:, 0:128], data1=f[:, 128 * q4:128 * (q4 + 1)], initial=0.0, op0=ALU.mult, op1=ALU.add), reads=[f.name, "onesf"], writes=[(f2.name, q4)])
            out_f(260 + 128 * j, tt, f2, 0, 128)

    LN8 = float(np.log(0.125))
    T0s = [HP + tt * TT for tt in range(NTT)]
    w = load_group(0)
    for tt in range(NTT):
        for ci in range(4):
            ps = psA.next()
            proj(ps, w, ci * 128, 128, T0s[tt], TT)
            if ci < 2:
                qknorm(ps, 0, LN8, 0 + 128 * ci, tt)
            else:
                qknorm(ps, 1, 0.0, 256 + 128 * (ci - 2), tt)
    w = load_group(1)
    for tt in range(NTT):
        for ci in range(4):
            ps = psA.next()
            proj(ps, w, ci * 128, 128, T0s[tt], TT)
            copy_out(ps, 1.0 if ci < 2 else 0.125, (512 + 128 * ci) if ci < 2 else (768 + 128 * (ci - 2)), tt)
    w = load_group(2)
    for tt in range(NTT):
        for ci in range(4):
            ps = psA.next()
            proj(ps, w, ci * 128, 128, T0s[tt], TT)
            copy_out(ps, 1.0, 1024 + 128 * ci, tt)
    wa = load_group(3)
    wb = load_group(4)
    for tt in range(NTT):
        for ci in range(4):
            pu = psA.next()
            proj(pu, wa, ci * 128, 128, T0s[tt], TT)
            pw = psB.next()
            proj(pw, wb, ci * 128, 128, T0s[tt], TT)
            a_, b_ = t1.next(), t2.next()
            p.op("dve", lambda e, pu=pu, a_=a_, tt=tt: e.tensor_tensor(out=a_[:], in0=pu[:], in1=cos2[:, tt * TT:(tt + 1) * TT], op=ALU.mult), reads=[pu.name, "cos2"], writes=[a_.name])
            p.op("dve", lambda e, pw=pw, b_=b_, tt=tt: e.tensor_tensor(out=b_[:], in0=pw[:], in1=sin2[:, tt * TT:(tt + 1) * TT], op=ALU.mult), reads=[pw.name, "sin2"], writes=[b_.name])
            o = obuf.next()
            p.op("pool", lambda e, o=o, a_=a_, b_=b_: e.tensor_tensor(out=o[:], in0=a_[:], in1=b_[:], op=ALU.add), reads=[a_.name, b_.name], writes=[o.name])
            out_b(1536 + 128 * ci, tt, o)
    w = load_group(5)
    for tt in range(NTT):
        for ci in range(2):
            ps = psA.next()
            proj(ps, w, ci * 128, 128, T0s[tt], TT)
            copy_out(ps, 1.0, 2048 + 128 * ci, tt)
        conv_chunk(w, 2, 4, T0s[tt], tt=tt, row0=3328)
        conv_chunk(w, 3, 5, T0s[tt], tt=tt, row0=3456)
    wx = load_group(6)
    wd = load_group(7)
    for tt in range(NTT):
        for j in range(4):
            conv_chunk(wx, j, j, T0s[tt], wd=wd, j=j, tt=tt, row0=2304 + 128 * j)
    for tt in range(NTT):
        T0 = T0s[tt]
        ps = psA.next()
        proj(ps, wsm, 0, 4, T0, TT)
        f = fbuf.next()
        p.op("act", lambda e, ps=ps, f=f: e.activation(out=f[0:4, :], in_=ps[0:4, :], func=AF.Exp, scale=-1.0, bias=npp[0:4, 2:3]), reads=[ps.name, "npp"], writes=[f.name])
        p.op("act", lambda e, f=f: e.activation(out=f[0:4, :], in_=f[0:4, :], func=AF.Ln, bias=1.0), reads=[f.name], writes=[f.name])
        p.op("dve", lambda e, f=f: e.tensor_scalar(out=f[0:4, :], in0=f[0:4, :], scalar1=-1.0, scalar2=None, op0=ALU.mult), reads=[f.name], writes=[f.name])
        if tt == 0:
            p.op("dve", lambda e, f=f: e.tensor_tensor_scan(out=crow[0:4, 0:TT], data0=onesf[0:4, :], data1=f[0:4, :], initial=0.0, op0=ALU.mult, op1=ALU.add), reads=[f.name, "onesf"], writes=[("crow", 0)])
        else:
            p.op("dve", lambda e, f=f, tt=tt: e.tensor_tensor_scan(out=crow[0:4, tt * TT:(tt + 1) * TT], data0=onesf[0:4, :], data1=f[0:4, :], initial=crow[0:4, tt * TT - 1:tt * TT], op0=ALU.mult, op1=ALU.add), reads=[f.name, "onesf", ("crow", tt - 1)], writes=[("crow", tt)])
        D(lambda e, tt=tt: e.dma_start(out=of[0:4, tt * TT:(tt + 1) * TT], in_=crow[0:4, tt * TT:(tt + 1) * TT]), reads=[("crow", tt)])
        ps = psA.next()
        proj(ps, wsm, 4, 16, T0, TT)
        g_ = glrT.next()
        p.op("act", lambda e, ps=ps, g_=g_: e.activation(out=g_[:], in_=ps[0:16, :], func=AF.Copy), reads=[ps.name], writes=[g_.name])
        for c2 in range(2):
            ps2 = psB.next()
            p.op("pe", lambda e, ps2=ps2, g_=g_, c2=c2: e.matmul(ps2[:], w2t[0:16, c2 * 128:(c2 + 1) * 128], g_[0:16, :], start=True, stop=True), reads=["w2t", g_.name], writes=[ps2.name])
            f = fbuf.next()
            p.op("act", lambda e, ps2=ps2, f=f, c2=c2: e.activation(out=f[:], in_=ps2[:], func=AF.Exp, scale=-1.0, bias=npp[:, 3 + c2:4 + c2]), reads=[ps2.name, "npp"], writes=[f.name])
            p.op("act", lambda e, f=f: e.activation(out=f[:], in_=f[:], func=AF.Ln, bias=1.0), reads=[f.name], writes=[f.name])
            p.op("dve", lambda e, f=f: e.tensor_scalar(out=f[:], in0=f[:], scalar1=-1.0 / 16.0, scalar2=None, op0=ALU.mult), reads=[f.name], writes=[f.name])
            f2 = fsc.next()
            for q4 in range(4):
                p.op("dve", lambda e, q4=q4, f=f, f2=f2: e.tensor_tensor_scan(out=f2[:, 128 * q4:128 * (q4 + 1)], data0=onesf[:, 0:128], data1=f[:, 128 * q4:128 * (q4 + 1)], initial=0.0, op0=ALU.mult, op1=ALU.add), reads=[f.name, "onesf"], writes=[(f2.name, q4)])
            out_f(4 + 128 * c2, tt, f2, 0, 128)
    run_prog(nc, p)
    return nc


COLS = dict(fq=0, fk=256, fv=512, ff=768, gq=772, gk=1028, gv=1284, glr=1540, gr=1556, rq=1812, rk=2068,
            rv=2324, rg=2580, z=2836, xs=3348, B=3860, C=3988, dt=4116, gates=4124)


def _swap_halves(idx):
    idx = idx.reshape(-1, 2, 32)
    return idx[:, ::-1, :].reshape(-1)


def l1_weight_cols():
    C = COLS
    r = np.arange
    rq = r(C["rq"], C["rq"] + 256)
    rk = r(C["rk"], C["rk"] + 256)
    dtrep = np.repeat(r(C["dt"], C["dt"] + 8), 64)
    cols = np.concatenate([
        r(C["fq"], C["fq"] + 256), r(C["fk"], C["fk"] + 256),
        r(C["fv"], C["fv"] + 256), r(C["gq"], C["gq"] + 256),
        r(C["gk"], C["gk"] + 256), r(C["gv"], C["gv"] + 256),
        rq, rk, _swap_halves(rq), _swap_halves(rk),
        r(C["rv"], C["rv"] + 256), r(C["B"], C["B"] + 128), r(C["C"], C["C"] + 128),
        r(C["xs"], C["xs"] + 512), dtrep,
        r(C["ff"], C["ff"] + 4), r(C["glr"], C["glr"] + 16)])
    return cols


def l1_params(P):
    pp = np.zeros((128, 48), np.float32)
    pp[:, 0] = np.tile(P["fox_qn"], 2)
    pp[:, 1] = np.tile(P["fox_kn"], 2)
    pp[0:4, 2] = P["fox_bf"]
    pp[:, 3] = P["gla_b"][0:128]
    pp[:, 4] = P["gla_b"][128:256]
    for j in range(4):
        pp[:, 5 + j] = np.repeat(P["ssd_dt_bias"][2 * j:2 * j + 2], 64)
        pp[:, 9 + j] = np.repeat(P["ssd_a_log"][2 * j:2 * j + 2], 64)
    for cidx in range(6):
        ch = slice(128 * cidx, 128 * (cidx + 1))
        for jj in range(4):
            pp[:, 13 + 4 * cidx + jj] = P["ssd_conv_w"][jj, ch]
        pp[:, 37 + cidx] = P["ssd_conv_b"][ch]
    return pp


def l1_consts():
    half = 32
    inv = (10000.0 ** (-(np.arange(half, dtype=np.float32)) / np.float32(half))).astype(np.float32)
    cst = np.zeros((128, 4), np.float32)
    d = np.arange(128) % 64
    cst[:, 0] = inv[d % 32]
    s = np.float32(np.sqrt(0.125))
    cst[:, 1] = np.where(d < 32, -s, s)
    cst[:, 2] = s
    return cst

SEQ = 16384
NCH = 128


def build_l2():
    c = Ctx()
    nc, p = c.nc, c.p
    D = p.dma
    fqT = c.dram("fqT", [64, 8192], BF16, "ExternalInput")
    fkT = c.dram("fkT", [64, SEQ], BF16, "ExternalInput")
    fv = c.dram("fv", [128, NCH, 64], BF16, "ExternalInput")
    cq = c.dram("cq", [8, 1024], F32, "ExternalInput")
    id8d = c.dram("id8", [8, 8], F32, "ExternalInput")
    ck = c.dram("ck", [128, NCH], F32, "ExternalInput")
    Ttm = c.dram("Ttm", [128, 8], F32, "ExternalInput")
    mskd = c.dram("msk", [128, 8, 512], BF16, "ExternalInput")
    idbd = c.dram("idb", [128, 128], BF16, "ExternalInput")
    mnegd = c.dram("mneg", [128, 128], F32, "ExternalInput")
    fo = c.dram("fo", [64, 8192], BF16, "ExternalOutput")
    fden = c.dram("fden", [1, 8192], F32, "ExternalOutput")
    gq = c.dram("gq", [2, 64, SEQ], BF16, "ExternalInput")
    gk = c.dram("gk", [2, 64, SEQ], BF16, "ExternalInput")
    gktm = c.dram("gktm", [2, 128, NCH, 64], BF16, "ExternalInput")
    gvtm = c.dram("gvtm", [2, 128, NCH, 64], BF16, "ExternalInput")
    gb = c.dram("gb", [2, 64, SEQ], F32, "ExternalInput")
    gbtm = c.dram("gbtm", [2, 128, NCH, 64], F32, "ExternalInput")
    gblast = c.dram("gblast", [2, 64, NCH], F32, "ExternalInput")
    gc128 = c.dram("gc128", [2, 128, SEQ], F32, "ExternalInput")
    gctm = c.dram("gctm", [2, 128, NCH], F32, "ExternalInput")
    gclast = c.dram("gclast", [2, 128, NCH], F32, "ExternalInput")
    go = c.dram("go", [2, 64, SEQ], F32, "ExternalOutput")

    GT = 1024
    CPG = 8
    NG = SEQ // GT
    mneg = c.sb("mneg_s", [128, 128], F32)
    psx = c.rot("psx", 2, [128, 512], F32, psum=True)
    D(lambda e: e.dma_start(out=mneg[:], in_=mnegd), writes=["mneg_s"])

    shared_tmp = (c.sb("E1", [64, GT], F32), c.sb("E2", [64, GT], F32), c.sb("EC", [64, GT], F32), c.sb("TK", [128, CPG, 64], F32))

    def gen_generic(h2):
        sfx = f"_{h2}"
        q_g = c.sb("q_g" + sfx, [64, GT], BF16)
        k_g = c.sb("k_g" + sfx, [64, GT], BF16)
        ktm_g = c.sb("ktm_g" + sfx, [128, CPG, 64], BF16)
        vtm_r = c.rot("vtm_g" + sfx, 3, [128, CPG, 64], BF16)
        b_g = c.sb("b_g" + sfx, [64, GT], F32)
        btm_g = c.sb("btm_g" + sfx, [128, CPG, 64], F32)
        c_g = c.sb("c_g" + sfx, [128, GT], F32)
        ost = c.rot("ost" + sfx, 1, [64, GT], F32)
        blast = c.sb("blast" + sfx, [64, NCH], F32)
        ctm = c.sb("ctm" + sfx, [128, NCH], F32)
        clast = c.sb("clast" + sfx, [128, NCH], F32)
        cend = c.sb("cend" + sfx, [128, NCH], F32)
        nctm = c.sb("nctm" + sfx, [128, NCH], F32)
        dtot = c.sb("dtot" + sfx, [64, NCH], F32)
        eb = c.sb("eb" + sfx, [64, NCH], F32)
        S = c.sb("S" + sfx, [64, 64], F32)
        Sb = c.sb("Sb" + sfx, [64, 64], BF16)
        E1, E2, EC, TK = shared_tmp
        QDs = [c.sb("QD%d" % b + sfx, [64, GT], BF16) for b in range(2)]
        KIs = [c.sb("KI%d" % b + sfx, [64, GT], BF16) for b in range(2)]
        QTs = [c.sb("QTg%d" % b + sfx, [64, GT], BF16) for b in range(2)]
        TDs = [c.sb("TD%d" % b + sfx, [128, GT], F32) for b in range(2)]
        KIEs = [c.sb("KIE%d" % b + sfx, [128, CPG, 64], BF16) for b in range(2)]
        AD = c.rot("AD" + sfx, 2, [128, 128], BF16)
        t3 = c.rot("t3" + sfx, 2, [64, 64], F32)
        pso = c.ps("pso" + sfx, [128, 512])
        n = lambda t: t.name

        D(lambda e: e.dma_start(out=blast[:], in_=gblast[h2]), writes=[n(blast)])
        D(lambda e: e.dma_start(out=ctm[:], in_=gctm[h2]), writes=[n(ctm)])
        D(lambda e: e.dma_start(out=clast[:], in_=gclast[h2]), writes=[n(clast)])
        p.op("dve", lambda e: e.tensor_tensor(out=cend[:], in0=clast[:], in1=ctm[:], op=ALU.subtract), reads=[n(clast), n(ctm)], writes=[n(cend)])
        p.op("dve", lambda e: e.tensor_scalar(out=nctm[:], in0=ctm[:], scalar1=-1.0, scalar2=None, op0=ALU.mult), reads=[n(ctm)], writes=[n(nctm)])
        p.op("dve", lambda e: e.tensor_tensor(out=dtot[:], in0=blast[:], in1=clast[0:64, :], op=ALU.add), reads=[n(blast), n(clast)], writes=[n(dtot)])
        p.op("act", lambda e: e.activation(out=dtot[:], in_=dtot[:], func=AF.Exp), reads=[n(dtot)], writes=[n(dtot)])
        p.op("act", lambda e: e.activation(out=eb[:], in_=blast[:], func=AF.Exp), reads=[n(blast)], writes=[n(eb)])
        p.op("pool", lambda e: e.memset(S[:], 0.0), writes=[n(S)])
        p.op("pool", lambda e: e.memset(Sb[:], 0.0), writes=[n(Sb)])
        vts = {}

        def load_group(G):
            cs0 = G * GT
            vt = vtm_r.next()
            vts[G] = vt
            D(lambda e: e.dma_start(out=q_g[:], in_=gq[h2, :, cs0:cs0 + GT]), writes=[n(q_g)])
            D(lambda e: e.dma_start(out=k_g[:], in_=gk[h2, :, cs0:cs0 + GT]), writes=[n(k_g)])
            D(lambda e: e.dma_start(out=ktm_g[:], in_=gktm[h2, :, CPG * G:CPG * G + CPG, :]), writes=[n(ktm_g)])
            D(lambda e: e.dma_start(out=vt[:], in_=gvtm[h2, :, CPG * G:CPG * G + CPG, :]), writes=[n(vt)])
            D(lambda e: e.dma_start(out=b_g[:], in_=gb[h2, :, cs0:cs0 + GT]), writes=[n(b_g)])
            D(lambda e: e.dma_start(out=btm_g[:], in_=gbtm[h2, :, CPG * G:CPG * G + CPG, :]), writes=[n(btm_g)])
            D(lambda e: e.dma_start(out=c_g[:], in_=gc128[h2, :, cs0:cs0 + GT]), writes=[n(c_g)])

        def prologue(G):
            b = G % 2
            QD, KI, QTt, TD, KIE = QDs[b], KIs[b], QTs[b], TDs[b], KIEs[b]
            chs = slice(CPG * G, CPG * G + CPG)
            v3 = lambda t: t[:].rearrange("p (c t) -> p c t", c=CPG)
            return [
                lambda: p.op("act", lambda e: e.activation(out=E1[:], in_=b_g[:], func=AF.Exp), reads=[n(b_g)], writes=[n(E1)]),
                lambda: p.op("act", lambda e: e.activation(out=E2[:], in_=b_g[:], func=AF.Exp, scale=-1.0), reads=[n(b_g)], writes=[n(E2)]),
                lambda: p.op("act", lambda e: e.activation(out=EC[:], in_=c_g[0:64, :], func=AF.Exp), reads=[n(c_g)], writes=[n(EC)]),
                lambda: p.op("dve", lambda e: e.tensor_tensor(out=QD[:], in0=q_g[:], in1=E1[:], op=ALU.mult), reads=[n(q_g), n(E1)], writes=[n(QD)]),
                lambda: p.op("dve", lambda e: e.tensor_tensor(out=KI[:], in0=k_g[:], in1=E2[:], op=ALU.mult), reads=[n(k_g), n(E2)], writes=[n(KI)]),
                lambda: p.op("dve", lambda e: e.tensor_tensor(out=E1[:], in0=E1[:], in1=EC[:], op=ALU.mult), reads=[n(E1), n(EC)], writes=[n(E1)]),
                lambda: p.op("dve", lambda e: e.tensor_tensor(out=QTt[:], in0=q_g[:], in1=E1[:], op=ALU.mult), reads=[n(q_g), n(E1)], writes=[n(QTt)]),
                lambda: p.op("dve", lambda e: e.tensor_tensor(out=v3(TD), in0=v3(c_g), in1=nctm[:, chs].unsqueeze(2).to_broadcast([128, CPG, 128]), op=ALU.add), reads=[n(c_g), n(nctm)], writes=[n(TD)]),
                lambda: p.op("pool", lambda e: e.tensor_tensor(out=v3(TD), in0=v3(TD), in1=mneg[:].unsqueeze(1).to_broadcast([128, CPG, 128]), op=ALU.add), reads=[n(TD), "mneg_s"], writes=[n(TD)]),
                lambda: p.op("act", lambda e: e.activation(out=TD[:], in_=TD[:], func=AF.Exp), reads=[n(TD)], writes=[n(TD)]),
                lambda: p.op("dve", lambda e: e.tensor_tensor(out=TK[:], in0=cend[:, chs].unsqueeze(2).to_broadcast([128, CPG, 64]), in1=btm_g[:], op=ALU.subtract), reads=[n(cend), n(btm_g)], writes=[n(TK)]),
                lambda: p.op("act", lambda e: e.activation(out=TK[:], in_=TK[:], func=AF.Exp), reads=[n(TK)], writes=[n(TK)]),
                lambda: p.op("dve", lambda e: e.tensor_tensor(out=KIE[:], in0=ktm_g[:], in1=TK[:], op=ALU.mult), reads=[n(ktm_g), n(TK)], writes=[n(KIE)]),
            ]

        load_group(0)
        for th in prologue(0):
            th()
        load_group(1)
        for G in range(NG):
            b = G % 2
            QD, KI, QTt, TD, KIE = QDs[b], KIs[b], QTs[b], TDs[b], KIEs[b]
            vt = vts.pop(G)
            cs0 = G * GT
            nxt = prologue(G + 1) if G + 1 < NG else []
            og = ost.next()
            for j in range(CPG):
                jj = CPG * G + j
                sl = slice(128 * j, 128 * (j + 1))
                oc = slice(128 * (j % 4), 128 * (j % 4 + 1))
                ps = psx.next()
                ad = AD.next()
                t3_ = t3.next()
                p.op("pe", lambda e, ps=ps, sl=sl, KI=KI, QD=QD: e.matmul(ps[:, 0:128], KI[:, sl], QD[:, sl], start=True, stop=True), reads=[n(KI), n(QD)], writes=[(ps.name, 0)])
                p.op("dve", lambda e, ad=ad, ps=ps, sl=sl, TD=TD: e.tensor_tensor(out=ad[:], in0=ps[:, 0:128], in1=TD[:, sl], op=ALU.mult), reads=[(ps.name, 0), n(TD)], writes=[ad.name])

                def mo(e, vt=vt, j=j, ad=ad, sl=sl, oc=oc, QTt=QTt):
                    e.matmul(pso[0:64, oc], vt[:, j, :], ad[:], start=True, stop=False)
                    return e.matmul(pso[0:64, oc], Sb[:], QTt[:, sl], start=False, stop=True)
                p.op("pe", mo, reads=[vt.name, ad.name, n(Sb), n(QTt)], writes=[(n(pso), j % 4)])
                p.op("pe", lambda e, ps=ps, vt=vt, j=j, KIE=KIE: e.matmul(ps[0:64, 256:320], KIE[:, j, :], vt[:, j, :], start=True, stop=True), reads=[n(KIE), vt.name], writes=[(ps.name, 2)])
                p.op("dve", lambda e, t3_=t3_, ps=ps, jj=jj: e.tensor_scalar(out=t3_[:], in0=ps[0:64, 256:320], scalar1=eb[:, jj:jj + 1], scalar2=None, op0=ALU.mult), reads=[(ps.name, 2), n(eb)], writes=[t3_.name])
                p.op("dve", lambda e, t3_=t3_, jj=jj: e.scalar_tensor_tensor(out=S[:], in0=S[:], scalar=dtot[:, jj:jj + 1], in1=t3_[:], op0=ALU.mult, op1=ALU.add), reads=[n(S), n(dtot), t3_.name], writes=[n(S)])
                p.op("dve", lambda e: e.tensor_copy(out=Sb[:], in_=S[:]), reads=[n(S)], writes=[n(Sb)])
                if j % 4 == 3:
                    q4 = j // 4
                    p.op("dve", lambda e, og=og, q4=q4: e.tensor_copy(out=og[:, 512 * q4:512 * (q4 + 1)], in_=pso[0:64, :]), reads=[n(pso)], writes=[(og.name, q4)])
                if nxt and j in (0, 2, 4):
                    k_ = {0: 7, 2: 3, 4: 3}[j]
                    for _ in range(k_):
                        nxt.pop(0)()
                yield
            while nxt:
                nxt.pop(0)()
            if G + 2 < NG:
                load_group(G + 2)
            D(lambda e, og=og, cs0=cs0: e.dma_start(out=go[h2, :, cs0:cs0 + GT], in_=og[:]), reads=[og.name])

    def gen_fox():
        QT = c.sb("QT", [67, 8192], BF16)
        KT = c.sb("KT", [67, SEQ], BF16)
        V = c.sb("V", [128, NCH, 65], BF16)
        msk = c.sb("msk_s", [128, 8, 512], BF16)
        idb = c.sb("idb_s", [128, 128], BF16)
        cqr = c.sb("cqr", [8, 1024], F32)
        r1 = c.sb("r1", [8, 1024], F32)
        hi = c.sb("hi", [8, 1024], BF16)
        mid = c.sb("mid", [8, 1024], BF16)
        lo = c.sb("lo", [8, 1024], BF16)
        id8 = c.sb("id8_s", [8, 8], F32)
        od8 = c.sb("od8", [8, 8], F32)
        offd = c.sb("offd", [8, 1], F32)
        negc = c.sb("negc", [128, NCH], F32)
        Tt = c.sb("Tt", [128, 8], F32)
        offk = c.sb("offk", [128, 8], F32)
        pt = c.rot("pt", 3, [128, 512], BF16)
        oun = c.rot("oun", 2, [64, 512], BF16)
        den = c.rot("den", 1, [65, 512], F32)
        ps_s = c.rot("ps_s", 3, [128, 512], F32, psum=True)
        ps_o = c.ps("ps_o")

        D(lambda e: e.dma_start(out=QT[0:64, :], in_=fqT), writes=[("QT", 0)])
        D(lambda e: e.dma_start(out=KT[0:64, :], in_=fkT), writes=[("KT", 0)])
        D(lambda e: e.dma_start(out=V[:, :, 0:64], in_=fv), writes=[("V", 0)])
        D(lambda e: e.dma_start(out=msk[:], in_=mskd), writes=["msk_s"])
        D(lambda e: e.dma_start(out=idb[:], in_=idbd), writes=["idb_s"])
        D(lambda e: e.dma_start(out=cqr[:], in_=cq), writes=["cqr"])
        D(lambda e: e.dma_start(out=id8[:], in_=id8d), writes=["id8_s"])
        D(lambda e: e.dma_start(out=negc[:], in_=ck), writes=["negc"])
        D(lambda e: e.dma_start(out=Tt[:], in_=Ttm), writes=["Tt"])
        p.op("pool", lambda e: e.memset(V[:, :, 64:65], 1.0), writes=[("V", 1)])
        p.op("pool", lambda e: e.memset(KT[64:67, :], 1.0), writes=[("KT", 1)])
        p.op("pool", lambda e: e.memset(offk[:], 0.0), writes=["offk"])
        for j in range(1, 8):
            p.op("dve", lambda e, j=j: e.tensor_tensor(out=offk[:, j:j + 1], in0=offk[:, j - 1:j], in1=Tt[:, j - 1:j], op=ALU.add), reads=["offk", "Tt"], writes=["offk"])
        for j in range(1, 8):
            p.op("dve", lambda e, j=j: e.tensor_scalar(out=negc[:, 16 * j:16 * j + 16], in0=negc[:, 16 * j:16 * j + 16], scalar1=offk[:, j:j + 1], scalar2=None, op0=ALU.add), reads=["negc", "offk"], writes=["negc"])
        p.op("dve", lambda e: e.tensor_scalar(out=negc[:], in0=negc[:], scalar1=-1.0, scalar2=None, op0=ALU.mult), reads=["negc"], writes=["negc"])
        p.op("dve", lambda e: e.tensor_tensor(out=od8[:], in0=offk[0:8, 0:8], in1=id8[:], op=ALU.mult), reads=["offk", "id8_s"], writes=["od8"])
        p.op("dve", lambda e: e.reduce_sum(out=offd[:], in_=od8[:], axis=AX.X), reads=["od8"], writes=["offd"])
        p.op("dve", lambda e: e.tensor_scalar(out=cqr[:], in0=cqr[:], scalar1=offd[:, 0:1], scalar2=None, op0=ALU.add), reads=["cqr", "offd"], writes=["cqr"])
        p.op("dve", lambda e: e.tensor_copy(out=hi[:], in_=cqr[:]), reads=["cqr"], writes=["hi"])
        p.op("dve", lambda e: e.tensor_tensor(out=r1[:], in0=cqr[:], in1=hi[:], op=ALU.subtract), reads=["cqr", "hi"], writes=["r1"])
        p.op("dve", lambda e: e.tensor_copy(out=mid[:], in_=r1[:]), reads=["r1"], writes=["mid"])
        p.op("dve", lambda e: e.tensor_tensor(out=r1[:], in0=r1[:], in1=mid[:], op=ALU.subtract), reads=["r1", "mid"], writes=["r1"])
        p.op("dve", lambda e: e.tensor_copy(out=lo[:], in_=r1[:]), reads=["r1"], writes=["lo"])
        for j in range(8):
            D(lambda e, j=j: e.dma_start(out=QT[64:65, 1024 * j:1024 * (j + 1)], in_=hi[j:j + 1, :]), reads=["hi"], writes=[("QT", 10 + j)])
            D(lambda e, j=j: e.dma_start(out=QT[65:66, 1024 * j:1024 * (j + 1)], in_=mid[j:j + 1, :]), reads=["mid"], writes=[("QT", 20 + j)])
            D(lambda e, j=j: e.dma_start(out=QT[66:67, 1024 * j:1024 * (j + 1)], in_=lo[j:j + 1, :]), reads=["lo"], writes=[("QT", 30 + j)])
        yield
        for i in range(16):
            nkb = 8 * i + 8
            qs = slice(512 * i, 512 * (i + 1))
            pend = {}

            def mmS(kb, i=i, qs=qs, pend=pend):
                ps = ps_s.next()
                pend[kb] = ps

                def f(e, ps=ps, kb=kb):
                    ins = e.matmul(ps[:], KT[0:67, 128 * kb:128 * (kb + 1)], QT[0:67, qs], start=True, stop=(kb < 8 * i))
                    if kb >= 8 * i:
                        ins = e.matmul(ps[:], idb[:], msk[:, kb - 8 * i, :], start=False, stop=True)
                    return ins
                p.op("pe", f, reads=["KT", "QT", "idb_s", "msk_s"], writes=[ps.name])

            mmS(0)
            mmS(1)
            for kb in range(nkb):
                if kb + 2 < nkb:
                    mmS(kb + 2)
                ps = pend.pop(kb)
                t_ = pt.next()
                p.op("act", lambda e, t_=t_, ps=ps, kb=kb: e.activation(out=t_[:], in_=ps[:], func=AF.Exp, bias=negc[:, kb:kb + 1]), reads=[ps.name, "negc"], writes=[t_.name])
                p.op("pe", lambda e, t_=t_, kb=kb, nkb=nkb: e.matmul(ps_o[0:65, :], V[:, kb, :], t_[:], start=(kb == 0), stop=(kb == nkb - 1)), reads=["V", t_.name], writes=["ps_o"])
                if kb == nkb - 1:
                    o_, d_ = oun.next(), den.next()
                    p.op("act", lambda e, o_=o_: e.activation(out=o_[:], in_=ps_o[0:64, :], func=AF.Copy), reads=["ps_o"], writes=[o_.name])
                    p.op("act", lambda e, d_=d_: e.activation(out=d_[64:65, :], in_=ps_o[64:65, :], func=AF.Copy), reads=["ps_o"], writes=[d_.name])
                    D(lambda e, o_=o_, qs=qs: e.dma_start(out=fo[:, qs], in_=o_[:]), reads=[o_.name])
                    D(lambda e, d_=d_, qs=qs: e.dma_start(out=fden[0:1, qs], in_=d_[64:65, :]), reads=[d_.name])
                yield

    gf, g0, g1 = gen_fox(), gen_generic(0), gen_generic(1)
    next(gf)
    alive = [True, True, True]
    while any(alive):
        for gi, g in ((0, g0), (1, g1)):
            if alive[gi]:
                try:
                    next(g)
                except StopIteration:
                    alive[gi] = False
        for _ in range(9):
            if alive[2]:
                try:
                    next(gf)
                except StopIteration:
                    alive[2] = False
    run_prog(nc, p)
    return nc


_NC = {}


def _get(name, fn):
    if name not in _NC:
        _NC[name] = fn()
    return _NC[name]


def run_l1(xfull, P, pos):
    nc = _get("l1", build_l1)
    w1 = np.ascontiguousarray(P["w_in"][:, l1_weight_cols()])
    ln1 = np.ascontiguousarray(P["ln1"].reshape(8, 128).T)
    pp = l1_params(P)
    cst = l1_consts()
    maps = []
    for c in range(8):
        t0 = c * NT
        xs = np.zeros((1024, HP + NT), np.float32)
        xs[:, HP:] = xfull[t0:t0 + NT].T
        if c > 0:
            xs[:, 1:HP] = xfull[t0 - 3:t0].T
        maps.append(dict(xT=xs, w1=w1, ln1=ln1, pp=pp, w2=np.ascontiguousarray(P["gla_w2"]),
                         pos=np.ascontiguousarray(np.broadcast_to(pos[t0:t0 + NT], (128, NT))).astype(np.int32), cst=cst))
    res = run_bass_kernel_spmd(nc, maps, core_ids=list(range(8))).results
    OB = np.concatenate([r["ob"] for r in res], axis=1)
    OF = np.concatenate([r["of"] for r in res], axis=1)
    return OB, OF


def _tm(a):
    R = a.shape[0]
    return np.ascontiguousarray(a.T.reshape(NCH, 128, R).transpose(1, 0, 2))


def run_l2(OB, OF):
    nc = _get("l2", build_l2)
    bf = OB.dtype
    idb = np.eye(128, dtype=np.float32).astype(bf)
    s_ = np.arange(128)[:, None]
    mneg = np.where(np.arange(128)[None, :] >= s_, 0.0, -1.0e4).astype(np.float32)
    id8 = np.eye(8, dtype=np.float32)
    lg = np.log(1.0 - 2.0 ** (-5.0 - np.arange(4, dtype=np.float32))).astype(np.float32)
    zeros_b = np.zeros((64, SEQ), np.float32)
    heads = []
    for h in range(4):
        heads.append(dict(q=OB[768 + 64 * h:832 + 64 * h], k=OB[1024 + 64 * h:1088 + 64 * h], v=OB[1280 + 64 * h:1344 + 64 * h],
                          b=OF[4 + 64 * h:68 + 64 * h], c=np.zeros(SEQ, np.float32)))
    for h in range(4):
        cc = np.tile((np.arange(128, dtype=np.float32) + 1.0) * lg[h], NCH).astype(np.float32)
        heads.append(dict(q=OB[1536 + 64 * h:1600 + 64 * h], k=OB[1792 + 64 * h:1856 + 64 * h], v=OB[2048 + 64 * h:2112 + 64 * h], b=zeros_b, c=cc))
    for h in range(8):
        g = h // 4
        heads.append(dict(q=OB[3456 + 64 * g:3520 + 64 * g], k=OB[3328 + 64 * g:3392 + 64 * g], v=OB[2816 + 64 * h:2880 + 64 * h], b=zeros_b, c=OF[260 + 64 * h]))
    maps = []
    for c in range(8):
        hf, par = c // 2, c % 2
        m = {}
        m["fqT"] = np.ascontiguousarray(OB[64 * hf:64 * hf + 64].reshape(64, 32, 512)[:, par::2].reshape(64, 8192))
        m["fkT"] = np.ascontiguousarray(OB[256 + 64 * hf:320 + 64 * hf])
        m["fv"] = _tm(OB[512 + 64 * hf:576 + 64 * hf])
        crow = OF[hf]
        m["cq"] = np.ascontiguousarray(crow.reshape(32, 512)[par::2].reshape(8, 1024))
        m["ck"] = np.ascontiguousarray(crow.reshape(NCH, 128).T)
        T = crow[NT - 1::NT]
        m["Ttm"] = np.ascontiguousarray(np.broadcast_to(T, (128, 8))).astype(np.float32)
        q_ = np.arange(512)[None, None, :]
        jb = np.arange(8)[None, :, None]
        ss = np.arange(128)[:, None, None]
        m["msk"] = np.where(128 * jb + ss <= 512 * par + q_, 0.0, -30000.0).astype(np.float32).astype(bf)
        m["idb"] = idb
        m["mneg"] = mneg
        m["id8"] = id8
        hs = [heads[2 * c], heads[2 * c + 1]]
        m["gq"] = np.stack([h["q"] for h in hs])
        m["gk"] = np.stack([h["k"] for h in hs])
        m["gktm"] = np.stack([_tm(h["k"]) for h in hs])
        m["gvtm"] = np.stack([_tm(h["v"]) for h in hs])
        m["gb"] = np.stack([h["b"] for h in hs]).astype(np.float32)
        m["gbtm"] = np.stack([_tm(h["b"]) for h in hs]).astype(np.float32)
        m["gblast"] = np.stack([np.ascontiguousarray(h["b"][:, 127::128]) for h in hs]).astype(np.float32)
        m["gc128"] = np.stack([np.ascontiguousarray(np.broadcast_to(h["c"], (128, SEQ))) for h in hs]).astype(np.float32)
        m["gctm"] = np.stack([np.ascontiguousarray(h["c"].reshape(NCH, 128).T) for h in hs]).astype(np.float32)
        m["gclast"] = np.stack([np.ascontiguousarray(np.broadcast_to(h["c"][127::128], (128, NCH))) for h in hs]).astype(np.float32)
        maps.append(m)
    res = run_bass_kernel_spmd(nc, maps, core_ids=list(range(8))).results
    FO = np.zeros((256, SEQ), dtype=bf)
    FD = np.zeros((4, SEQ), np.float32)
    GO = np.zeros((16, 64, SEQ), np.float32)
    for c in range(8):
        hf, par = c // 2, c % 2
        FO[64 * hf:64 * hf + 64].reshape(64, 32, 512)[:, par::2] = res[c]["fo"].reshape(64, 16, 512)
        FD[hf].reshape(32, 512)[par::2] = res[c]["fden"].reshape(16, 512)
        GO[2 * c:2 * c + 2] = res[c]["go"]
    return FO, GO, FD


def build_l3():
    c = Ctx()
    nc, p = c.nc, c.p
    D = p.dma
    xT = c.dram("xT", [1024, NT], F32, "ExternalInput")
    w3 = c.dram("w3", [1024, 5120], F32, "ExternalInput")
    wup = c.dram("wup", [1280, 1024], F32, "ExternalInput")
    wout = c.dram("wout", [1024, 1024], F32, "ExternalInput")
    wfi = c.dram("wfi", [1024, 5632], F32, "ExternalInput")
    wfo = c.dram("wfo", [2816, 1024], F32, "ExternalInput")
    ln1 = c.dram("ln1", [128, 8], F32, "ExternalInput")
    ln2 = c.dram("ln2", [128, 8], F32, "ExternalInput")
    pp = c.dram("pp", [128, 16], F32, "ExternalInput")
    ya = c.dram("ya", [256, NT], BF16, "ExternalInput")
    yden = c.dram("yden", [256, NT], F32, "ExternalInput")
    goT = c.dram("goT", [1024, NT], F32, "ExternalInput")
    xsT = c.dram("xsT", [512, NT], BF16, "ExternalInput")
    xo = c.dram("xo", [1024, NT], F32, "ExternalOutput")

    XR = c.sb("XR", [128, 8, NT], F32)
    HT = c.sb("HT", [128, 8, NT], BF16)
    Y = c.sb("Y", [128, 10, 2, TT], BF16)
    mg = c.sb("mg", [128, 8, 2, TT], BF16)
    wun = c.rot("wun", 2, [128, 10, 128], BF16)
    woutt = c.sb("woutt", [128, 8, 1024], BF16)
    ws = c.rot("ws", 3, [128, 8, 512], BF16)
    sq = c.rot("sq", 2, [128, TT], BF16)
    rs = c.rot("rs", 2, [128, TT], F32)
    og = c.rot("og", 1, [128, TT], F32)
    yun = c.sb("yun", [128, 2, TT], BF16)
    ydn = c.sb("ydn", [128, 2, TT], F32)
    xsb = c.rot("xsb", 1, [128, TT], BF16)
    sg = c.rot("sg", 2, [128, TT], F32)
    ta = c.rot("ta", 1, [128, TT], F32)
    u2 = c.rot("u2_", 4, [128, TT], BF16)
    mt = c.rot("mt", 1, [128, TT], F32)
    ones = c.sb("ones", [128, 128], BF16)
    bd = c.sb("bd", [128, 128], BF16)
    lnw1 = c.sb("lnw1", [128, 8], F32)
    lnw2 = c.sb("lnw2", [128, 8], F32)
    ppt = c.sb("ppt", [128, 16], F32)
    epsb = c.sb("epsb", [128, 1], F32)
    psA = c.rot("psA", 4, [128, TT], F32, psum=True)
    psB = c.rot("psB", 3, [128, TT], F32, psum=True)

    D(lambda e: e.dma_start(out=lnw1[:], in_=ln1), writes=["lnw1"])
    D(lambda e: e.dma_start(out=lnw2[:], in_=ln2), writes=["lnw2"])
    D(lambda e: e.dma_start(out=ppt[:], in_=pp), writes=["ppt"])
    D(lambda e: e.dma_start(out=woutt[:], in_=wout.rearrange("(kc p) m -> p kc m", p=128)), writes=["woutt"], eng="pool")
    p.op("pool", lambda e: e.memset(ones[:], 1.0), writes=["ones"])
    p.op("pool", lambda e: e.memset(bd[:], 0.0), writes=["bd"])
    p.op("pool", lambda e: e.memset(bd[0:64, 0:64], 1.0), reads=["bd"], writes=["bd"])
    p.op("pool", lambda e: e.memset(bd[64:128, 64:128], 1.0), reads=["bd"], writes=["bd"])
    p.op("pool", lambda e: e.memset(epsb[:], EPS), writes=["epsb"])

    def rms(tt, lnw):
        ts_ = slice(tt * TT, (tt + 1) * TT)
        ps = psA.next()
        for kc in range(8):
            s = sq.next()
            p.op("act", lambda e, s=s, kc=kc: e.activation(out=s[:], in_=XR[:, kc, ts_], func=AF.Square), reads=[("XR", tt)], writes=[s.name])
            p.op("pe", lambda e, s=s, kc=kc: e.matmul(ps[:], ones[:], s[:], start=(kc == 0), stop=(kc == 7)), reads=[s.name, "ones"], writes=[ps.name])
        r = rs.next()
        p.op("act", lambda e: e.activation(out=r[:], in_=ps[:], func=AF.Ln, scale=1.0 / 1024, bias=epsb[:, 0:1]), reads=[ps.name, "epsb"], writes=[r.name])
        p.op("act", lambda e: e.activation(out=r[:], in_=r[:], func=AF.Exp, scale=-0.5), reads=[r.name], writes=[r.name])
        for kc in range(8):
            p.op("dve", lambda e, kc=kc: e.scalar_tensor_tensor(out=HT[:, kc, ts_], in0=XR[:, kc, ts_], scalar=lnw[:, kc:kc + 1], in1=r[:], op0=ALU.mult, op1=ALU.mult),
                 reads=[("XR", tt), lnw.name, r.name], writes=[("HT", tt)])

    def proj(ps, w, c0, tt, m=128):
        ts_ = slice(tt * TT, (tt + 1) * TT)

        def f(e):
            ins = None
            for kc in range(8):
                ins = e.matmul(ps[0:m, :], w[:, kc, c0:c0 + m], HT[:, kc, ts_], start=(kc == 0), stop=(kc == 7))
            return ins
        p.op("pe", f, reads=[w.name, ("HT", tt)], writes=[ps.name])

    def load_w(src, c0, ncols=512):
        w = ws.next()
        D(lambda e: e.dma_start(out=w[:, :, 0:ncols], in_=src[:, c0:c0 + ncols].rearrange("(kc p) m -> p kc m", p=128)), writes=[w.name], eng="pool")
        return w

    kcs = [(0, 2), (2, 2), (4, 2), (6, 4)]
    for half in range(2):
        tts = [2 * half, 2 * half + 1]
        for tt in tts:
            ts_ = slice(tt * TT, (tt + 1) * TT)
            D(lambda e, ts_=ts_: e.dma_start(out=XR[:, :, ts_], in_=xT[:, ts_].rearrange("(kc p) t -> p kc t", p=128)), writes=[("XR", tt)])
            rms(tt, lnw1)
        for ti, tt in enumerate(tts):
            ts_ = slice(tt * TT, (tt + 1) * TT)
            D(lambda e, ts_=ts_: e.dma_start(out=yun[:], in_=ya[:, ts_].rearrange("(c p) t -> p c t", p=128)), writes=["yun"])
            D(lambda e, ts_=ts_: e.dma_start(out=ydn[:], in_=yden[:, ts_].rearrange("(c p) t -> p c t", p=128)), writes=["ydn"])
            p.op("dve", lambda e: e.reciprocal(out=ydn[:], in_=ydn[:]), reads=["ydn"], writes=["ydn"])
            p.op("dve", lambda e, ti=ti: e.tensor_tensor(out=Y[:, 0:2, ti, :], in0=yun[:], in1=ydn[:], op=ALU.mult), reads=["yun", "ydn"], writes=[("Y", (0, ti)), ("Y", (1, ti))])
        w = load_w(w3, 0)
        for ti, tt in enumerate(tts):
            ts_ = slice(tt * TT, (tt + 1) * TT)
            for i4 in range(4):
                o_ = og.next()
                D(lambda e, o_=o_, i4=i4, ts_=ts_: e.dma_start(out=o_[:], in_=goT[128 * i4:128 * (i4 + 1), ts_]), writes=[o_.name])
                s = sq.next()
                p.op("act", lambda e, s=s, o_=o_: e.activation(out=s[:], in_=o_[:], func=AF.Square), reads=[o_.name], writes=[s.name])
                ps2 = psB.next()
                p.op("pe", lambda e, ps2=ps2, s=s: e.matmul(ps2[:], bd[:], s[:], start=True, stop=True), reads=["bd", s.name], writes=[ps2.name])
                r = rs.next()
                p.op("act", lambda e, r=r, ps2=ps2: e.activation(out=r[:], in_=ps2[:], func=AF.Ln, scale=1.0 / 64, bias=epsb[:, 0:1]), reads=[ps2.name, "epsb"], writes=[r.name])
                p.op("act", lambda e, r=r: e.activation(out=r[:], in_=r[:], func=AF.Exp, scale=-0.5), reads=[r.name], writes=[r.name])
                ps = psA.next()
                proj(ps, w, 128 * i4, tt)
                g_ = sg.next()
                p.op("act", lambda e, g_=g_, ps=ps: e.activation(out=g_[:], in_=ps[:], func=AF.Silu), reads=[ps.name], writes=[g_.name])
                t_ = ta.next()
                col = 0 if i4 < 2 else 1
                p.op("dve", lambda e, t_=t_, o_=o_, r=r, col=col: e.scalar_tensor_tensor(out=t_[:], in0=o_[:], scalar=ppt[:, col:col + 1], in1=r[:], op0=ALU.mult, op1=ALU.mult), reads=[o_.name, "ppt", r.name], writes=[t_.name])
                p.op("pool", lambda e, t_=t_, g_=g_, i4=i4, ti=ti: e.tensor_tensor(out=Y[:, 2 + i4, ti, :], in0=t_[:], in1=g_[:], op=ALU.mult), reads=[t_.name, g_.name], writes=[("Y", (2 + i4, ti))])
        w = load_w(w3, 512)
        for ti, tt in enumerate(tts):
            ts_ = slice(tt * TT, (tt + 1) * TT)
            us = []
            for j in range(4):
                o_ = og.next()
                D(lambda e, o_=o_, j=j, ts_=ts_: e.dma_start(out=o_[:], in_=goT[512 + 128 * j:640 + 128 * j, ts_]), writes=[o_.name])
                xb = xsb.next()
                D(lambda e, xb=xb, j=j, ts_=ts_: e.dma_start(out=xb[:], in_=xsT[128 * j:128 * (j + 1), ts_]), writes=[xb.name])
                t_ = ta.next()
                p.op("dve", lambda e, t_=t_, xb=xb, o_=o_, j=j: e.scalar_tensor_tensor(out=t_[:], in0=xb[:], scalar=ppt[:, 2 + j:3 + j], in1=o_[:], op0=ALU.mult, op1=ALU.add), reads=[xb.name, "ppt", o_.name], writes=[t_.name])
                ps = psA.next()
                proj(ps, w, 128 * j, tt)
                g_ = sg.next()
                p.op("act", lambda e, g_=g_, ps=ps: e.activation(out=g_[:], in_=ps[:], func=AF.Silu), reads=[ps.name], writes=[g_.name])
                u_ = u2.next()
                p.op("pool", lambda e, u_=u_, t_=t_, g_=g_: e.tensor_tensor(out=u_[:], in0=t_[:], in1=g_[:], op=ALU.mult), reads=[t_.name, g_.name], writes=[u_.name])
                us.append(u_)
            for gI in range(2):
                ps2 = psB.next()
                for k2 in range(2):
                    s = sq.next()
                    u_ = us[2 * gI + k2]
                    p.op("act", lambda e, s=s, u_=u_: e.activation(out=s[:], in_=u_[:], func=AF.Square), reads=[u_.name], writes=[s.name])
                    p.op("pe", lambda e, ps2=ps2, s=s, k2=k2: e.matmul(ps2[:], ones[:], s[:], start=(k2 == 0), stop=(k2 == 1)), reads=["ones", s.name], writes=[ps2.name])
                r = rs.next()
                p.op("act", lambda e, r=r, ps2=ps2: e.activation(out=r[:], in_=ps2[:], func=AF.Ln, scale=1.0 / 256, bias=epsb[:, 0:1]), reads=[ps2.name, "epsb"], writes=[r.name])
                p.op("act", lambda e, r=r: e.activation(out=r[:], in_=r[:], func=AF.Exp, scale=-0.5), reads=[r.name], writes=[r.name])
                for k2 in range(2):
                    j = 2 * gI + k2
                    u_ = us[j]
                    p.op("dve", lambda e, u_=u_, r=r, j=j, ti=ti: e.scalar_tensor_tensor(out=Y[:, 6 + j, ti, :], in0=u_[:], scalar=ppt[:, 6 + j:7 + j], in1=r[:], op0=ALU.mult, op1=ALU.mult), reads=[u_.name, "ppt", r.name], writes=[("Y", (6 + j, ti))])
        for n in range(8):
            w = load_w(w3, 1024 + 512 * n)
            wu_ = wun.next()
            D(lambda e, wu_=wu_, n=n: e.dma_start(out=wu_[:], in_=wup[:, 128 * n:128 * (n + 1)].rearrange("(kc p) m -> p kc m", p=128)), writes=[wu_.name], eng="pool")
            for ti, tt in enumerate(tts):
                m_ = mt.next()
                for b in range(4):
                    psg_ = psA.next()
                    proj(psg_, w, 128 * b, tt)
                    g_ = sg.next()
                    p.op("act", lambda e, g_=g_, psg_=psg_: e.activation(out=g_[:], in_=psg_[:], func=AF.Sigmoid), reads=[psg_.name], writes=[g_.name])
                    psu = psB.next()
                    k0, nk = kcs[b]

                    def fu(e, psu=psu, k0=k0, nk=nk, wu_=wu_, ti=ti):
                        ins = None
                        for q in range(nk):
                            ins = e.matmul(psu[:], wu_[:, k0 + q, :], Y[:, k0 + q, ti, :], start=(q == 0), stop=(q == nk - 1))
                        return ins
                    p.op("pe", fu, reads=[wu_.name] + [("Y", (k0 + q, ti)) for q in range(nk)], writes=[psu.name])
                    if b == 0:
                        p.op("dve", lambda e, m_=m_, psu=psu, g_=g_: e.tensor_tensor(out=m_[:], in0=psu[:], in1=g_[:], op=ALU.mult), reads=[psu.name, g_.name], writes=[m_.name])
                    else:
                        t_ = ta.next()
                        p.op("dve", lambda e, t_=t_, psu=psu, g_=g_: e.tensor_tensor(out=t_[:], in0=psu[:], in1=g_[:], op=ALU.mult), reads=[psu.name, g_.name], writes=[t_.name])
                        if b < 3:
                            p.op("pool", lambda e, m_=m_, t_=t_: e.tensor_tensor(out=m_[:], in0=m_[:], in1=t_[:], op=ALU.add), reads=[m_.name, t_.name], writes=[m_.name])
                        else:
                            p.op("pool", lambda e, m_=m_, t_=t_, n=n, ti=ti: e.tensor_tensor(out=mg[:, n, ti, :], in0=m_[:], in1=t_[:], op=ALU.add), reads=[m_.name, t_.name], writes=[("mg", (n, ti))])
        for ti, tt in enumerate(tts):
            ts_ = slice(tt * TT, (tt + 1) * TT)
            for n in range(8):
                ps = psA.next()

                def fo_(e, ps=ps, n=n, ti=ti):
                    ins = None
                    for kc in range(8):
                        ins = e.matmul(ps[:], woutt[:, kc, 128 * n:128 * (n + 1)], mg[:, kc, ti, :], start=(kc == 0), stop=(kc == 7))
                    return ins
                p.op("pe", fo_, reads=["woutt"] + [("mg", (kc, ti)) for kc in range(8)], writes=[ps.name])
                p.op("dve", lambda e, ps=ps, n=n, ts_=ts_: e.tensor_tensor(out=XR[:, n, ts_], in0=XR[:, n, ts_], in1=ps[:], op=ALU.add), reads=[("XR", tt), ps.name], writes=[("XR", tt)])
            rms(tt, lnw2)
    slot = 0
    for hg in range(6):
        nh = 4 if hg < 5 else 2
        wg = load_w(wfi, 512 * hg, 128 * nh)
        wu = load_w(wfi, 2816 + 512 * hg, 128 * nh)
        half = hg % 2
        wkey = ("woutt", "h%d" % half)
        D(lambda e, half=half, hg=hg, nh=nh: e.dma_start(out=woutt[:, 4 * half:4 * half + nh, :], in_=wfo[512 * hg:512 * hg + 128 * nh, :].rearrange("(kc p) m -> p kc m", p=128)), writes=[wkey], eng="pool")
        for tt in range(NTT):
            ts_ = slice(tt * TT, (tt + 1) * TT)
            a0 = slot % 2
            slot += 1
            akeys = [("mg", (q, a0)) for q in range(4)]
            for hc in range(nh):
                pg, pu = psA.next(), psB.next()
                proj(pg, wg, 128 * hc, tt)
                proj(pu, wu, 128 * hc, tt)
                g_ = sg.next()
                p.op("act", lambda e, g_=g_, pg=pg: e.activation(out=g_[:], in_=pg[:], func=AF.Silu), reads=[pg.name], writes=[g_.name])
                p.op("dve", lambda e, a0=a0, hc=hc, g_=g_, pu=pu: e.tensor_tensor(out=mg[:, hc, a0, :], in0=pu[:], in1=g_[:], op=ALU.mult), reads=[pu.name, g_.name], writes=[("mg", (hc, a0))])
            for n in range(8):
                ps = psA.next()

                def ff_(e, ps=ps, n=n, half=half, a0=a0, nh=nh):
                    ins = None
                    for hc in range(nh):
                        ins = e.matmul(ps[:], woutt[:, 4 * half + hc, 128 * n:128 * (n + 1)], mg[:, hc, a0, :], start=(hc == 0), stop=(hc == nh - 1))
                    return ins
                p.op("pe", ff_, reads=[wkey] + akeys, writes=[ps.name])
                p.op("dve", lambda e, ps=ps, n=n, ts_=ts_: e.tensor_tensor(out=XR[:, n, ts_], in0=XR[:, n, ts_], in1=ps[:], op=ALU.add), reads=[("XR", tt), ps.name], writes=[("XR", tt)])
    for tt in range(NTT):
        ts_ = slice(tt * TT, (tt + 1) * TT)
        D(lambda e, ts_=ts_: e.dma_start(out=xo[:, ts_].rearrange("(kc p) t -> p kc t", p=128), in_=XR[:, :, ts_]), reads=[("XR", tt)])
    run_prog(nc, p)
    return nc


def l3_weight_cols():
    C = COLS
    r = np.arange
    gates = []
    for n in range(8):
        for b in range(4):
            gates.append(r(C["gates"] + 1024 * b + 128 * n, C["gates"] + 1024 * b + 128 * (n + 1)))
    return np.concatenate([r(C["gr"], C["gr"] + 256), r(C["rg"], C["rg"] + 256), r(C["z"], C["z"] + 512)] + gates)


def run_l3(xfull, P, FO, GO, OB, FD):
    nc = _get("l3", build_l3)
    w3 = np.ascontiguousarray(P["w_in"][:, l3_weight_cols()])
    wup = np.ascontiguousarray(np.concatenate([P["w_up_a"], P["w_up_b"], P["w_up_c"], P["w_up_d"]], axis=0))
    pp = np.zeros((128, 16), np.float32)
    pp[:, 0] = np.tile(P["gla_norm"], 2)
    pp[:, 1] = np.tile(P["ret_norm"], 2)
    for j in range(4):
        pp[:, 2 + j] = np.repeat(P["ssd_d"][2 * j:2 * j + 2], 64)
        pp[:, 6 + j] = P["ssd_norm"][128 * j:128 * (j + 1)]
    ln1 = np.ascontiguousarray(P["ln1"].reshape(8, 128).T)
    ln2 = np.ascontiguousarray(P["ln2"].reshape(8, 128).T)
    GOf = GO.reshape(1024, SEQ)
    FDr = np.repeat(FD, 64, axis=0)
    maps = []
    for c in range(8):
        sl = slice(c * NT, (c + 1) * NT)
        maps.append(dict(xT=np.ascontiguousarray(xfull[sl].T), w3=w3, wup=wup, wout=np.ascontiguousarray(P["w_out"]),
                         wfi=np.ascontiguousarray(P["w_ffn_in"]), wfo=np.ascontiguousarray(P["w_ffn_out"]), ln1=ln1, ln2=ln2, pp=pp,
                         ya=np.ascontiguousarray(FO[:, sl]), yden=np.ascontiguousarray(FDr[:, sl]), goT=np.ascontiguousarray(GOf[:, sl]), xsT=np.ascontiguousarray(OB[2304:2816, sl])))
    res = run_bass_kernel_spmd(nc, maps, core_ids=list(range(8))).results
    return np.concatenate([r["xo"].T for r in res], axis=0)


def kernel(**inputs):
    x = np.asarray(inputs["x"], np.float32)[0]
    pos = np.asarray(inputs["positions"])[0]
    names = [k for k in inputs if k not in ("x", "positions")]
    for l in range(4):
        P = {k: np.asarray(inputs[k][l], np.float32) for k in names}
        OB, OF = run_l1(x, P, pos)
        FO, GO, FD = run_l2(OB, OF)
        x = run_l3(x, P, FO, GO, OB, FD)
    return x[None].astype(np.float32)
```
